# Optimizing a Trainium2 kernel written in Bass

```python
import jax, jax.numpy as jnp
from jax import lax
import numpy as np

D_MODEL = 4096
BATCH = 4
SEQ = 2048
DEPTH = 1

MIX_W = D_MODEL
RWKV_W = MIX_W // 2
RWKV_HEAD = 64
N_RWKV_HEADS = RWKV_W // RWKV_HEAD
DECAY_LORA = 96
AAA_LORA = 96
GATE_LORA = 256
LRU_W = MIX_W - RWKV_W
LRU_BLOCK_W = 128
LRU_BLOCKS = LRU_W // LRU_BLOCK_W
CONV_WIDTH = 4
LRU_C = 8.0
D_FF = ((8 * D_MODEL + 3 * 256 - 1) // (3 * 256)) * 256
RWKV_COLS = 3 * RWKV_W + DECAY_LORA + AAA_LORA + GATE_LORA
IN_COLS = RWKV_COLS + 2 * LRU_W
NORM_EPS = 1e-6
GN_EPS = 64e-5

kernel_name = "hymba_rwkv7_rglru_swiglu"


def _rmsnorm(x, g):
    xf = x.astype(jnp.float32)
    return xf * lax.rsqrt(jnp.mean(xf * xf, axis=-1, keepdims=True) + NORM_EPS) * g.astype(jnp.float32)


def _token_shift(p):
    return jnp.pad(p, ((0, 0), (1, 0), (0, 0)))[:, :-1]


def _rwkv7_scan(r, w, k, v, a, b):
    Bsz, T, H, N = r.shape

    def step(S, inp):
        r_t, w_t, k_t, v_t, a_t, b_t = inp
        sa = jnp.einsum('bhvk,bhk->bhv', S, a_t)
        S = (S * w_t[:, :, None, :] + sa[..., None] * b_t[:, :, None, :]
             + v_t[..., None] * k_t[:, :, None, :])
        y_t = jnp.einsum('bhvk,bhk->bhv', S, r_t)
        return S, y_t

    seq = tuple(jnp.swapaxes(t, 0, 1) for t in (r, w, k, v, a, b))
    S0 = jnp.zeros((Bsz, H, N, N), jnp.float32)
    _, ys = lax.scan(step, S0, seq)
    return jnp.swapaxes(ys, 0, 1)


def _rwkv7_mixer(p, mu, w0, w2, a0, a2, g2, k_k, k_a, r_k, ln_g, ln_b):
    Bsz, T, _ = p.shape
    p = p + (_token_shift(p) - p) * mu
    o1, o2, o3 = RWKV_W, 2 * RWKV_W, 3 * RWKV_W
    r, k, v, wd, ad, gd = jnp.split(p, [o1, o2, o3, o3 + DECAY_LORA, o3 + DECAY_LORA + AAA_LORA], axis=-1)
    w = -jax.nn.softplus(-(w0 + jnp.tanh(wd) @ w2)) - 0.5
    decay = jnp.exp(-jnp.exp(w))
    a = jax.nn.sigmoid(a0 + ad @ a2)
    g = jax.nn.sigmoid(gd) @ g2

    def heads(t):
        return t.reshape(Bsz, T, N_RWKV_HEADS, RWKV_HEAD)

    kk = heads(k * k_k)
    kk = kk * lax.rsqrt(jnp.maximum(jnp.sum(kk * kk, axis=-1, keepdims=True), 1e-24))
    k = k * (1.0 + (a - 1.0) * k_a)
    r, k, v, decay, a = heads(r), heads(k), heads(v), heads(decay), heads(a)
    y = _rwkv7_scan(r, decay, k, v, -kk, kk * a)
    mean = jnp.mean(y, axis=-1, keepdims=True)
    var = jnp.mean(jnp.square(y - mean), axis=-1, keepdims=True)
    y = (y - mean) * lax.rsqrt(var + GN_EPS)
    y = y * ln_g.reshape(N_RWKV_HEADS, RWKV_HEAD) + ln_b.reshape(N_RWKV_HEADS, RWKV_HEAD)
    y = y + jnp.sum(r * k * r_k, axis=-1, keepdims=True) * v
    return y.reshape(Bsz, T, RWKV_W) * g


def _rglru_mixer(p, conv_w, conv_b, wr, br, wi, bi, lam, norm_g):
    Bsz, T, _ = p.shape
    xb, gate = jnp.split(p, [LRU_W], axis=-1)
    xpad = jnp.pad(xb, ((0, 0), (CONV_WIDTH - 1, 0), (0, 0)))
    xc = conv_b + xpad[:, 0:T] * conv_w[0]
    for j in range(1, CONV_WIDTH):
        xc = xc + xpad[:, j:j + T] * conv_w[j]
    xh = xc.reshape(Bsz, T, LRU_BLOCKS, LRU_BLOCK_W)
    rg = jax.nn.sigmoid(jnp.einsum('bthi,hij->bthj', xh, wr).reshape(Bsz, T, LRU_W) + br)
    ig = jax.nn.sigmoid(jnp.einsum('bthi,hij->bthj', xh, wi).reshape(Bsz, T, LRU_W) + bi)
    log_a = -LRU_C * rg * jax.nn.softplus(-lam)
    a = jnp.exp(log_a)
    mult = jnp.sqrt(-jnp.expm1(2.0 * log_a))
    first = (jnp.arange(T) == 0)[None, :, None]
    mult = jnp.where(first, 1.0, mult)
    bx = mult * ig * xc

    def combine(left, right):
        a1, b1 = left
        a2, b2 = right
        return a1 * a2, a2 * b1 + b2

    _, h = lax.associative_scan(combine, (a, bx), axis=1)
    y = h * jax.nn.gelu(gate)
    return _rmsnorm(y, norm_g)


def setup_inputs(seed: int = 0) -> dict:
    key = jax.random.key(seed)
    ks = jax.random.split(key, 32)
    f32 = jnp.float32
    L = DEPTH

    def nrm(k, shape, scale):
        return jax.random.normal(k, shape, f32) * scale

    u = jax.random.uniform(ks[20], (L, LRU_W), f32, 0.9, 0.999)
    a_base = u ** (1.0 / LRU_C)
    lru_lambda = jnp.log(a_base) - jnp.log1p(-a_base)
    return {
        "x": nrm(ks[0], (BATCH, SEQ, D_MODEL), 1.0),
        "norm_mix_g": 1.0 + nrm(ks[1], (L, D_MODEL), 0.02),
        "w_in": nrm(ks[2], (L, D_MODEL, IN_COLS), D_MODEL ** -0.5),
        "mu_shift": jax.random.uniform(ks[3], (L, RWKV_COLS), f32),
        "rwkv_w0": jax.random.uniform(ks[4], (L, RWKV_W), f32, -6.0, 0.0),
        "rwkv_w2": nrm(ks[5], (L, DECAY_LORA, RWKV_W), 0.1 * DECAY_LORA ** -0.5),
        "rwkv_a0": nrm(ks[6], (L, RWKV_W), 0.1),
        "rwkv_a2": nrm(ks[7], (L, AAA_LORA, RWKV_W), AAA_LORA ** -0.5),
        "rwkv_g2": nrm(ks[8], (L, GATE_LORA, RWKV_W), GATE_LORA ** -0.5),
        "rwkv_k_k": 0.85 + nrm(ks[9], (L, RWKV_W), 0.02),
        "rwkv_k_a": 1.0 + nrm(ks[10], (L, RWKV_W), 0.02),
        "rwkv_r_k": nrm(ks[11], (L, N_RWKV_HEADS, RWKV_HEAD), 0.1),
        "rwkv_ln_g": 1.0 + nrm(ks[12], (L, RWKV_W), 0.02),
        "rwkv_ln_b": nrm(ks[13], (L, RWKV_W), 0.01),
        "conv_w": nrm(ks[14], (L, CONV_WIDTH, LRU_W), CONV_WIDTH ** -0.5),
        "conv_b": nrm(ks[15], (L, LRU_W), 0.01),
        "lru_wr": nrm(ks[16], (L, LRU_BLOCKS, LRU_BLOCK_W, LRU_BLOCK_W), LRU_BLOCK_W ** -0.5),
        "lru_br": nrm(ks[17], (L, LRU_W), 0.01),
        "lru_wi": nrm(ks[18], (L, LRU_BLOCKS, LRU_BLOCK_W, LRU_BLOCK_W), LRU_BLOCK_W ** -0.5),
        "lru_bi": nrm(ks[19], (L, LRU_W), 0.01),
        "lru_lambda": lru_lambda,
        "lru_norm_g": 1.0 + nrm(ks[21], (L, LRU_W), 0.02),
        "w_out": nrm(ks[22], (L, MIX_W, D_MODEL), MIX_W ** -0.5),
        "norm_ffn_g": 1.0 + nrm(ks[23], (L, D_MODEL), 0.02),
        "ffn_w_gate": nrm(ks[24], (L, D_MODEL, D_FF), D_MODEL ** -0.5),
        "ffn_w_up": nrm(ks[25], (L, D_MODEL, D_FF), D_MODEL ** -0.5),
        "ffn_w_down": nrm(ks[26], (L, D_FF, D_MODEL), D_FF ** -0.5),
        "norm_final_g": 1.0 + nrm(ks[27], (D_MODEL,), 0.02),
    }


def reference(x, norm_mix_g, w_in, mu_shift, rwkv_w0, rwkv_w2, rwkv_a0, rwkv_a2, rwkv_g2,
              rwkv_k_k, rwkv_k_a, rwkv_r_k, rwkv_ln_g, rwkv_ln_b, conv_w, conv_b,
              lru_wr, lru_br, lru_wi, lru_bi, lru_lambda, lru_norm_g, w_out,
              norm_ffn_g, ffn_w_gate, ffn_w_up, ffn_w_down, norm_final_g):
    h = x.astype(jnp.float32)
    for l in range(DEPTH):
        u = _rmsnorm(h, norm_mix_g[l])
        p = u @ w_in[l]
        y_a = _rwkv7_mixer(p[..., :RWKV_COLS], mu_shift[l], rwkv_w0[l], rwkv_w2[l], rwkv_a0[l],
                           rwkv_a2[l], rwkv_g2[l], rwkv_k_k[l], rwkv_k_a[l], rwkv_r_k[l],
                           rwkv_ln_g[l], rwkv_ln_b[l])
        y_b = _rglru_mixer(p[..., RWKV_COLS:], conv_w[l], conv_b[l], lru_wr[l], lru_br[l],
                           lru_wi[l], lru_bi[l], lru_lambda[l], lru_norm_g[l])
        h = h + jnp.concatenate([y_a, y_b], axis=-1) @ w_out[l]
        u = _rmsnorm(h, norm_ffn_g[l])
        h = h + (jax.nn.silu(u @ ffn_w_gate[l]) * (u @ ffn_w_up[l])) @ ffn_w_down[l]
    return _rmsnorm(h, norm_final_g).astype(x.dtype)
```

```python
import os as _os
import numpy as np
from contextlib import ExitStack
import concourse.bass as bass
import concourse.mybir as mybir
from concourse.bass_utils import run_bass_kernel_spmd

F32, BF16 = mybir.dt.float32, mybir.dt.bfloat16
AF = mybir.ActivationFunctionType
ALU = mybir.AluOpType

T = 2048
TO = 1024
D = 4096
KC = 32
NCH = 44
DFF = 11008
NF = 86
RW = 1024
CH = 64
NCK = T // CH
NPT = 176
EPS = 1e-6
GN_EPS = 64e-5

ENGS = ['pe', 'dve', 'act', 'pool', 'sp', 'poolq', 'actq']
HOST = {'pe': 'tensor', 'dve': 'vector', 'act': 'scalar', 'pool': 'gpsimd', 'sp': 'sync',
        'poolq': 'gpsimd', 'actq': 'scalar'}


class Prog:
    cnt = 0

    def __init__(self, nc, name):
        self.nc = nc
        self.name = name
        self.stages = []

    def stage(self):
        st = {e: [] for e in ENGS}
        self.stages.append(st)
        return st

    def emit(self):
        nc = self.nc
        with ExitStack() as es:
            Prog.cnt += 1
            sems = {e: es.enter_context(nc.semaphore(f"{self.name}{Prog.cnt}_{e}")) for e in ENGS}
            inc = {e: (16 if e in ('sp', 'poolq', 'actq') else 1) for e in ENGS}
            cum = {e: [] for e in ENGS}
            run = {e: 0 for e in ENGS}
            for st in self.stages:
                for e in ENGS:
                    if st[e]:
                        run[e] += inc[e] * (len(st[e]) if inc[e] == 16 else 1)
                    cum[e].append(run[e])
            stages = self.stages
            with nc.Block() as blk0:
                def clr(eng):
                    for e in ENGS:
                        eng.sem_clear(sems[e])
                blk0.gpsimd(clr)
            blk = es.enter_context(nc.Block())

            def make(hosteng):
                mine = [e for e in ENGS if HOST[e] == hosteng]

                def f(eng):
                    waited = {e: 0 for e in ENGS}
                    for k, st in enumerate(stages):
                        if not any(st[e] for e in mine):
                            continue
                        if k > 0:
                            for x in ENGS:
                                need = cum[x][k - 1]
                                if need > waited[x]:
                                    eng.wait_ge(sems[x], need)
                                    waited[x] = need
                        for e in mine:
                            ops = st[e]
                            for i, op in enumerate(ops):
                                ins = op(eng)
                                if inc[e] == 16:
                                    ins.then_inc(sems[e], 16)
                                elif i == len(ops) - 1:
                                    ins.then_inc(sems[e], 1)
                    for x in ENGS:
                        if run[x] > waited[x]:
                            eng.wait_ge(sems[x], run[x])
                return f
            blk.tensor(make('tensor'))
            blk.vector(make('vector'))
            blk.scalar(make('scalar'))
            blk.gpsimd(make('gpsimd'))
            blk.sync(make('sync'))


def build_nc(debug=None, ncores=8):
    early = debug in ('p1', 'p2', 'lora', 'prep', 'scan', 'post', 'rwkv', 'mix')
    nc = bass.Bass("TRN2", target_bir_lowering=False)

    MIXKEEP = ("xb", "g1bc", "win", "ptabh", "ptabl", "ptabu", "masks2", "w2", "a2", "g2w", "wr", "wi", "masks", "bones", "cmask")

    def din(name, shape, dt=F32):
        if early and name not in MIXKEEP:
            shape = [1, 1]
        return nc.dram_tensor(name, shape, dt, kind="ExternalInput").ap()

    xb = din("xb", [T, D])
    xo = din("xo", [TO, D])
    win = din("win", [NCH, 128, KC * 128])
    g1bc = din("g1bc", [128, D])
    g2bc = din("g2bc", [128, D])
    g3bc = din("g3bc", [128, D])
    ptabh_d = din("ptabh", [64, 192])
    ptabl_d = din("ptabl", [128, 4])
    ptabu_d = din("ptabu", [128, 72])
    ngtab_d = din("ngtab", [128, 16])
    w2_d = din("w2", [96, RW])
    a2_d = din("a2", [96, RW])
    g2w_d = din("g2w", [128, 2, RW])
    wr_d = din("wr", [128, 8, 128])
    wi_d = din("wi", [128, 8, 128])
    masks_d = din("masks", [64, 3, 512])
    masks2_d = din("masks2", [64, 1024])
    bones_d = din("bones", [128, 128])
    cmask_d = din("cmask", [128, 512])
    wout = din("wout", [8, 128, KC * 512])
    wg = din("wg", [NF, 128, KC * 128])
    wu = din("wu", [NF, 128, KC * 128])
    wdn = din("wdn", [8, 128, NF, 512])
    sel_d = din("sel", [128, 2])
    out = nc.dram_tensor("out", [TO, D], F32, kind="ExternalOutput").ap()

    def dscr(name, shape, dt):
        return nc.dram_tensor(name, shape, dt).ap()

    if not early:
        pT = dscr("pT", [NCH * 128, T], F32)
    gT = dscr("gT", [RW, T], F32)
    bvT = dscr("bvT", [RW, T], F32)
    if early:
        ybuf = nc.dram_tensor("ybuf", [2048, T], BF16, kind="ExternalOutput").ap()
        pT = nc.dram_tensor("pT", [NCH * 128, T], F32, kind="ExternalOutput").ap()
    else:
        ybuf = dscr("ybuf", [2048, T], BF16)
    gbufs = [dscr(f"gbuf{k}", [1024, T], BF16) for k in range(4)]
    h1buf = dscr("h1buf", [TO, D], F32)
    hT = dscr("hT", [DFF, TO], BF16)
    h2buf = dscr("h2buf", [TO, D], F32)

    uid = [0]

    def sbt(es, name, shape, dt):
        uid[0] += 1
        return es.enter_context(nc.sbuf_tensor(f"s{uid[0]}_{name}", shape, dt))

    def pst(es, name, shape, dt):
        uid[0] += 1
        return es.enter_context(nc.psum_tensor(f"p{uid[0]}_{name}", shape, dt))

    def norm_transpose(name, es, src, gbc_d, ntok, dstT, ident):
        ntile = ntok // 128
        gbc = sbt(es, name + "gbc", [128, D], F32)
        xt = [sbt(es, f"{name}xt{i}", [128, D], F32) for i in range(2)]
        xn = [sbt(es, f"{name}xn{i}", [128, D], BF16) for i in range(2)]
        junk = sbt(es, name + "junk", [128, D], BF16)
        ss = sbt(es, name + "ss", [128, 2], F32)
        rstd = sbt(es, name + "rstd", [128, 2], F32)
        ptr = [pst(es, f"{name}ptr{i}", [128, 1024], BF16) for i in range(4)]
        P = Prog(nc, name)
        st = P.stage()
        st['sp'].append(lambda e: e.dma_start(out=gbc[:], in_=gbc_d))
        st['sp'].append(lambda e: e.dma_start(out=xt[0][:], in_=src[0:128, :]))
        st['dve'].append(lambda e: e.memset(ss[:], 0.0))
        for i in range(ntile + 1):
            b = i % 2
            pb = (i - 1) % 2
            s1 = P.stage()
            if i < ntile:
                s1['act'].append(lambda e, b=b: e.activation(out=junk[:], in_=xt[b][:], func=AF.Square,
                                                               accum_out=ss[:, b:b + 1]))
            if i + 1 < ntile:
                s1['sp'].append(lambda e, i=i: e.dma_start(out=xt[(i + 1) % 2][:],
                                                           in_=src[(i + 1) * 128:(i + 2) * 128, :]))
            if i >= 1:
                for kc in range(KC):
                    s1['pe'].append(lambda e, kc=kc, pb=pb: e.transpose(
                        ptr[kc // 8][:, (kc % 8) * 128:(kc % 8 + 1) * 128],
                        xn[pb][:, kc * 128:(kc + 1) * 128], ident[:]))
            s2 = P.stage()
            if i < ntile:
                s2['act'].append(lambda e, b=b: e.activation(out=rstd[:, b:b + 1], in_=ss[:, b:b + 1], func=AF.Sqrt,
                                                              scale=1.0 / D, bias=EPS))
            if i >= 1:
                t0 = (i - 1) * 128
                for q in range(4):
                    eng = 'act' if q % 2 == 0 else 'dve'
                    o = dstT[:, q * 8:(q + 1) * 8, t0:t0 + 128]
                    src_ps = ptr[q][:].rearrange("p (a b) -> p a b", a=8)
                    if eng == 'act':
                        s2['act'].append(lambda e, o=o, s=src_ps: e.copy(o, s))
                    else:
                        s2['dve'].append(lambda e, o=o, s=src_ps: e.tensor_copy(o, s))
            if i < ntile:
                s3 = P.stage()
                s3['dve'].append(lambda e, b=b: e.reciprocal(rstd[:, b:b + 1], rstd[:, b:b + 1]))
                s3['pool'].append(lambda e, b=b: e.memset(ss[:, b:b + 1], 0.0))
                s4 = P.stage()
                s4['dve'].append(lambda e, b=b: e.scalar_tensor_tensor(out=xn[b][:], in0=xt[b][:],
                                                                        scalar=rstd[:, b:b + 1], in1=gbc[:],
                                                                        op0=ALU.mult, op1=ALU.mult))
        P.emit()

    with ExitStack() as top:
        ident = sbt(top, "ident", [128, 128], BF16)
        P = Prog(nc, "init")
        st = P.stage()
        st['pool'].append(lambda e: e.memset(ident[:], 0.0))
        st = P.stage()
        st['pool'].append(lambda e: e.affine_select(out=ident[:], in_=ident[:], pattern=[[-1, 128]],
                                                    compare_op=ALU.not_equal, fill=1.0, base=0,
                                                    channel_multiplier=1))
        P.emit()

        with ExitStack() as es:
            uT = sbt(es, "uT", [128, KC, T], BF16)
            with ExitStack() as es1:
                norm_transpose("n1", es1, xb, g1bc, T, uT, ident)
            if debug == 'p1':
                return nc
            with ExitStack() as es2:
                wb = [sbt(es2, f"wb{i}", [128, KC, 128], BF16) for i in range(2)]
                ost = [sbt(es2, f"ost{i}", [128, T], F32) for i in range(2)]
                pp = [[pst(es2, f"pp{s}_{q}", [128, 512], F32) for q in range(4)] for s in range(2)]
                P = Prog(nc, "inproj")
                st = P.stage()
                st['poolq'].append(lambda e: e.dma_start(out=wb[0][:].rearrange("p a b -> p (a b)"), in_=win[0]))
                for s in range(NCH + 2):
                    st = P.stage()
                    if s + 1 < NCH:
                        st['poolq'].append(lambda e, s=s: e.dma_start(
                            out=wb[(s + 1) % 2][:].rearrange("p a b -> p (a b)"), in_=win[s + 1]))
                    if s < NCH:
                        for kc in range(KC):
                            for q in range(4):
                                st['pe'].append(lambda e, s=s, kc=kc, q=q: e.matmul(
                                    pp[s % 2][q][:], wb[s % 2][:, kc, :], uT[:, kc, q * 512:(q + 1) * 512],
                                    start=(kc == 0), stop=(kc == KC - 1)))
                    if 1 <= s <= NCH:
                        j = s - 1
                        for q in range(4):
                            o = ost[j % 2][:, q * 512:(q + 1) * 512]
                            if q % 2 == 0:
                                st['act'].append(lambda e, o=o, j=j, q=q: e.copy(o, pp[j % 2][q][:]))
                            else:
                                st['dve'].append(lambda e, o=o, j=j, q=q: e.tensor_copy(o, pp[j % 2][q][:]))
                    if 2 <= s:
                        j = s - 2
                        st['sp'].append(lambda e, j=j: e.dma_start(out=pT[j * 128:(j + 1) * 128, :],
                                                                    in_=ost[j % 2][:]))
                P.emit()
            if debug == 'p2':
                return nc

        with ExitStack() as es:
            ptab = sbt(es, "ptabh", [64, 192], F32)
            ptl = sbt(es, "ptabl", [128, 4], F32)
            omk = sbt(es, "omk", [64, 16], F32)
            g8 = sbt(es, "g8", [64, 16], F32)
            lora = sbt(es, "lora", [128, 4, T], BF16)
            w2b = sbt(es, "w2b", [96, RW], BF16)
            a2b = sbt(es, "a2b", [96, RW], BF16)
            g2b = sbt(es, "g2b", [128, 2, RW], BF16)
            masks = sbt(es, "masks", [64, 3, 512], F32)
            masks2 = sbt(es, "masks2", [64, 1024], F32)
            ones64 = sbt(es, "ones64", [64, 64], F32)
            cmask = sbt(es, "cmask", [64, 512], F32)
            HT = 1024
            ARs = sbt(es, "ARs", [64, 8, 2 * HT], BF16)
            Bs = sbt(es, "Bs", [64, 8, HT], BF16)
            Ks = sbt(es, "Ks", [64, 8, HT], BF16)
            Vs = sbt(es, "Vs", [64, 8, HT], BF16)
            WC = sbt(es, "WC", [64, 8, 16], F32)
            obfs = [sbt(es, f"obf{i}", [64, 512], BF16) for i in range(2)]
            STs = [sbt(es, f"ST{g}", [64, 8, 64], F32) for g in range(2)]
            STbs = [sbt(es, f"STb{g}", [64, 8, 64], BF16) for g in range(2)]
            id64 = ident[0:64, 0:64]

            def merge_units(P, fns):
                subs = []
                for fn in fns:
                    Pu = Prog(nc, "sub")
                    fn(Pu)
                    subs.append(Pu.stages)
                n = len(subs[0])
                assert all(len(x) == n for x in subs)
                for k in range(n):
                    stg = P.stage()
                    for x in subs:
                        for e in ENGS:
                            stg[e].extend(x[k][e])

            def M3(i):
                return masks[:, i, :].rearrange("p (a b) -> p a b", a=8)

            def v8(t):
                return t[0:64, :].rearrange("p (a b) -> p a b", a=8)

            et = ExitStack()
            raw = sbt(et, "raw", [128, 4, 513], F32)
            dd = sbt(et, "dd", [128, 4, 512], F32)
            sh = sbt(et, "sh", [128, 4, 512], F32)
            P = Prog(nc, "rw0")
            st = P.stage()
            for (dst, srcd) in ((ptab, ptabh_d), (ptl, ptabl_d), (masks, masks_d), (masks2, masks2_d), (cmask, cmask_d[0:64, :])):
                st['sp'].append(lambda e, dst=dst, srcd=srcd: e.dma_start(out=dst[:], in_=srcd))
            for (dst, srcd) in ((w2b, w2_d), (a2b, a2_d), (g2b, g2w_d)):
                st['poolq'].append(lambda e, dst=dst, srcd=srcd: e.dma_start(out=dst[:], in_=srcd))
            st['pool'].append(lambda e: e.memset(ones64[:], 1.0))
            for g in range(2):
                st['dve'].append(lambda e, g=g: e.memset(STs[g][:], 0.0))
                st['pool'].append(lambda e, g=g: e.memset(STbs[g][:], 0.0))
            st = P.stage()
            kav = ptab[:, :].rearrange("p (h k) -> p h k", k=12)[:, :, 6]
            lgv = ptab[:, :].rearrange("p (h k) -> p h k", k=12)[:, :, 9]
            st['dve'].append(lambda e: e.tensor_scalar(omk[:], kav, -1.0, 1.0, ALU.mult, ALU.add))
            st['dve'].append(lambda e: e.tensor_scalar(g8[:], lgv, 8.0, None, ALU.mult))
            for tb in range(4):
                c0 = tb * 512
                st = P.stage()
                for q in range(4):
                    rows = slice((24 + q) * 128, (25 + q) * 128)
                    if tb == 0:
                        st['sp'].append(lambda e, q=q, rows=rows: e.dma_start(out=raw[:, q, 1:513],
                                                                               in_=pT[rows, 0:512]))
                    else:
                        st['sp'].append(lambda e, q=q, rows=rows, c0=c0: e.dma_start(
                            out=raw[:, q, 0:513], in_=pT[rows, c0 - 1:c0 + 512]))
                if tb == 0:
                    st['pool'].append(lambda e: e.memset(raw[:, :, 0:1], 0.0))
                st = P.stage()
                st['dve'].append(lambda e: e.tensor_tensor(out=dd[:], in0=raw[:, :, 0:512], in1=raw[:, :, 1:513],
                                                            op=ALU.subtract))
                st = P.stage()
                for q in range(4):
                    st['dve'].append(lambda e, q=q: e.scalar_tensor_tensor(
                        out=sh[:, q, :], in0=dd[:, q, :], scalar=ptl[:, q:q + 1], in1=raw[:, q, 1:513],
                        op0=ALU.mult, op1=ALU.add))
                st = P.stage()
                for q in range(4):
                    fn = [AF.Tanh, AF.Copy, AF.Sigmoid, AF.Sigmoid][q]
                    st['act'].append(lambda e, q=q, fn=fn, c0=c0: e.activation(out=lora[:, q, c0:c0 + 512],
                                                                               in_=sh[:, q, :], func=fn))
            P.emit()
            et.close()
            if debug == 'lora':
                return nc

            for gi in range(2):
              for half in range(2):
                ST, STb = STs[gi], STbs[gi]
                et = ExitStack()
                raws = [sbt(et, f"raw{i}", [64, 3, 513], F32) for i in range(2)]
                shs = [sbt(et, f"sh{i}", [64, 3, 512], F32) for i in range(2)]
                tmpbs = [sbt(et, f"tmpb{i}", [64, 14, 512], F32) for i in range(2)]
                pps = [[pst(et, f"pp{i}_{q}", [128, 512], F32) for q in range(4)] for i in range(2)]
                P = Prog(nc, f"rwp{gi}{half}")
                for hl2 in range(4):
                  for tbl in range(2):
                    fns = []
                    for slot in range(2):
                      def unit(P, slot=slot, hl2=hl2, tbl=tbl):
                        hl = hl2 * 2 + slot
                        h = gi * 8 + hl
                        pc = h * 12
                        col = lambda k, pc=pc: ptab[:, pc + k:pc + k + 1]
                        cs = slice(h * 64, (h + 1) * 64)
                        raw, sh, tmpb = raws[slot], shs[slot], tmpbs[slot]
                        pA, pB, pC, pD = pps[slot]
                        dd = tmpb[:, 11:14, :]
                        tmp = [tmpb[:, i, :] for i in range(14)]
                        if True:
                            tb = half * 2 + tbl
                            c0 = tb * 512
                            l0 = tbl * 512
                            (kk, kk2, sg, aicl, gst, rinv, logw, t1, cum, kkn, k2, Wt, Winv, lp) = tmp
                            st = P.stage()
                        for q in range(3):
                            rows = slice(q * 1024 + h * 64, q * 1024 + (h + 1) * 64)
                            if tb == 0:
                                st['sp'].append(lambda e, q=q, rows=rows: e.dma_start(out=raw[:, q, 1:513],
                                                                                       in_=pT[rows, 0:512]))
                            else:
                                st['sp'].append(lambda e, q=q, rows=rows, c0=c0: e.dma_start(
                                    out=raw[:, q, 0:513], in_=pT[rows, c0 - 1:c0 + 512]))
                        if tb == 0:
                            st['pool'].append(lambda e: e.memset(raw[:, 0:3, 0:1], 0.0))
                        st = P.stage()
                        st['dve'].append(lambda e: e.tensor_tensor(out=dd[:], in0=raw[:, :, 0:512],
                                                                    in1=raw[:, :, 1:513], op=ALU.subtract))
                        for q in range(3):
                            st['dve'].append(lambda e, q=q, col=col: e.scalar_tensor_tensor(
                                out=sh[:, q, :], in0=dd[:, q, :], scalar=col(q), in1=raw[:, q, 1:513],
                                op0=ALU.mult, op1=ALU.add))
                        rs, ks, vs = sh[:, 0, :], sh[:, 1, :], sh[:, 2, :]
                        A3 = lambda ap: ap.rearrange("p (c n) -> p c n", n=64)
                        ARv = ARs[:, hl, tbl * 1024:(tbl + 1) * 1024].rearrange("p (c two n) -> p c two n", two=2, n=64)
                        st = P.stage()
                        st['pe'].append(lambda e, cs=cs, c0=c0: e.matmul(pA[0:64, :], w2b[0:96, cs], lora[0:96, 0, c0:c0 + 512],
                                                                           start=True, stop=True))
                        st['pe'].append(lambda e, cs=cs, c0=c0: e.matmul(pB[0:64, :], a2b[0:96, cs], lora[0:96, 1, c0:c0 + 512],
                                                                           start=True, stop=True))
                        st['pe'].append(lambda e, cs=cs, c0=c0: e.matmul(pC[0:64, :], g2b[:, 0, cs], lora[:, 2, c0:c0 + 512],
                                                                           start=True, stop=False))
                        st['pe'].append(lambda e, cs=cs, c0=c0: e.matmul(pC[0:64, :], g2b[:, 1, cs], lora[:, 3, c0:c0 + 512],
                                                                           start=False, stop=True))
                        st['act'].append(lambda e, col=col, ks=ks: e.activation(out=kk2[:], in_=ks, func=AF.Square,
                                                                                 scale=col(5)))
                        st['act'].append(lambda e, col=col, ks=ks: e.activation(out=kk[:], in_=ks, func=AF.Copy,
                                                                                 scale=col(5)))
                        st = P.stage()
                        st['pe'].append(lambda e: e.matmul(pD[0:64, :], ones64[:], kk2[:], start=True, stop=True))
                        st['act'].append(lambda e, col=col: e.activation(out=sg[:], in_=pA[0:64, :], func=AF.Sigmoid,
                                                                          bias=col(3)))
                        st['act'].append(lambda e, col=col: e.activation(out=aicl[:], in_=pB[0:64, :], func=AF.Sigmoid,
                                                                          bias=col(4)))
                        st['dve'].append(lambda e: e.tensor_copy(gst[:], pC[0:64, :]))
                        st = P.stage()
                        st['act'].append(lambda e: e.activation(out=rinv[:], in_=pD[0:64, :], func=AF.Ln, bias=1e-24))
                        st['act'].append(lambda e: e.activation(out=rinv[:], in_=rinv[:], func=AF.Exp, scale=-0.5))
                        st['act'].append(lambda e: e.mul(logw[:], sg[:], -0.6065306597126334))
                        st['dve'].append(lambda e, col=col, h=h: e.tensor_scalar(t1[:], aicl[:], col(6), omk[:, h:h + 1],
                                                                                  ALU.mult, ALU.add))
                        st['dve'].append(lambda e, ks=ks: e.tensor_tensor(out=k2[:], in0=ks, in1=t1[:], op=ALU.mult))
                        st['dve'].append(lambda e, rs=rs: e.tensor_tensor(out=kk2[:], in0=rs, in1=k2[:], op=ALU.mult))
                        st['dve'].append(lambda e, col=col: e.tensor_scalar(t1[:], kk2[:], col(8), None, ALU.mult))
                        st['sp'].append(lambda e, cs=cs, c0=c0: e.dma_start(out=gT[cs, c0:c0 + 512], in_=gst[:]))
                        st = P.stage()
                        st['dve'].append(lambda e: e.tensor_tensor_scan(out=cum[:], data0=cmask[:], data1=logw[:],
                                                                         initial=0.0, op0=ALU.mult, op1=ALU.add))
                        st['dve'].append(lambda e: e.tensor_tensor(out=lp[:], in0=cum[:], in1=logw[:], op=ALU.subtract))
                        st['dve'].append(lambda e: e.tensor_tensor(out=kkn[:], in0=kk[:], in1=rinv[:], op=ALU.mult))
                        st['dve'].append(lambda e: e.tensor_tensor(out=kk[:], in0=kkn[:], in1=aicl[:], op=ALU.mult))
                        st['pe'].append(lambda e: e.matmul(pA[0:64, :], ones64[:], t1[:], start=True, stop=True))
                        st['act'].append(lambda e, hl=hl, l0=l0, vs=vs: e.copy(Vs[:, hl, l0:l0 + 512], vs))
                        st = P.stage()
                        st['act'].append(lambda e: e.activation(out=Wt[:], in_=cum[:], func=AF.Exp))
                        st['act'].append(lambda e: e.activation(out=Winv[:], in_=cum[:], func=AF.Exp, scale=-1.0))
                        st['act'].append(lambda e: e.activation(out=sg[:], in_=lp[:], func=AF.Exp))
                        st['dve'].append(lambda e, vs=vs: e.tensor_tensor(out=gst[:], in0=pA[0:64, :], in1=vs, op=ALU.mult))
                        st = P.stage()
                        st['dve'].append(lambda e, rs=rs, ARv=ARv: e.tensor_tensor(out=ARv[:, :, 1, :], in0=A3(rs), in1=A3(Wt),
                                                                                    op=ALU.mult))
                        st['dve'].append(lambda e, ARv=ARv: e.scalar_tensor_tensor(out=ARv[:, :, 0, :], in0=A3(kkn), scalar=-1.0,
                                                                                    in1=A3(sg), op0=ALU.mult, op1=ALU.mult))
                        st['dve'].append(lambda e, hl=hl, l0=l0: e.tensor_tensor(out=Ks[:, hl, l0:l0 + 512], in0=k2[:],
                                                                                  in1=Winv[:], op=ALU.mult))
                        st['dve'].append(lambda e, hl=hl, l0=l0: e.tensor_tensor(out=Bs[:, hl, l0:l0 + 512], in0=kk[:],
                                                                                  in1=Winv[:], op=ALU.mult))
                        st['dve'].append(lambda e, hl=hl, tbl=tbl: e.tensor_copy(
                            WC[:, hl, tbl * 8:(tbl + 1) * 8], A3(Wt)[:, :, 63]))
                        st['sp'].append(lambda e, cs=cs, c0=c0: e.dma_start(out=bvT[cs, c0:c0 + 512], in_=gst[:]))
                      fns.append(unit)
                    merge_units(P, fns)
                P.emit()
                et.close()
                if debug == 'prep':
                    return nc

                et = ExitStack()
                ybig = sbt(et, "ybig", [64, 8, HT], F32)
                ptr1 = pst(et, "ptr1", [128, 1024], BF16)
                ptr2 = pst(et, "ptr2", [128, 1024], BF16)
                pABs = [pst(et, f"pAB{i}", [128, 512], F32) for i in range(2)]
                pKRs = [pst(et, f"pKR{i}", [128, 512], F32) for i in range(2)]

                def hb2(ts, h):
                    return ts[h // 4][0:64, (h % 4) * 128:(h % 4 + 1) * 128]
                pT_ = pst(et, "pT_", [128, 512], F32)
                pF = pst(et, "pF", [128, 512], F32)
                tmpS = sbt(et, "tmpS", [64, 8, 64], F32)
                tokms = [sbt(et, f"tokm{i}", [64, 3, 8, 64], BF16) for i in range(2)]
                ABm = sbt(et, "ABm", [64, 8, 128], BF16)
                AKm = sbt(et, "AKm", [64, 8, 128], BF16)
                akm2 = sbt(et, "akm2", [64, 8, 64], BF16)
                Pp = [sbt(et, f"Pp{i}", [64, 8, 64], BF16) for i in range(2)]
                ZQ = [sbt(et, f"ZQ{i}", [64, 8, 128], BF16) for i in range(2)]
                Zq = [sbt(et, f"Zq{i}", [64, 8, 64], F32) for i in range(2)]
                UTb = sbt(et, "UTb", [64, 8, 64], BF16)
                m2v = masks2[:, :].rearrange("p (a b) -> p a b", a=8)

                def v16(t):
                    return t[0:64, :].rearrange("p (a b) -> p a b", a=8)
                P = Prog(nc, f"rws{gi}{half}")
                for c in range(HT // CH):
                    tc = slice(c * 64, (c + 1) * 64)
                    ac = slice(c * 128, c * 128 + 64)
                    rc = slice(c * 128 + 64, c * 128 + 128)
                    arc = slice(c * 128, (c + 1) * 128)
                    tokm = tokms[c % 2]

                    def emit_T(stg, cn):
                        tcn = slice(cn * 64, (cn + 1) * 64)
                        for h in range(8):
                            hc = slice(h * 64, (h + 1) * 64)
                            stg['pe'].append(lambda e, h=h, hc=hc, tcn=tcn: e.transpose(ptr1[0:64, hc], Bs[:, h, tcn], id64))
                            stg['pe'].append(lambda e, h=h, hc=hc, tcn=tcn: e.transpose(
                                ptr1[0:64, 512 + h * 64:512 + (h + 1) * 64], Ks[:, h, tcn], id64))
                            stg['pe'].append(lambda e, h=h, hc=hc, tcn=tcn: e.transpose(ptr2[0:64, hc], Vs[:, h, tcn], id64))

                    def emit_Tevac(stg, cn):
                        tk = tokms[cn % 2]
                        stg['act'].append(lambda e, tk=tk: e.copy(tk[:, 0:2, :, :].rearrange("p a b c -> p (a b c)"), ptr1[0:64, :]))
                        stg['act'].append(lambda e, tk=tk: e.copy(tk[:, 2, :, :].rearrange("p b c -> p (b c)"), ptr2[0:64, 0:512]))
                    if c == 0:
                        st = P.stage()
                        emit_T(st, 0)
                        st = P.stage()
                        emit_Tevac(st, 0)
                    st = P.stage()
                    for h in range(8):
                        hc = slice(h * 64, (h + 1) * 64)
                        hc2 = slice(h * 128, (h + 1) * 128)
                        Bt, Kt = Bs[:, h, tc], Ks[:, h, tc]
                        At, ARc = ARs[:, h, ac], ARs[:, h, arc]
                        st['pe'].append(lambda e, h=h, Bt=Bt, ARc=ARc: e.matmul(hb2(pABs, h), Bt, ARc, start=True, stop=True))
                        st['pe'].append(lambda e, h=h, Kt=Kt, ARc=ARc: e.matmul(hb2(pKRs, h), Kt, ARc, start=True, stop=True))
                        st['pe'].append(lambda e, hc=hc, At=At, Bt=Bt: e.matmul(pT_[0:64, hc], At, Bt, start=True, stop=True))
                    st = P.stage()
                    for hb in range(2):
                        hs4 = slice(hb * 4, hb * 4 + 4)
                        st['dve'].append(lambda e, hs4=hs4, hb=hb: e.tensor_tensor(
                            out=AKm[:, hs4, :], in0=pKRs[hb][0:64, :].rearrange("p (a b) -> p a b", a=4), in1=m2v[:, hs4, :], op=ALU.mult))
                        st['dve'].append(lambda e, hs4=hs4, hb=hb: e.tensor_tensor(
                            out=ABm[:, hs4, :], in0=pABs[hb][0:64, :].rearrange("p (a b) -> p a b", a=4), in1=m2v[:, hs4, :], op=ALU.mult))
                    st['dve'].append(lambda e: e.tensor_tensor(out=ZQ[0][:, :, 64:128], in0=v8(pT_), in1=M3(2), op=ALU.mult))
                    st = P.stage()
                    for h in range(8):
                        hc = slice(h * 64, (h + 1) * 64)
                        At = ARs[:, h, ac]
                        VTh = tokm[:, 2, h, :]
                        st['pe'].append(lambda e, At=At, h=h, hc=hc: e.matmul(
                            pF[0:64, hc], At, STb[:, h, :], start=True, stop=False))
                        st['pe'].append(lambda e, h=h, VTh=VTh, hc=hc: e.matmul(
                            pF[0:64, hc], AKm[:, h, 0:64], VTh, start=False, stop=True))
                    st = P.stage()
                    st['act'].append(lambda e: e.copy(Zq[0][:], v8(pF)))
                    st['act'].append(lambda e: e.copy(ZQ[0][:, :, 0:64], v8(pF)))
                    for j in range(1, 7):
                        st = P.stage()
                        if j == 1 and c + 1 < HT // CH:
                            emit_T(st, c + 1)
                        zi, zo = Zq[(j - 1) % 2], Zq[j % 2]
                        zqi, zqo = ZQ[(j - 1) % 2], ZQ[j % 2]
                        for h in range(8):
                            hc = slice(h * 64, (h + 1) * 64)
                            hc2 = slice(h * 128, (h + 1) * 128)
                            Pj = ABm[:, h, 0:64] if j == 1 else Pp[j % 2][:, h, :]
                            if j < 6:
                                st['pe'].append(lambda e, Pj=Pj, zqi=zqi, h=h: e.matmul(
                                    hb2(pKRs, h), Pj, zqi[:, h, :], start=True, stop=True))
                                st['pe'].append(lambda e, hc=hc, Pj=Pj, zqi=zqi, h=h: e.matmul(
                                    pT_[0:64, hc], zqi[:, h, 64:128], Pj, start=True, stop=True))
                            else:
                                st['pe'].append(lambda e, Pj=Pj, zqi=zqi, h=h: e.matmul(
                                    hb2(pKRs, h)[:, 0:64], Pj, zqi[:, h, 0:64], start=True, stop=True))
                        st = P.stage()
                        for hb in range(2):
                            hs4 = slice(hb * 4, hb * 4 + 4)
                            pk = pKRs[hb][0:64, :].rearrange("p (a b) -> p a b", a=4)
                            zp = pk[:, :, 0:64]
                            if j < 6:
                                st['dve'].append(lambda e, zi=zi, zo=zo, zp=zp, hs4=hs4: e.tensor_tensor(
                                    out=zo[:, hs4, :], in0=zp, in1=zi[:, hs4, :], op=ALU.add))
                                st['dve'].append(lambda e, zi=zi, zqo=zqo, zp=zp, hs4=hs4: e.tensor_tensor(
                                    out=zqo[:, hs4, 0:64], in0=zp, in1=zi[:, hs4, :], op=ALU.add))
                                st['dve'].append(lambda e, zqo=zqo, pk=pk, hs4=hs4: e.tensor_copy(zqo[:, hs4, 64:128], pk[:, :, 64:128]))
                            else:
                                st['dve'].append(lambda e, zi=zi, zp=zp, hs4=hs4: e.tensor_tensor(
                                    out=UTb[:, hs4, :], in0=zp, in1=zi[:, hs4, :], op=ALU.add))
                        if j < 6:
                            st['act'].append(lambda e, j=j: e.copy(Pp[(j + 1) % 2][:], v8(pT_)))
                        if j == 2 and c + 1 < HT // CH:
                            emit_Tevac(st, c + 1)
                    st = P.stage()
                    for h in range(8):
                        hc = slice(h * 64, (h + 1) * 64)
                        Rt = ARs[:, h, rc]
                        VTh = tokm[:, 2, h, :]
                        BTh = tokm[:, 0, h, :]
                        KTh = tokm[:, 1, h, :]
                        oy = pABs[0][0:64, hc]
                        os_ = pABs[1][0:64, hc]
                        st['pe'].append(lambda e, oy=oy, h=h, Rt=Rt: e.matmul(oy, STb[:, h, :], Rt,
                                                                               start=True, stop=False))
                        st['pe'].append(lambda e, oy=oy, h=h: e.matmul(oy, UTb[:, h, :], ABm[:, h, 64:128],
                                                                        start=False, stop=False))
                        st['pe'].append(lambda e, oy=oy, h=h, VTh=VTh: e.matmul(oy, VTh, AKm[:, h, 64:128],
                                                                                 start=False, stop=True))
                        st['pe'].append(lambda e, os_=os_, h=h, BTh=BTh: e.matmul(os_, BTh, UTb[:, h, :],
                                                                                   start=True, stop=False))
                        st['pe'].append(lambda e, os_=os_, KTh=KTh, VTh=VTh: e.matmul(os_, KTh, VTh,
                                                                                       start=False, stop=True))
                    st = P.stage()
                    st['act'].append(lambda e, tc=tc: e.copy(ybig[:, :, tc], v8(pABs[0])))
                    st['dve'].append(lambda e: e.tensor_tensor(out=tmpS[:], in0=v8(pABs[1]), in1=ST[:], op=ALU.add))
                    wcb = WC[:, :, c].unsqueeze(2).to_broadcast([64, 8, 64])
                    st['dve'].append(lambda e, wcb=wcb: e.tensor_tensor(out=ST[:], in0=tmpS[:], in1=wcb, op=ALU.mult))
                    st['dve'].append(lambda e, wcb=wcb: e.tensor_tensor(out=STb[:], in0=tmpS[:], in1=wcb, op=ALU.mult))
                if debug == 'scan' and _os.environ.get('SCAN_STOP'):
                    P.stages = P.stages[:int(_os.environ['SCAN_STOP'])]
                P.emit()
                if debug == 'scan':
                    et.close()
                    return nc

                et2 = ExitStack()
                tmps = [[sbt(et2, f"tmp{s}_{i}", [64, 512], F32) for i in range(5)] for s in range(2)]
                P = Prog(nc, f"rwo{gi}{half}")
                for hl2 in range(4):
                  for tbl in range(2):
                    fns = []
                    for slot in range(2):
                      def unit(P, slot=slot, hl2=hl2, tbl=tbl):
                        hl = hl2 * 2 + slot
                        h = gi * 8 + hl
                        pc = h * 12
                        cs = slice(h * 64, (h + 1) * 64)
                        tmp = tmps[slot]
                        obf = obfs[slot]
                        pA_, pB_ = (pABs[0], pABs[1]) if slot == 0 else (pKRs[0], pKRs[1])
                        if True:
                            tb = half * 2 + tbl
                            c0 = tb * 512
                            l0 = tbl * 512
                            gl_, bvl, yc, sq, rs_ = tmp[0:5]
                            t_a, t_b, t_c = sq, yc, sq
                            st = P.stage()
                        st['sp'].append(lambda e, cs=cs, c0=c0: e.dma_start(out=gl_[:], in_=gT[cs, c0:c0 + 512]))
                        st['sp'].append(lambda e, cs=cs, c0=c0: e.dma_start(out=bvl[:], in_=bvT[cs, c0:c0 + 512]))
                        st['pe'].append(lambda e, hl=hl, l0=l0: e.matmul(pA_[0:64, :], ones64[:], ybig[:, hl, l0:l0 + 512],
                                                                          start=True, stop=True))
                        st = P.stage()
                        st['dve'].append(lambda e, hl=hl, l0=l0: e.scalar_tensor_tensor(
                            out=yc[:], in0=pA_[0:64, :], scalar=-1.0 / 64, in1=ybig[:, hl, l0:l0 + 512],
                            op0=ALU.mult, op1=ALU.add))
                        st = P.stage()
                        st['act'].append(lambda e: e.activation(out=sq[:], in_=yc[:], func=AF.Square))
                        st = P.stage()
                        st['pe'].append(lambda e: e.matmul(pB_[0:64, :], ones64[:], sq[:], start=True, stop=True))
                        st = P.stage()
                        st['act'].append(lambda e: e.activation(out=rs_[:], in_=pB_[0:64, :], func=AF.Ln, bias=64.0 * GN_EPS))
                        st['act'].append(lambda e: e.activation(out=rs_[:], in_=rs_[:], func=AF.Exp, scale=-0.5))
                        st = P.stage()
                        st['dve'].append(lambda e: e.tensor_tensor(out=t_a[:], in0=yc[:], in1=rs_[:], op=ALU.mult))
                        st['dve'].append(lambda e, h=h, pc=pc: e.tensor_scalar(t_b[:], t_a[:], g8[:, h:h + 1],
                                                                                 ptab[:, pc + 10:pc + 11],
                                                                                 ALU.mult, ALU.add))
                        st['dve'].append(lambda e: e.tensor_tensor(out=t_c[:], in0=t_b[:], in1=bvl[:], op=ALU.add))
                        st['dve'].append(lambda e: e.tensor_tensor(out=obf[:], in0=t_c[:], in1=gl_[:], op=ALU.mult))
                        st = P.stage()
                        st['sp'].append(lambda e, cs=cs, c0=c0: e.dma_start(out=ybuf[cs, c0:c0 + 512], in_=obf[:]))
                      fns.append(unit)
                    merge_units(P, fns)
                P.emit()
                et2.close()
                et.close()
                if debug == 'post':
                    return nc

        if debug == 'rwkv':
            return nc
        with ExitStack() as es:
            ptab = sbt(es, "ptabL", [128, 72], F32)
            cch = sbt(es, "cch", [128, 8], F32)
            c2h = sbt(es, "c2h", [128, 8], F32)
            etmp = sbt(es, "etmp", [128, 8], F32)
            wrb = sbt(es, "wrb", [128, 8, 128], BF16)
            wib = sbt(es, "wib", [128, 8, 128], BF16)
            xpad = sbt(es, "xpad", [128, 515], F32)
            gat = sbt(es, "gat", [128, 512], F32)
            tl = [sbt(es, f"tl{i}", [128, 512], F32) for i in range(12)]
            xcb = sbt(es, "xcb", [128, 512], BF16)
            ybf = sbt(es, "ybf", [128, 512], BF16)
            hcar = sbt(es, "hcar", [128, 1], F32)
            p1 = pst(es, "lp1", [128, 512], F32)
            p2 = pst(es, "lp2", [128, 512], F32)
            P = Prog(nc, "lru")
            st = P.stage()
            st['sp'].append(lambda e: e.dma_start(out=ptab[:], in_=ptabu_d))
            st['poolq'].append(lambda e: e.dma_start(out=wrb[:], in_=wr_d))
            st['poolq'].append(lambda e: e.dma_start(out=wib[:], in_=wi_d))
            lamv = ptab[:, 0:72].rearrange("p (j k) -> p j k", k=9)[:, :, 7]
            st = P.stage()
            st['act'].append(lambda e: e.activation(out=etmp[:], in_=lamv, func=AF.Exp, scale=-1.0))
            st = P.stage()
            st['act'].append(lambda e: e.activation(out=etmp[:], in_=etmp[:], func=AF.Ln, bias=1.0))
            st = P.stage()
            st['dve'].append(lambda e: e.tensor_scalar(cch[:], etmp[:], -8.0, None, ALU.mult))
            st['dve'].append(lambda e: e.tensor_scalar(c2h[:], etmp[:], -16.0, None, ALU.mult))
            for j in range(8):
                pc = j * 9
                col = lambda k, pc=pc: ptab[:, pc + k:pc + k + 1]
                for tb in range(4):
                    c0 = tb * 512
                    (xc0, xc1, x2, inner, inner2, sgm, gel, rg, ig, av, a2v, gx) = tl
                    rx = slice((28 + j) * 128, (29 + j) * 128)
                    rg_ = slice((36 + j) * 128, (37 + j) * 128)
                    st = P.stage()
                    if tb == 0:
                        st['sp'].append(lambda e, rx=rx: e.dma_start(out=xpad[:, 3:515], in_=pT[rx, 0:512]))
                        st['pool'].append(lambda e: e.memset(xpad[:, 0:3], 0.0))
                    else:
                        st['sp'].append(lambda e, rx=rx, c0=c0: e.dma_start(out=xpad[:], in_=pT[rx, c0 - 3:c0 + 512]))
                    st['sp'].append(lambda e, rg_=rg_, c0=c0: e.dma_start(out=gat[:], in_=pT[rg_, c0:c0 + 512]))
                    xc = xc1
                    st = P.stage()
                    st['dve'].append(lambda e, col=col: e.tensor_scalar(xc0[:], xpad[:, 0:512], col(0), col(4),
                                                                         ALU.mult, ALU.add))
                    st['dve'].append(lambda e, col=col: e.scalar_tensor_tensor(out=xc1[:], in0=xpad[:, 1:513], scalar=col(1),
                                                                                in1=xc0[:], op0=ALU.mult, op1=ALU.add))
                    st['dve'].append(lambda e, col=col: e.scalar_tensor_tensor(out=xc0[:], in0=xpad[:, 2:514], scalar=col(2),
                                                                                in1=xc1[:], op0=ALU.mult, op1=ALU.add))
                    st['dve'].append(lambda e, col=col: e.scalar_tensor_tensor(out=xc1[:], in0=xpad[:, 3:515], scalar=col(3),
                                                                                in1=xc0[:], op0=ALU.mult, op1=ALU.add))
                    st['act'].append(lambda e: e.activation(out=x2[:], in_=gat[:], func=AF.Square))
                    st = P.stage()
                    st['act'].append(lambda e: e.copy(xcb[:], xc[:]))
                    st['dve'].append(lambda e: e.tensor_scalar(inner[:], x2[:], 0.044715, 1.0, ALU.mult, ALU.add))
                    st['dve'].append(lambda e: e.tensor_tensor(out=inner2[:], in0=inner[:], in1=gat[:], op=ALU.mult))
                    st = P.stage()
                    st['pe'].append(lambda e, j=j: e.matmul(p1[:], wrb[:, j, :], xcb[:], start=True, stop=True))
                    st['pe'].append(lambda e, j=j: e.matmul(p2[:], wib[:, j, :], xcb[:], start=True, stop=True))
                    st['act'].append(lambda e: e.activation(out=sgm[:], in_=inner2[:], func=AF.Sigmoid,
                                                             scale=1.5957691216057308))
                    st = P.stage()
                    st['dve'].append(lambda e: e.tensor_copy(rg[:], p1[:]))
                    st['dve'].append(lambda e: e.tensor_copy(ig[:], p2[:]))
                    st['dve'].append(lambda e: e.tensor_tensor(out=gel[:], in0=gat[:], in1=sgm[:], op=ALU.mult))
                    st = P.stage()
                    st['act'].append(lambda e, col=col: e.activation(out=rg[:], in_=rg[:], func=AF.Sigmoid, bias=col(5)))
                    st['act'].append(lambda e, col=col: e.activation(out=ig[:], in_=ig[:], func=AF.Sigmoid, bias=col(6)))
                    st['act'].append(lambda e, j=j: e.activation(out=av[:], in_=rg[:], func=AF.Exp, scale=cch[:, j:j + 1]))
                    st['act'].append(lambda e, j=j: e.activation(out=a2v[:], in_=rg[:], func=AF.Exp, scale=c2h[:, j:j + 1]))
                    st = P.stage()
                    st['dve'].append(lambda e: e.tensor_tensor(out=gx[:], in0=ig[:], in1=xc[:], op=ALU.mult))
                    st['dve'].append(lambda e: e.tensor_scalar(x2[:], a2v[:], 1.0, -1.0, ALU.min, ALU.mult))
                    st = P.stage()
                    st['act'].append(lambda e: e.activation(out=inner[:], in_=x2[:], func=AF.Sqrt, bias=1.0))
                    st = P.stage()
                    if tb == 0:
                        st['dve'].append(lambda e: e.memset(inner[:, 0:1], 1.0))
                    st['dve'].append(lambda e: e.tensor_tensor(out=inner2[:], in0=inner[:], in1=gx[:], op=ALU.mult))
                    if tb == 0:
                        st['dve'].append(lambda e: e.tensor_tensor_scan(out=sgm[:], data0=av[:], data1=inner2[:],
                                                                         initial=0.0, op0=ALU.mult, op1=ALU.add))
                    else:
                        st['dve'].append(lambda e: e.tensor_tensor_scan(out=sgm[:], data0=av[:], data1=inner2[:],
                                                                         initial=hcar[:, 0:1], op0=ALU.mult, op1=ALU.add))
                    st['dve'].append(lambda e: e.tensor_tensor(out=ybf[:], in0=sgm[:], in1=gel[:], op=ALU.mult))
                    st = P.stage()
                    st['act'].append(lambda e: e.copy(hcar[:], sgm[:, 511:512]))
                    rows = slice(RW + j * 128, RW + (j + 1) * 128)
                    st['sp'].append(lambda e, rows=rows, c0=c0: e.dma_start(out=ybuf[rows, c0:c0 + 512], in_=ybf[:]))
            st = P.stage()
            if early:
                st['pool'].append(lambda e: e.memset(hcar[:], 0.0))
            else:
              for k in range(4):
                if k > 0:
                    st = P.stage()
                st['pool'].append(lambda e, k=k: e.collective_compute(
                    "AllGather", ALU.bypass, replica_groups=[[2 * i, 2 * i + 1] for i in range(ncores // 2)],
                    ins=[ybuf[k * 512:(k + 1) * 512, :].opt()], outs=[gbufs[k].opt()]))
            if debug is not None and _os.environ.get('LRU_STOP'):
                P.stages = P.stages[:int(_os.environ['LRU_STOP'])]
            P.emit()

        if early or debug == 'gather':
            return nc
        with ExitStack() as es:
            yT = sbt(es, "yT", [128, KC, TO], BF16)
            sel = sbt(es, "sel", [128, 2], F32)
            ngt = sbt(es, "ngt", [128, 16], F32)
            ones32 = sbt(es, "ones32", [128, 128], F32)
            rbc = sbt(es, "rbc", [128, TO], F32)
            with ExitStack() as es1:
                gl = [sbt(es1, f"gl{i}", [128, 2, TO], BF16) for i in range(3)]
                tb_ = [sbt(es1, f"tbl{i}", [128, TO], BF16) for i in range(2)]
                sqf = [sbt(es1, f"sqf{i}", [128, TO], F32) for i in range(2)]
                pq = [pst(es1, f"f1p{i}", [128, 512], F32) for i in range(2)]
                P = Prog(nc, "blend")
                st = P.stage()
                st['sp'].append(lambda e: e.dma_start(out=sel[:], in_=sel_d))
                st['sp'].append(lambda e: e.dma_start(out=ngt[:], in_=ngtab_d))
                st['pool'].append(lambda e: e.memset(ones32[:], 1.0))
                def gsrc(cc):
                    r, q = cc // 16, cc % 16
                    return gbufs[q // 4][r * 512 + (q % 4) * 128:r * 512 + (q % 4 + 1) * 128, :]
                st['sp'].append(lambda e: e.dma_start(out=gl[0][:].rearrange("p a b -> p (a b)"), in_=gsrc(0)))
                for cc in range(KC + 2):
                    st = P.stage()
                    if cc + 1 < KC:
                        st['sp'].append(lambda e, cc=cc: e.dma_start(
                            out=gl[(cc + 1) % 3][:].rearrange("p a b -> p (a b)"),
                            in_=gsrc(cc + 1)))
                    if cc < KC:
                        st['dve'].append(lambda e, cc=cc: e.tensor_scalar(tb_[cc % 2][:], gl[cc % 3][:, 0, :],
                                                                           sel[:, 0:1], None, ALU.mult))
                    if 1 <= cc <= KC:
                        c1 = cc - 1
                        st['dve'].append(lambda e, c1=c1: e.scalar_tensor_tensor(
                            out=yT[:, c1, :], in0=gl[c1 % 3][:, 1, :], scalar=sel[:, 1:2], in1=tb_[c1 % 2][:],
                            op0=ALU.mult, op1=ALU.add))
                lch = list(range(8, 16)) + list(range(24, 32))
                for q in range(17):
                    st = P.stage()
                    if q < 16:
                        st['act'].append(lambda e, q=q: e.activation(out=sqf[q % 2][:], in_=yT[:, lch[q], :],
                                                                      func=AF.Square))
                    if q >= 1:
                        for hf in range(2):
                            st['pe'].append(lambda e, q=q, hf=hf: e.matmul(
                                pq[hf][:], ones32[:], sqf[(q - 1) % 2][:, hf * 512:(hf + 1) * 512],
                                start=(q == 1), stop=(q == 16)))
                st = P.stage()
                for hf in range(2):
                    st['act'].append(lambda e, hf=hf: e.activation(out=rbc[:, hf * 512:(hf + 1) * 512], in_=pq[hf][:],
                                                                    func=AF.Sqrt, scale=1.0 / 2048, bias=EPS))
                st = P.stage()
                st['dve'].append(lambda e: e.reciprocal(rbc[:], rbc[:]))
                st = P.stage()
                for q in range(16):
                    eng = 'dve'
                    st[eng].append(lambda e, q=q: e.scalar_tensor_tensor(
                        out=yT[:, lch[q], :], in0=yT[:, lch[q], :], scalar=ngt[:, q:q + 1], in1=rbc[:],
                        op0=ALU.mult, op1=ALU.mult))
                P.emit()
            with ExitStack() as es2:
                wo = [sbt(es2, f"wo{i}", [128, KC, 512], BF16) for i in range(2)]
                xot = [sbt(es2, f"xot{i}", [128, 2, 512], F32) for i in range(2)]
                hst = [sbt(es2, f"hst{i}", [128, 2, 512], F32) for i in range(2)]
                po = [pst(es2, f"po{i}", [128, 512], F32) for i in range(4)]
                xov = xo.rearrange("(a p) n -> p a n", p=128)
                h1v6 = h1buf.rearrange("(a p) n -> p a n", p=128)
                P = Prog(nc, "outproj")
                st = P.stage()
                st['poolq'].append(lambda e: e.dma_start(out=wo[0][:].rearrange("p a b -> p (a b)"), in_=wout[0]))
                NU = 32
                for u in range(NU + 2):
                    st = P.stage()
                    if u < NU:
                        db, tp = u // 4, u % 4
                        if db + 1 < 8:
                            st['poolq'].append(lambda e, db=db, tp=tp: e.dma_start(
                                out=wo[(db + 1) % 2][:, tp * 8:(tp + 1) * 8, :].rearrange("p a b -> p (a b)"),
                                in_=wout[db + 1][:, tp * 4096:(tp + 1) * 4096]))
                        for a in range(2):
                            tcx = tp * 2 + a
                            for cc in range(KC):
                                st['pe'].append(lambda e, u=u, db=db, tcx=tcx, cc=cc, a=a: e.matmul(
                                    po[2 * (u % 2) + a][:], yT[:, cc, tcx * 128:(tcx + 1) * 128], wo[db % 2][:, cc, :],
                                    start=(cc == 0), stop=(cc == KC - 1)))
                        st['sp'].append(lambda e, u=u, db=db, tp=tp: e.dma_start(
                            out=xot[u % 2][:], in_=xov[:, tp * 2:tp * 2 + 2, db * 512:(db + 1) * 512]))
                    if 1 <= u <= NU:
                        u1 = u - 1
                        for a in range(2):
                            st['dve'].append(lambda e, u1=u1, a=a: e.tensor_tensor(
                                out=hst[u1 % 2][:, a, :], in0=po[2 * (u1 % 2) + a][:], in1=xot[u1 % 2][:, a, :], op=ALU.add))
                    if u >= 2:
                        u2 = u - 2
                        db2, tp2 = u2 // 4, u2 % 4
                        st['sp'].append(lambda e, u2=u2, db2=db2, tp2=tp2: e.dma_start(
                            out=h1v6[:, tp2 * 2:tp2 * 2 + 2, db2 * 512:(db2 + 1) * 512], in_=hst[u2 % 2][:]))
                P.emit()
        if debug == 'p6':
            return nc

        with ExitStack() as es:
            u2T = sbt(es, "u2T", [128, KC, TO], BF16)
            with ExitStack() as es1:
                norm_transpose("n2", es1, h1buf, g2bc, TO, u2T, ident)
            with ExitStack() as es2:
                wgb = [sbt(es2, f"wgb{i}", [128, KC, 128], BF16) for i in range(2)]
                wub = [sbt(es2, f"wub{i}", [128, KC, 128], BF16) for i in range(2)]
                sgs = [sbt(es2, f"sgs{i}", [128, TO], F32) for i in range(2)]
                ups = [sbt(es2, f"ups{i}", [128, TO], F32) for i in range(2)]
                hs = [sbt(es2, f"hs{i}", [128, TO], BF16) for i in range(2)]
                pg = [[pst(es2, f"pg{s}_{q}", [128, 512], F32) for q in range(4)] for s in range(2)]
                P = Prog(nc, "gateup")
                st = P.stage()
                st['poolq'].append(lambda e: e.dma_start(out=wgb[0][:].rearrange("p a b -> p (a b)"), in_=wg[0]))
                st['poolq'].append(lambda e: e.dma_start(out=wub[0][:].rearrange("p a b -> p (a b)"), in_=wu[0]))
                for f in range(NF + 3):
                    st = P.stage()
                    if f + 1 < NF:
                        st['poolq'].append(lambda e, f=f: e.dma_start(
                            out=wgb[(f + 1) % 2][:].rearrange("p a b -> p (a b)"), in_=wg[f + 1]))
                        st['poolq'].append(lambda e, f=f: e.dma_start(
                            out=wub[(f + 1) % 2][:].rearrange("p a b -> p (a b)"), in_=wu[f + 1]))
                    if f < NF:
                        for kc in range(KC):
                            for q in range(4):
                                wsrc = wgb if q < 2 else wub
                                st['pe'].append(lambda e, f=f, kc=kc, q=q, wsrc=wsrc: e.matmul(
                                    pg[f % 2][q][:], wsrc[f % 2][:, kc, :],
                                    u2T[:, kc, (q % 2) * 512:(q % 2 + 1) * 512],
                                    start=(kc == 0), stop=(kc == KC - 1)))
                    if 1 <= f <= NF:
                        f1 = f - 1
                        for hf in range(2):
                            st['act'].append(lambda e, f1=f1, hf=hf: e.activation(
                                out=sgs[f1 % 2][:, hf * 512:(hf + 1) * 512], in_=pg[f1 % 2][hf][:], func=AF.Silu))
                            st['dve'].append(lambda e, f1=f1, hf=hf: e.tensor_copy(
                                ups[f1 % 2][:, hf * 512:(hf + 1) * 512], pg[f1 % 2][2 + hf][:]))
                    if 2 <= f <= NF + 1:
                        f2 = f - 2
                        st['pool'].append(lambda e, f2=f2: e.tensor_tensor(out=hs[f2 % 2][:], in0=sgs[f2 % 2][:],
                                                                            in1=ups[f2 % 2][:], op=ALU.mult))
                    if 3 <= f:
                        f3 = f - 3
                        st['sp'].append(lambda e, f3=f3: e.dma_start(out=hT[f3 * 128:(f3 + 1) * 128, :],
                                                                      in_=hs[f3 % 2][:]))
                P.emit()
        if debug == 'p8':
            return nc

        with ExitStack() as es:
            hTs = sbt(es, "hTs", [128, NF, 512], BF16)
            GS = [(0, 22), (22, 44), (44, 65), (65, 86)]
            wdb = [sbt(es, f"wdb{i}", [128, 22, 512], BF16) for i in range(2)]
            h1p = [sbt(es, f"h1p{i}", [128, 4, 512], F32) for i in range(2)]
            h2s = [sbt(es, f"h2s{i}", [128, 4, 512], F32) for i in range(2)]
            pd = [[pst(es, f"pd{s}_{q}", [128, 512], F32) for q in range(4)] for s in range(2)]
            hTv = hT.rearrange("(f p) t -> p f t", p=128)
            h1v = h1buf.rearrange("(a p) n -> p a n", p=128)
            h2v = h2buf.rearrange("(a p) n -> p a n", p=128)
            P = Prog(nc, "down")
            units = [(th, db, g) for th in range(2) for db in range(8) for g in range(4)]

            def wload(st, ui):
                th, db, g = units[ui]
                f0, f1 = GS[g]
                st['poolq'].append(lambda e, ui=ui, db=db, f0=f0, f1=f1: e.dma_start(
                    out=wdb[ui % 2][:, 0:f1 - f0, :], in_=wdn[db, :, f0:f1, :]))
            st = P.stage()
            wload(st, 0)
            for ui in range(len(units) + 2):
                st = P.stage()
                if ui < len(units):
                    th, db, g = units[ui]
                    f0, f1 = GS[g]
                    if db == 0 and g == 0:
                        pass
                    if ui + 1 < len(units):
                        wload(st, ui + 1)
                    if g == 0:
                        st['sp'].append(lambda e, th=th, db=db: e.dma_start(
                            out=h1p[db % 2][:], in_=h1v[:, th * 4:(th + 1) * 4, db * 512:(db + 1) * 512]))
                    for ff in range(f0, f1):
                        for tq in range(4):
                            st['pe'].append(lambda e, ui=ui, db=db, ff=ff, f0=f0, tq=tq: e.matmul(
                                pd[db % 2][tq][:], hTs[:, ff, tq * 128:(tq + 1) * 128], wdb[ui % 2][:, ff - f0, :],
                                start=(ff == 0), stop=(ff == NF - 1)))
                if ui >= 1 and ui - 1 < len(units) and units[ui - 1][2] == 3:
                    th1, db1, _ = units[ui - 1]
                    for tq in range(4):
                        st['dve'].append(lambda e, db1=db1, tq=tq: e.tensor_tensor(
                            out=h2s[db1 % 2][:, tq, :], in0=pd[db1 % 2][tq][:], in1=h1p[db1 % 2][:, tq, :], op=ALU.add))
                if ui >= 2 and ui - 2 < len(units) and units[ui - 2][2] == 3:
                    th2, db2, _ = units[ui - 2]
                    st['sp'].append(lambda e, th2=th2, db2=db2: e.dma_start(
                        out=h2v[:, th2 * 4:(th2 + 1) * 4, db2 * 512:(db2 + 1) * 512], in_=h2s[db2 % 2][:]))
                if ui + 1 < len(units) and units[ui + 1][1] == 0 and units[ui + 1][2] == 0 and ui + 1 > 0:
                    pass
            stages = P.stages
            def hload_stage(th):
                stl = {e: [] for e in ENGS}
                for k in range(0, NF, 11):
                    k1 = min(NF, k + 11)
                    stl['sp'].append(lambda e, k=k, k1=k1, th=th: e.dma_start(
                        out=hTs[:, k:k1, :], in_=hTv[:, k:k1, th * 512:(th + 1) * 512]))
                return stl
            new = [stages[0], hload_stage(0)]
            for si in range(1, len(stages)):
                ui = si - 1
                if ui == 32:
                    new.append(hload_stage(1))
                new.append(stages[si])
            P.stages = new
            P.emit()
        if debug == 'p9':
            return nc

        with ExitStack() as es:
            gbc = sbt(es, "g3", [128, D], F32)
            xt = [sbt(es, f"fxt{i}", [128, D], F32) for i in range(2)]
            ot = [sbt(es, f"fot{i}", [128, D], F32) for i in range(2)]
            junk = sbt(es, "fjunk", [128, D], BF16)
            ss = sbt(es, "fss", [128, 8], F32)
            rstd = sbt(es, "frstd", [128, 8], F32)
            P = Prog(nc, "fin")
            st = P.stage()
            st['sp'].append(lambda e: e.dma_start(out=gbc[:], in_=g3bc))
            st['sp'].append(lambda e: e.dma_start(out=xt[0][:], in_=h2buf[0:128, :]))
            st['dve'].append(lambda e: e.memset(ss[:], 0.0))
            NTL = TO // 128
            for i in range(NTL):
                b = i % 2
                st = P.stage()
                st['act'].append(lambda e, b=b, i=i: e.activation(out=junk[:], in_=xt[b][:], func=AF.Square,
                                                                   accum_out=ss[:, i:i + 1]))
                if i + 1 < NTL:
                    st['sp'].append(lambda e, i=i: e.dma_start(out=xt[(i + 1) % 2][:],
                                                               in_=h2buf[(i + 1) * 128:(i + 2) * 128, :]))
                st = P.stage()
                st['act'].append(lambda e, i=i: e.activation(out=rstd[:, i:i + 1], in_=ss[:, i:i + 1], func=AF.Sqrt,
                                                              scale=1.0 / D, bias=EPS))
                st = P.stage()
                st['dve'].append(lambda e, i=i: e.reciprocal(rstd[:, i:i + 1], rstd[:, i:i + 1]))
                st = P.stage()
                st['dve'].append(lambda e, b=b, i=i: e.scalar_tensor_tensor(out=ot[b][:], in0=xt[b][:],
                                                                             scalar=rstd[:, i:i + 1], in1=gbc[:],
                                                                             op0=ALU.mult, op1=ALU.mult))
                st = P.stage()
                st['sp'].append(lambda e, b=b, i=i: e.dma_start(out=out[i * 128:(i + 1) * 128, :], in_=ot[b][:]))
            P.emit()
    return nc


def _tile_cols(w, cols_list):
    nch = len(cols_list)
    outw = np.zeros((nch, 128, KC, 128), np.float32)
    for j, cols in enumerate(cols_list):
        blk = w[:, cols]
        outw[j, :, :, :blk.shape[1]] = blk.reshape(KC, 128, -1).transpose(1, 0, 2)
    return outw.reshape(nch, 128, KC * 128)


def _prep(inp):
    f = lambda k: np.asarray(inp[k], dtype=np.float32)
    x = f("x")
    w_in = f("w_in")[0]
    mu = f("mu_shift")[0]
    R = 2048
    o3 = 3 * R
    g1bc = np.ascontiguousarray(np.broadcast_to(f("norm_mix_g")[0][None, :], (128, D)))
    g2bc = np.ascontiguousarray(np.broadcast_to(f("norm_ffn_g")[0][None, :], (128, D)))
    g3bc = np.ascontiguousarray(np.broadcast_to(f("norm_final_g")[None, :], (128, D)))
    w_out = f("w_out")[0]
    perm = np.concatenate([np.arange(0, 1024), np.arange(2048, 3072), np.arange(1024, 2048), np.arange(3072, 4096)])
    wop = w_out[perm]
    wout_t = np.ascontiguousarray(wop.reshape(KC, 128, 8, 512).transpose(2, 1, 0, 3)).reshape(8, 128, KC * 512)
    wgate = f("ffn_w_gate")[0]
    wup = f("ffn_w_up")[0]
    wdown = f("ffn_w_down")[0]
    wg_t = np.ascontiguousarray(wgate.reshape(KC, 128, NF, 128).transpose(2, 1, 0, 3)).reshape(NF, 128, KC * 128)
    wu_t = np.ascontiguousarray(wup.reshape(KC, 128, NF, 128).transpose(2, 1, 0, 3)).reshape(NF, 128, KC * 128)
    wdn_t = np.ascontiguousarray(wdown.reshape(NF, 128, 8, 512).transpose(2, 1, 0, 3))
    ii = np.arange(64)
    su = (ii[:, None] < ii[None, :]).astype(np.float32)
    iu = (ii[:, None] <= ii[None, :]).astype(np.float32)
    sl = (ii[:, None] > ii[None, :]).astype(np.float32)
    masks = np.stack([np.tile(m, (1, 8)) for m in (su, iu, sl)], axis=1).astype(np.float32)
    masks2 = np.ascontiguousarray(np.tile(np.concatenate([su, iu], axis=1), (1, 8)).astype(np.float32))
    bones = np.kron(np.eye(2, dtype=np.float32), np.ones((64, 64), np.float32))
    cmask = np.ones((128, 512), np.float32)
    cmask[:, ::64] = 0.0
    ngtab = np.ascontiguousarray(f("lru_norm_g")[0].reshape(16, 128).T)
    per_half = {}
    for hh in range(2):
        rsl = np.arange(hh * 1024, (hh + 1) * 1024)
        cols_list = []
        for base in (0, R, 2 * R):
            for j in range(8):
                cols_list.append(base + rsl[j * 128:(j + 1) * 128])
        cols_list.append(np.arange(o3, o3 + 96))
        cols_list.append(np.arange(o3 + 96, o3 + 192))
        cols_list.append(np.arange(o3 + 192, o3 + 320))
        cols_list.append(np.arange(o3 + 320, o3 + 448))
        lb = o3 + 448
        for base in (lb, lb + R):
            for j in range(8):
                cols_list.append(base + rsl[j * 128:(j + 1) * 128])
        win_t = _tile_cols(w_in, cols_list)
        ptabh = np.zeros((64, 192), np.float32)
        ptabl = np.zeros((128, 4), np.float32)
        ptabu = np.zeros((128, 72), np.float32)
        pv = {k: f(k)[0] for k in ("rwkv_w0", "rwkv_a0", "rwkv_k_k", "rwkv_k_a", "rwkv_ln_g", "rwkv_ln_b",
                                   "conv_b", "lru_br", "lru_bi", "lru_lambda")}
        rk = f("rwkv_r_k")[0].reshape(-1)
        cw = f("conv_w")[0]
        for h in range(16):
            ch = rsl[h * 64:(h + 1) * 64]
            b = h * 12
            ptabh[:, b + 0] = mu[ch]
            ptabh[:, b + 1] = mu[R + ch]
            ptabh[:, b + 2] = mu[2 * R + ch]
            ptabh[:, b + 3] = pv["rwkv_w0"][ch]
            ptabh[:, b + 4] = pv["rwkv_a0"][ch]
            ptabh[:, b + 5] = pv["rwkv_k_k"][ch]
            ptabh[:, b + 6] = pv["rwkv_k_a"][ch]
            ptabh[:, b + 8] = rk[ch]
            ptabh[:, b + 9] = pv["rwkv_ln_g"][ch]
            ptabh[:, b + 10] = pv["rwkv_ln_b"][ch]
        for j in range(8):
            ch = rsl[j * 128:(j + 1) * 128]
            b = j * 9
            for k in range(4):
                ptabu[:, b + k] = cw[k, ch]
            ptabu[:, b + 4] = pv["conv_b"][ch]
            ptabu[:, b + 5] = pv["lru_br"][ch]
            ptabu[:, b + 6] = pv["lru_bi"][ch]
            ptabu[:, b + 7] = pv["lru_lambda"][ch]
        ptabl[:96, 0] = mu[o3:o3 + 96]
        ptabl[:96, 1] = mu[o3 + 96:o3 + 192]
        ptabl[:, 2] = mu[o3 + 192:o3 + 320]
        ptabl[:, 3] = mu[o3 + 320:o3 + 448]
        per_half[hh] = dict(
            win=win_t, ptabh=ptabh, ptabl=ptabl, ptabu=ptabu,
            w2=np.ascontiguousarray(f("rwkv_w2")[0][:, rsl]),
            a2=np.ascontiguousarray(f("rwkv_a2")[0][:, rsl]),
            g2w=np.ascontiguousarray(f("rwkv_g2")[0][:, rsl].reshape(2, 128, 1024).transpose(1, 0, 2)),
            wr=np.ascontiguousarray(f("lru_wr")[0][hh * 8:(hh + 1) * 8].transpose(1, 0, 2)),
            wi=np.ascontiguousarray(f("lru_wi")[0][hh * 8:(hh + 1) * 8].transpose(1, 0, 2)),
        )
    in_maps = []
    for c in range(8):
        b, hh = c // 2, c % 2
        sel = np.zeros((128, 2), np.float32)
        sel[:, hh] = 1.0
        m = dict(xb=np.ascontiguousarray(x[b]), xo=np.ascontiguousarray(x[b, hh * TO:(hh + 1) * TO]),
                 g1bc=g1bc, g2bc=g2bc, g3bc=g3bc, ngtab=ngtab, masks=masks, masks2=masks2, bones=bones, cmask=cmask,
                 wout=wout_t, wg=wg_t, wu=wu_t, wdn=wdn_t, sel=sel)
        m.update(per_half[hh])
        in_maps.append(m)
    return in_maps


def kernel(**inp):
    in_maps = _prep(inp)
    nc = build_nc()
    res = run_bass_kernel_spmd(nc, in_maps, core_ids=list(range(8)))
    outp = np.zeros((4, T, D), np.float32)
    for c in range(8):
        b, hh = c // 2, c % 2
        outp[b, hh * TO:(hh + 1) * TO] = res.results[c]["out"]
    return outp
```

```python
import os as _os
import numpy as np
from contextlib import ExitStack
import concourse.bass as bass
import concourse.mybir as mybir
from concourse.bass_utils import run_bass_kernel_spmd

F32, BF16 = mybir.dt.float32, mybir.dt.bfloat16
AF = mybir.ActivationFunctionType
ALU = mybir.AluOpType

T = 2048
TO = 1024
D = 4096
KC = 32
NCH = 44
DFF = 11008
NF = 86
RW = 1024
CH = 64
NCK = T // CH
NPT = 176
EPS = 1e-6
GN_EPS = 64e-5

ENGS = ['pe', 'dve', 'act', 'pool', 'sp', 'poolq', 'actq']
HOST = {'pe': 'tensor', 'dve': 'vector', 'act': 'scalar', 'pool': 'gpsimd', 'sp': 'sync',
        'poolq': 'gpsimd', 'actq': 'scalar'}


class Prog:
    cnt = 0

    def __init__(self, nc, name):
        self.nc = nc
        self.name = name
        self.stages = []

    def stage(self):
        st = {e: [] for e in ENGS}
        self.stages.append(st)
        return st

    def emit(self):
        nc = self.nc
        with ExitStack() as es:
            Prog.cnt += 1
            sems = {e: es.enter_context(nc.semaphore(f"{self.name}{Prog.cnt}_{e}")) for e in ENGS}
            inc = {e: (16 if e in ('sp', 'poolq', 'actq') else 1) for e in ENGS}
            cum = {e: [] for e in ENGS}
            run = {e: 0 for e in ENGS}
            for st in self.stages:
                for e in ENGS:
                    if st[e]:
                        run[e] += inc[e] * (len(st[e]) if inc[e] == 16 else 1)
                    cum[e].append(run[e])
            stages = self.stages
            with nc.Block() as blk0:
                def clr(eng):
                    for e in ENGS:
                        eng.sem_clear(sems[e])
                blk0.gpsimd(clr)
            blk = es.enter_context(nc.Block())

            def make(hosteng):
                mine = [e for e in ENGS if HOST[e] == hosteng]

                def f(eng):
                    waited = {e: 0 for e in ENGS}
                    for k, st in enumerate(stages):
                        if not any(st[e] for e in mine):
                            continue
                        if k > 0:
                            for x in ENGS:
                                need = cum[x][k - 1]
                                if need > waited[x]:
                                    eng.wait_ge(sems[x], need)
                                    waited[x] = need
                        for e in mine:
                            ops = st[e]
                            for i, op in enumerate(ops):
                                ins = op(eng)
                                if inc[e] == 16:
                                    ins.then_inc(sems[e], 16)
                                elif i == len(ops) - 1:
                                    ins.then_inc(sems[e], 1)
                    for x in ENGS:
                        if run[x] > waited[x]:
                            eng.wait_ge(sems[x], run[x])
                return f
            blk.tensor(make('tensor'))
            blk.vector(make('vector'))
            blk.scalar(make('scalar'))
            blk.gpsimd(make('gpsimd'))
            blk.sync(make('sync'))


def build_nc(debug=None, ncores=8):
    early = debug in ('p1', 'p2', 'lora', 'prep', 'scan', 'post', 'rwkv', 'mix')
    nc = bass.Bass("TRN2", target_bir_lowering=False)

    MIXKEEP = ("xb", "g1bc", "win", "ptabh", "ptabl", "ptabu", "masks2", "w2", "a2", "g2w", "wr", "wi", "masks", "bones", "cmask")

    def din(name, shape, dt=F32):
        if early and name not in MIXKEEP:
            shape = [1, 1]
        return nc.dram_tensor(name, shape, dt, kind="ExternalInput").ap()

    xb = din("xb", [T, D])
    xo = din("xo", [TO, D])
    win = din("win", [NCH, 128, KC * 128])
    g1bc = din("g1bc", [128, D])
    g2bc = din("g2bc", [128, D])
    g3bc = din("g3bc", [128, D])
    ptabh_d = din("ptabh", [64, 192])
    ptabl_d = din("ptabl", [128, 4])
    ptabu_d = din("ptabu", [128, 72])
    ngtab_d = din("ngtab", [128, 16])
    w2_d = din("w2", [96, RW])
    a2_d = din("a2", [96, RW])
    g2w_d = din("g2w", [128, 2, RW])
    wr_d = din("wr", [128, 8, 128])
    wi_d = din("wi", [128, 8, 128])
    masks_d = din("masks", [64, 3, 512])
    masks2_d = din("masks2", [64, 1024])
    bones_d = din("bones", [128, 128])
    cmask_d = din("cmask", [128, 512])
    wout = din("wout", [8, 128, KC * 512])
    wg = din("wg", [NF, 128, KC * 128])
    wu = din("wu", [NF, 128, KC * 128])
    wdn = din("wdn", [8, 128, NF, 512])
    sel_d = din("sel", [128, 2])
    out = nc.dram_tensor("out", [TO, D], F32, kind="ExternalOutput").ap()

    def dscr(name, shape, dt):
        return nc.dram_tensor(name, shape, dt).ap()

    if not early:
        pT = dscr("pT", [NCH * 128, T], F32)
    gT = dscr("gT", [RW, T], F32)
    bvT = dscr("bvT", [RW, T], F32)
    if early:
        ybuf = nc.dram_tensor("ybuf", [2048, T], BF16, kind="ExternalOutput").ap()
        pT = nc.dram_tensor("pT", [NCH * 128, T], F32, kind="ExternalOutput").ap()
    else:
        ybuf = dscr("ybuf", [2048, T], BF16)
    gbufs = [dscr(f"gbuf{k}", [1024, T], BF16) for k in range(4)]
    h1buf = dscr("h1buf", [TO, D], F32)
    hT = dscr("hT", [DFF, TO], BF16)
    h2buf = dscr("h2buf", [TO, D], F32)

    uid = [0]

    def sbt(es, name, shape, dt):
        uid[0] += 1
        return es.enter_context(nc.sbuf_tensor(f"s{uid[0]}_{name}", shape, dt))

    def pst(es, name, shape, dt):
        uid[0] += 1
        return es.enter_context(nc.psum_tensor(f"p{uid[0]}_{name}", shape, dt))

    def norm_transpose(name, es, src, gbc_d, ntok, dstT, ident):
        ntile = ntok // 128
        gbc = sbt(es, name + "gbc", [128, D], F32)
        xt = [sbt(es, f"{name}xt{i}", [128, D], F32) for i in range(2)]
        xn = [sbt(es, f"{name}xn{i}", [128, D], BF16) for i in range(2)]
        junk = sbt(es, name + "junk", [128, D], BF16)
        ss = sbt(es, name + "ss", [128, 2], F32)
        rstd = sbt(es, name + "rstd", [128, 2], F32)
        ptr = [pst(es, f"{name}ptr{i}", [128, 1024], BF16) for i in range(4)]
        P = Prog(nc, name)
        st = P.stage()
        st['sp'].append(lambda e: e.dma_start(out=gbc[:], in_=gbc_d))
        st['sp'].append(lambda e: e.dma_start(out=xt[0][:], in_=src[0:128, :]))
        st['dve'].append(lambda e: e.memset(ss[:], 0.0))
        for i in range(ntile + 1):
            b = i % 2
            pb = (i - 1) % 2
            s1 = P.stage()
            if i < ntile:
                s1['act'].append(lambda e, b=b: e.activation(out=junk[:], in_=xt[b][:], func=AF.Square,
                                                               accum_out=ss[:, b:b + 1]))
            if i + 1 < ntile:
                s1['sp'].append(lambda e, i=i: e.dma_start(out=xt[(i + 1) % 2][:],
                                                           in_=src[(i + 1) * 128:(i + 2) * 128, :]))
            if i >= 1:
                for kc in range(KC):
                    s1['pe'].append(lambda e, kc=kc, pb=pb: e.transpose(
                        ptr[kc // 8][:, (kc % 8) * 128:(kc % 8 + 1) * 128],
                        xn[pb][:, kc * 128:(kc + 1) * 128], ident[:]))
            s2 = P.stage()
            if i < ntile:
                s2['act'].append(lambda e, b=b: e.activation(out=rstd[:, b:b + 1], in_=ss[:, b:b + 1], func=AF.Sqrt,
                                                              scale=1.0 / D, bias=EPS))
            if i >= 1:
                t0 = (i - 1) * 128
                for q in range(4):
                    eng = 'act' if q % 2 == 0 else 'dve'
                    o = dstT[:, q * 8:(q + 1) * 8, t0:t0 + 128]
                    src_ps = ptr[q][:].rearrange("p (a b) -> p a b", a=8)
                    if eng == 'act':
                        s2['act'].append(lambda e, o=o, s=src_ps: e.copy(o, s))
                    else:
                        s2['dve'].append(lambda e, o=o, s=src_ps: e.tensor_copy(o, s))
            if i < ntile:
                s3 = P.stage()
                s3['dve'].append(lambda e, b=b: e.reciprocal(rstd[:, b:b + 1], rstd[:, b:b + 1]))
                s3['pool'].append(lambda e, b=b: e.memset(ss[:, b:b + 1], 0.0))
                s4 = P.stage()
                s4['dve'].append(lambda e, b=b: e.scalar_tensor_tensor(out=xn[b][:], in0=xt[b][:],
                                                                        scalar=rstd[:, b:b + 1], in1=gbc[:],
                                                                        op0=ALU.mult, op1=ALU.mult))
        P.emit()

    with ExitStack() as top:
        ident = sbt(top, "ident", [128, 128], BF16)
        P = Prog(nc, "init")
        st = P.stage()
        st['pool'].append(lambda e: e.memset(ident[:], 0.0))
        st = P.stage()
        st['pool'].append(lambda e: e.affine_select(out=ident[:], in_=ident[:], pattern=[[-1, 128]],
                                                    compare_op=ALU.not_equal, fill=1.0, base=0,
                                                    channel_multiplier=1))
        P.emit()

        with ExitStack() as es:
            uT = sbt(es, "uT", [128, KC, T], BF16)
            with ExitStack() as es1:
                norm_transpose("n1", es1, xb, g1bc, T, uT, ident)
            if debug == 'p1':
                return nc
            with ExitStack() as es2:
                wb = [sbt(es2, f"wb{i}", [128, KC, 128], BF16) for i in range(2)]
                ost = [sbt(es2, f"ost{i}", [128, T], F32) for i in range(2)]
                pp = [[pst(es2, f"pp{s}_{q}", [128, 512], F32) for q in range(4)] for s in range(2)]
                P = Prog(nc, "inproj")
                st = P.stage()
                st['poolq'].append(lambda e: e.dma_start(out=wb[0][:].rearrange("p a b -> p (a b)"), in_=win[0]))
                for s in range(NCH + 2):
                    st = P.stage()
                    if s + 1 < NCH:
                        st['poolq'].append(lambda e, s=s: e.dma_start(
                            out=wb[(s + 1) % 2][:].rearrange("p a b -> p (a b)"), in_=win[s + 1]))
                    if s < NCH:
                        for kc in range(KC):
                            for q in range(4):
                                st['pe'].append(lambda e, s=s, kc=kc, q=q: e.matmul(
                                    pp[s % 2][q][:], wb[s % 2][:, kc, :], uT[:, kc, q * 512:(q + 1) * 512],
                                    start=(kc == 0), stop=(kc == KC - 1)))
                    if 1 <= s <= NCH:
                        j = s - 1
                        for q in range(4):
                            o = ost[j % 2][:, q * 512:(q + 1) * 512]
                            if q % 2 == 0:
                                st['act'].append(lambda e, o=o, j=j, q=q: e.copy(o, pp[j % 2][q][:]))
                            else:
                                st['dve'].append(lambda e, o=o, j=j, q=q: e.tensor_copy(o, pp[j % 2][q][:]))
                    if 2 <= s:
                        j = s - 2
                        st['sp'].append(lambda e, j=j: e.dma_start(out=pT[j * 128:(j + 1) * 128, :],
                                                                    in_=ost[j % 2][:]))
                P.emit()
            if debug == 'p2':
                return nc

        with ExitStack() as es:
            ptab = sbt(es, "ptabh", [64, 192], F32)
            ptl = sbt(es, "ptabl", [128, 4], F32)
            omk = sbt(es, "omk", [64, 16], F32)
            g8 = sbt(es, "g8", [64, 16], F32)
            lora = sbt(es, "lora", [128, 4, T], BF16)
            w2b = sbt(es, "w2b", [96, RW], BF16)
            a2b = sbt(es, "a2b", [96, RW], BF16)
            g2b = sbt(es, "g2b", [128, 2, RW], BF16)
            masks = sbt(es, "masks", [64, 3, 512], F32)
            masks2 = sbt(es, "masks2", [64, 1024], F32)
            ones64 = sbt(es, "ones64", [64, 64], F32)
            cmask = sbt(es, "cmask", [64, 512], F32)
            HT = 1024
            ARs = sbt(es, "ARs", [64, 8, 2 * HT], BF16)
            Bs = sbt(es, "Bs", [64, 8, HT], BF16)
            Ks = sbt(es, "Ks", [64, 8, HT], BF16)
            Vs = sbt(es, "Vs", [64, 8, HT], BF16)
            WC = sbt(es, "WC", [64, 8, 16], F32)
            obfs = [sbt(es, f"obf{i}", [64, 512], BF16) for i in range(2)]
            STs = [sbt(es, f"ST{g}", [64, 8, 64], F32) for g in range(2)]
            STbs = [sbt(es, f"STb{g}", [64, 8, 64], BF16) for g in range(2)]
            id64 = ident[0:64, 0:64]

            def merge_units(P, fns):
                subs = []
                for fn in fns:
                    Pu = Prog(nc, "sub")
                    fn(Pu)
                    subs.append(Pu.stages)
                n = len(subs[0])
                assert all(len(x) == n for x in subs)
                for k in range(n):
                    stg = P.stage()
                    for x in subs:
                        for e in ENGS:
                            stg[e].extend(x[k][e])

            def M3(i):
                return masks[:, i, :].rearrange("p (a b) -> p a b", a=8)

            def v8(t):
                return t[0:64, :].rearrange("p (a b) -> p a b", a=8)

            et = ExitStack()
            raw = sbt(et, "raw", [128, 4, 513], F32)
            dd = sbt(et, "dd", [128, 4, 512], F32)
            sh = sbt(et, "sh", [128, 4, 512], F32)
            P = Prog(nc, "rw0")
            st = P.stage()
            for (dst, srcd) in ((ptab, ptabh_d), (ptl, ptabl_d), (masks, masks_d), (masks2, masks2_d), (cmask, cmask_d[0:64, :])):
                st['sp'].append(lambda e, dst=dst, srcd=srcd: e.dma_start(out=dst[:], in_=srcd))
            for (dst, srcd) in ((w2b, w2_d), (a2b, a2_d), (g2b, g2w_d)):
                st['poolq'].append(lambda e, dst=dst, srcd=srcd: e.dma_start(out=dst[:], in_=srcd))
            st['pool'].append(lambda e: e.memset(ones64[:], 1.0))
            for g in range(2):
                st['dve'].append(lambda e, g=g: e.memset(STs[g][:], 0.0))
                st['pool'].append(lambda e, g=g: e.memset(STbs[g][:], 0.0))
            st = P.stage()
            kav = ptab[:, :].rearrange("p (h k) -> p h k", k=12)[:, :, 6]
            lgv = ptab[:, :].rearrange("p (h k) -> p h k", k=12)[:, :, 9]
            st['dve'].append(lambda e: e.tensor_scalar(omk[:], kav, -1.0, 1.0, ALU.mult, ALU.add))
            st['dve'].append(lambda e: e.tensor_scalar(g8[:], lgv, 8.0, None, ALU.mult))
            for tb in range(4):
                c0 = tb * 512
                st = P.stage()
                for q in range(4):
                    rows = slice((24 + q) * 128, (25 + q) * 128)
                    if tb == 0:
                        st['sp'].append(lambda e, q=q, rows=rows: e.dma_start(out=raw[:, q, 1:513],
                                                                               in_=pT[rows, 0:512]))
                    else:
                        st['sp'].append(lambda e, q=q, rows=rows, c0=c0: e.dma_start(
                            out=raw[:, q, 0:513], in_=pT[rows, c0 - 1:c0 + 512]))
                if tb == 0:
                    st['pool'].append(lambda e: e.memset(raw[:, :, 0:1], 0.0))
                st = P.stage()
                st['dve'].append(lambda e: e.tensor_tensor(out=dd[:], in0=raw[:, :, 0:512], in1=raw[:, :, 1:513],
                                                            op=ALU.subtract))
                st = P.stage()
                for q in range(4):
                    st['dve'].append(lambda e, q=q: e.scalar_tensor_tensor(
                        out=sh[:, q, :], in0=dd[:, q, :], scalar=ptl[:, q:q + 1], in1=raw[:, q, 1:513],
                        op0=ALU.mult, op1=ALU.add))
                st = P.stage()
                for q in range(4):
                    fn = [AF.Tanh, AF.Copy, AF.Sigmoid, AF.Sigmoid][q]
                    st['act'].append(lambda e, q=q, fn=fn, c0=c0: e.activation(out=lora[:, q, c0:c0 + 512],
                                                                               in_=sh[:, q, :], func=fn))
            P.emit()
            et.close()
            if debug == 'lora':
                return nc

            for gi in range(2):
              for half in range(2):
                ST, STb = STs[gi], STbs[gi]
                et = ExitStack()
                raws = [sbt(et, f"raw{i}", [64, 3, 513], F32) for i in range(2)]
                shs = [sbt(et, f"sh{i}", [64, 3, 512], F32) for i in range(2)]
                tmpbs = [sbt(et, f"tmpb{i}", [64, 14, 512], F32) for i in range(2)]
                pps = [[pst(et, f"pp{i}_{q}", [128, 512], F32) for q in range(4)] for i in range(2)]
                P = Prog(nc, f"rwp{gi}{half}")
                for hl2 in range(4):
                  for tbl in range(2):
                    fns = []
                    for slot in range(2):
                      def unit(P, slot=slot, hl2=hl2, tbl=tbl):
                        hl = hl2 * 2 + slot
                        h = gi * 8 + hl
                        pc = h * 12
                        col = lambda k, pc=pc: ptab[:, pc + k:pc + k + 1]
                        cs = slice(h * 64, (h + 1) * 64)
                        raw, sh, tmpb = raws[slot], shs[slot], tmpbs[slot]
                        pA, pB, pC, pD = pps[slot]
                        dd = tmpb[:, 11:14, :]
                        tmp = [tmpb[:, i, :] for i in range(14)]
                        if True:
                            tb = half * 2 + tbl
                            c0 = tb * 512
                            l0 = tbl * 512
                            (kk, kk2, sg, aicl, gst, rinv, logw, t1, cum, kkn, k2, Wt, Winv, lp) = tmp
                            st = P.stage()
                        for q in range(3):
                            rows = slice(q * 1024 + h * 64, q * 1024 + (h + 1) * 64)
                            if tb == 0:
                                st['sp'].append(lambda e, q=q, rows=rows: e.dma_start(out=raw[:, q, 1:513],
                                                                                       in_=pT[rows, 0:512]))
                            else:
                                st['sp'].append(lambda e, q=q, rows=rows, c0=c0: e.dma_start(
                                    out=raw[:, q, 0:513], in_=pT[rows, c0 - 1:c0 + 512]))
                        if tb == 0:
                            st['pool'].append(lambda e: e.memset(raw[:, 0:3, 0:1], 0.0))
                        st = P.stage()
                        st['dve'].append(lambda e: e.tensor_tensor(out=dd[:], in0=raw[:, :, 0:512],
                                                                    in1=raw[:, :, 1:513], op=ALU.subtract))
                        for q in range(3):
                            st['dve'].append(lambda e, q=q, col=col: e.scalar_tensor_tensor(
                                out=sh[:, q, :], in0=dd[:, q, :], scalar=col(q), in1=raw[:, q, 1:513],
                                op0=ALU.mult, op1=ALU.add))
                        rs, ks, vs = sh[:, 0, :], sh[:, 1, :], sh[:, 2, :]
                        A3 = lambda ap: ap.rearrange("p (c n) -> p c n", n=64)
                        ARv = ARs[:, hl, tbl * 1024:(tbl + 1) * 1024].rearrange("p (c two n) -> p c two n", two=2, n=64)
                        st = P.stage()
                        st['pe'].append(lambda e, cs=cs, c0=c0: e.matmul(pA[0:64, :], w2b[0:96, cs], lora[0:96, 0, c0:c0 + 512],
                                                                           start=True, stop=True))
                        st['pe'].append(lambda e, cs=cs, c0=c0: e.matmul(pB[0:64, :], a2b[0:96, cs], lora[0:96, 1, c0:c0 + 512],
                                                                           start=True, stop=True))
                        st['pe'].append(lambda e, cs=cs, c0=c0: e.matmul(pC[0:64, :], g2b[:, 0, cs], lora[:, 2, c0:c0 + 512],
                                                                           start=True, stop=False))
                        st['pe'].append(lambda e, cs=cs, c0=c0: e.matmul(pC[0:64, :], g2b[:, 1, cs], lora[:, 3, c0:c0 + 512],
                                                                           start=False, stop=True))
                        st['act'].append(lambda e, col=col, ks=ks: e.activation(out=kk2[:], in_=ks, func=AF.Square,
                                                                                 scale=col(5)))
                        st['act'].append(lambda e, col=col, ks=ks: e.activation(out=kk[:], in_=ks, func=AF.Copy,
                                                                                 scale=col(5)))
                        st = P.stage()
                        st['pe'].append(lambda e: e.matmul(pD[0:64, :], ones64[:], kk2[:], start=True, stop=True))
                        st['act'].append(lambda e, col=col: e.activation(out=sg[:], in_=pA[0:64, :], func=AF.Sigmoid,
                                                                          bias=col(3)))
                        st['act'].append(lambda e, col=col: e.activation(out=aicl[:], in_=pB[0:64, :], func=AF.Sigmoid,
                                                                          bias=col(4)))
                        st['dve'].append(lambda e: e.tensor_copy(gst[:], pC[0:64, :]))
                        st = P.stage()
                        st['act'].append(lambda e: e.activation(out=rinv[:], in_=pD[0:64, :], func=AF.Ln, bias=1e-24))
                        st['act'].append(lambda e: e.activation(out=rinv[:], in_=rinv[:], func=AF.Exp, scale=-0.5))
                        st['act'].append(lambda e: e.mul(logw[:], sg[:], -0.6065306597126334))
                        st['dve'].append(lambda e, col=col, h=h: e.tensor_scalar(t1[:], aicl[:], col(6), omk[:, h:h + 1],
                                                                                  ALU.mult, ALU.add))
                        st['dve'].append(lambda e, ks=ks: e.tensor_tensor(out=k2[:], in0=ks, in1=t1[:], op=ALU.mult))
                        st['dve'].append(lambda e, rs=rs: e.tensor_tensor(out=kk2[:], in0=rs, in1=k2[:], op=ALU.mult))
                        st['dve'].append(lambda e, col=col: e.tensor_scalar(t1[:], kk2[:], col(8), None, ALU.mult))
                        st['sp'].append(lambda e, cs=cs, c0=c0: e.dma_start(out=gT[cs, c0:c0 + 512], in_=gst[:]))
                        st = P.stage()
                        st['dve'].append(lambda e: e.tensor_tensor_scan(out=cum[:], data0=cmask[:], data1=logw[:],
                                                                         initial=0.0, op0=ALU.mult, op1=ALU.add))
                        st['dve'].append(lambda e: e.tensor_tensor(out=lp[:], in0=cum[:], in1=logw[:], op=ALU.subtract))
                        st['dve'].append(lambda e: e.tensor_tensor(out=kkn[:], in0=kk[:], in1=rinv[:], op=ALU.mult))
                        st['dve'].append(lambda e: e.tensor_tensor(out=kk[:], in0=kkn[:], in1=aicl[:], op=ALU.mult))
                        st['pe'].append(lambda e: e.matmul(pA[0:64, :], ones64[:], t1[:], start=True, stop=True))
                        st['act'].append(lambda e, hl=hl, l0=l0, vs=vs: e.copy(Vs[:, hl, l0:l0 + 512], vs))
                        st = P.stage()
                        st['act'].append(lambda e: e.activation(out=Wt[:], in_=cum[:], func=AF.Exp))
                        st['act'].append(lambda e: e.activation(out=Winv[:], in_=cum[:], func=AF.Exp, scale=-1.0))
                        st['act'].append(lambda e: e.activation(out=sg[:], in_=lp[:], func=AF.Exp))
                        st['dve'].append(lambda e, vs=vs: e.tensor_tensor(out=gst[:], in0=pA[0:64, :], in1=vs, op=ALU.mult))
                        st = P.stage()
                        st['dve'].append(lambda e, rs=rs, ARv=ARv: e.tensor_tensor(out=ARv[:, :, 1, :], in0=A3(rs), in1=A3(Wt),
                                                                                    op=ALU.mult))
                        st['dve'].append(lambda e, ARv=ARv: e.scalar_tensor_tensor(out=ARv[:, :, 0, :], in0=A3(kkn), scalar=-1.0,
                                                                                    in1=A3(sg), op0=ALU.mult, op1=ALU.mult))
                        st['dve'].append(lambda e, hl=hl, l0=l0: e.tensor_tensor(out=Ks[:, hl, l0:l0 + 512], in0=k2[:],
                                                                                  in1=Winv[:], op=ALU.mult))
                        st['dve'].append(lambda e, hl=hl, l0=l0: e.tensor_tensor(out=Bs[:, hl, l0:l0 + 512], in0=kk[:],
                                                                                  in1=Winv[:], op=ALU.mult))
                        st['dve'].append(lambda e, hl=hl, tbl=tbl: e.tensor_copy(
                            WC[:, hl, tbl * 8:(tbl + 1) * 8], A3(Wt)[:, :, 63]))
                        st['sp'].append(lambda e, cs=cs, c0=c0: e.dma_start(out=bvT[cs, c0:c0 + 512], in_=gst[:]))
                      fns.append(unit)
                    merge_units(P, fns)
                P.emit()
                et.close()
                if debug == 'prep':
                    return nc

                et = ExitStack()
                ybig = sbt(et, "ybig", [64, 8, HT], F32)
                ptr1 = pst(et, "ptr1", [128, 1024], BF16)
                ptr2 = pst(et, "ptr2", [128, 1024], BF16)
                pABs = [pst(et, f"pAB{i}", [128, 512], F32) for i in range(2)]
                pKRs = [pst(et, f"pKR{i}", [128, 512], F32) for i in range(2)]

                def hb2(ts, h):
                    return ts[h // 4][0:64, (h % 4) * 128:(h % 4 + 1) * 128]
                pT_ = pst(et, "pT_", [128, 512], F32)
                pF = pst(et, "pF", [128, 512], F32)
                tmpS = sbt(et, "tmpS", [64, 8, 64], F32)
                tokms = [sbt(et, f"tokm{i}", [64, 3, 8, 64], BF16) for i in range(2)]
                ABm = sbt(et, "ABm", [64, 8, 128], BF16)
                AKm = sbt(et, "AKm", [64, 8, 128], BF16)
                akm2 = sbt(et, "akm2", [64, 8, 64], BF16)
                Pp = [sbt(et, f"Pp{i}", [64, 8, 64], BF16) for i in range(2)]
                ZQ = [sbt(et, f"ZQ{i}", [64, 8, 128], BF16) for i in range(2)]
                Zq = [sbt(et, f"Zq{i}", [64, 8, 64], F32) for i in range(2)]
                UTb = sbt(et, "UTb", [64, 8, 64], BF16)
                m2v = masks2[:, :].rearrange("p (a b) -> p a b", a=8)

                def v16(t):
                    return t[0:64, :].rearrange("p (a b) -> p a b", a=8)
                P = Prog(nc, f"rws{gi}{half}")
                for c in range(HT // CH):
                    tc = slice(c * 64, (c + 1) * 64)
                    ac = slice(c * 128, c * 128 + 64)
                    rc = slice(c * 128 + 64, c * 128 + 128)
                    arc = slice(c * 128, (c + 1) * 128)
                    tokm = tokms[c % 2]

                    def emit_T(stg, cn):
                        tcn = slice(cn * 64, (cn + 1) * 64)
                        for h in range(8):
                            hc = slice(h * 64, (h + 1) * 64)
                            stg['pe'].append(lambda e, h=h, hc=hc, tcn=tcn: e.transpose(ptr1[0:64, hc], Bs[:, h, tcn], id64))
                            stg['pe'].append(lambda e, h=h, hc=hc, tcn=tcn: e.transpose(
                                ptr1[0:64, 512 + h * 64:512 + (h + 1) * 64], Ks[:, h, tcn], id64))
                            stg['pe'].append(lambda e, h=h, hc=hc, tcn=tcn: e.transpose(ptr2[0:64, hc], Vs[:, h, tcn], id64))

                    def emit_Tevac(stg, cn):
                        tk = tokms[cn % 2]
                        stg['act'].append(lambda e, tk=tk: e.copy(tk[:, 0:2, :, :].rearrange("p a b c -> p (a b c)"), ptr1[0:64, :]))
                        stg['act'].append(lambda e, tk=tk: e.copy(tk[:, 2, :, :].rearrange("p b c -> p (b c)"), ptr2[0:64, 0:512]))
                    if c == 0:
                        st = P.stage()
                        emit_T(st, 0)
                        st = P.stage()
                        emit_Tevac(st, 0)
                    st = P.stage()
                    for h in range(8):
                        hc = slice(h * 64, (h + 1) * 64)
                        hc2 = slice(h * 128, (h + 1) * 128)
                        Bt, Kt = Bs[:, h, tc], Ks[:, h, tc]
                        At, ARc = ARs[:, h, ac], ARs[:, h, arc]
                        st['pe'].append(lambda e, h=h, Bt=Bt, ARc=ARc: e.matmul(hb2(pABs, h), Bt, ARc, start=True, stop=True))
                        st['pe'].append(lambda e, h=h, Kt=Kt, ARc=ARc: e.matmul(hb2(pKRs, h), Kt, ARc, start=True, stop=True))
                        st['pe'].append(lambda e, hc=hc, At=At, Bt=Bt: e.matmul(pT_[0:64, hc], At, Bt, start=True, stop=True))
                    st = P.stage()
                    for hb in range(2):
                        hs4 = slice(hb * 4, hb * 4 + 4)
                        st['dve'].append(lambda e, hs4=hs4, hb=hb: e.tensor_tensor(
                            out=AKm[:, hs4, :], in0=pKRs[hb][0:64, :].rearrange("p (a b) -> p a b", a=4), in1=m2v[:, hs4, :], op=ALU.mult))
                        st['dve'].append(lambda e, hs4=hs4, hb=hb: e.tensor_tensor(
                            out=ABm[:, hs4, :], in0=pABs[hb][0:64, :].rearrange("p (a b) -> p a b", a=4), in1=m2v[:, hs4, :], op=ALU.mult))
                    st['dve'].append(lambda e: e.tensor_tensor(out=ZQ[0][:, :, 64:128], in0=v8(pT_), in1=M3(2), op=ALU.mult))
                    st = P.stage()
                    for h in range(8):
                        hc = slice(h * 64, (h + 1) * 64)
                        At = ARs[:, h, ac]
                        VTh = tokm[:, 2, h, :]
                        st['pe'].append(lambda e, At=At, h=h, hc=hc: e.matmul(
                            pF[0:64, hc], At, STb[:, h, :], start=True, stop=False))
                        st['pe'].append(lambda e, h=h, VTh=VTh, hc=hc: e.matmul(
                            pF[0:64, hc], AKm[:, h, 0:64], VTh, start=False, stop=True))
                    st = P.stage()
                    st['act'].append(lambda e: e.copy(Zq[0][:], v8(pF)))
                    st['act'].append(lambda e: e.copy(ZQ[0][:, :, 0:64], v8(pF)))
                    for j in range(1, 7):
                        st = P.stage()
                        if j == 1 and c + 1 < HT // CH:
                            emit_T(st, c + 1)
                        zi, zo = Zq[(j - 1) % 2], Zq[j % 2]
                        zqi, zqo = ZQ[(j - 1) % 2], ZQ[j % 2]
                        for h in range(8):
                            hc = slice(h * 64, (h + 1) * 64)
                            hc2 = slice(h * 128, (h + 1) * 128)
                            Pj = ABm[:, h, 0:64] if j == 1 else Pp[j % 2][:, h, :]
                            if j < 6:
                                st['pe'].append(lambda e, Pj=Pj, zqi=zqi, h=h: e.matmul(
                                    hb2(pKRs, h), Pj, zqi[:, h, :], start=True, stop=True))
                                st['pe'].append(lambda e, hc=hc, Pj=Pj, zqi=zqi, h=h: e.matmul(
                                    pT_[0:64, hc], zqi[:, h, 64:128], Pj, start=True, stop=True))
                            else:
                                st['pe'].append(lambda e, Pj=Pj, zqi=zqi, h=h: e.matmul(
                                    hb2(pKRs, h)[:, 0:64], Pj, zqi[:, h, 0:64], start=True, stop=True))
                        st = P.stage()
                        for hb in range(2):
                            hs4 = slice(hb * 4, hb * 4 + 4)
                            pk = pKRs[hb][0:64, :].rearrange("p (a b) -> p a b", a=4)
                            zp = pk[:, :, 0:64]
                            if j < 6:
                                st['dve'].append(lambda e, zi=zi, zo=zo, zp=zp, hs4=hs4: e.tensor_tensor(
                                    out=zo[:, hs4, :], in0=zp, in1=zi[:, hs4, :], op=ALU.add))
                                st['dve'].append(lambda e, zi=zi, zqo=zqo, zp=zp, hs4=hs4: e.tensor_tensor(
                                    out=zqo[:, hs4, 0:64], in0=zp, in1=zi[:, hs4, :], op=ALU.add))
                                st['dve'].append(lambda e, zqo=zqo, pk=pk, hs4=hs4: e.tensor_copy(zqo[:, hs4, 64:128], pk[:, :, 64:128]))
                            else:
                                st['dve'].append(lambda e, zi=zi, zp=zp, hs4=hs4: e.tensor_tensor(
                                    out=UTb[:, hs4, :], in0=zp, in1=zi[:, hs4, :], op=ALU.add))
                        if j < 6:
                            st['act'].append(lambda e, j=j: e.copy(Pp[(j + 1) % 2][:], v8(pT_)))
                        if j == 2 and c + 1 < HT // CH:
                            emit_Tevac(st, c + 1)
                    st = P.stage()
                    for h in range(8):
                        hc = slice(h * 64, (h + 1) * 64)
                        Rt = ARs[:, h, rc]
                        VTh = tokm[:, 2, h, :]
                        BTh = tokm[:, 0, h, :]
                        KTh = tokm[:, 1, h, :]
                        oy = pABs[0][0:64, hc]
                        os_ = pABs[1][0:64, hc]
                        st['pe'].append(lambda e, oy=oy, h=h, Rt=Rt: e.matmul(oy, STb[:, h, :], Rt,
                                                                               start=True, stop=False))
                        st['pe'].append(lambda e, oy=oy, h=h: e.matmul(oy, UTb[:, h, :], ABm[:, h, 64:128],
                                                                        start=False, stop=False))
                        st['pe'].append(lambda e, oy=oy, h=h, VTh=VTh: e.matmul(oy, VTh, AKm[:, h, 64:128],
                                                                                 start=False, stop=True))
                        st['pe'].append(lambda e, os_=os_, h=h, BTh=BTh: e.matmul(os_, BTh, UTb[:, h, :],
                                                                                   start=True, stop=False))
                        st['pe'].append(lambda e, os_=os_, KTh=KTh, VTh=VTh: e.matmul(os_, KTh, VTh,
                                                                                       start=False, stop=True))
                    st = P.stage()
                    st['act'].append(lambda e, tc=tc: e.copy(ybig[:, :, tc], v8(pABs[0])))
                    st['dve'].append(lambda e: e.tensor_tensor(out=tmpS[:], in0=v8(pABs[1]), in1=ST[:], op=ALU.add))
                    wcb = WC[:, :, c].unsqueeze(2).to_broadcast([64, 8, 64])
                    st['dve'].append(lambda e, wcb=wcb: e.tensor_tensor(out=ST[:], in0=tmpS[:], in1=wcb, op=ALU.mult))
                    st['dve'].append(lambda e, wcb=wcb: e.tensor_tensor(out=STb[:], in0=tmpS[:], in1=wcb, op=ALU.mult))
                if debug == 'scan' and _os.environ.get('SCAN_STOP'):
                    P.stages = P.stages[:int(_os.environ['SCAN_STOP'])]
                P.emit()
                if debug == 'scan':
                    et.close()
                    return nc

                et2 = ExitStack()
                tmps = [[sbt(et2, f"tmp{s}_{i}", [64, 512], F32) for i in range(5)] for s in range(2)]
                P = Prog(nc, f"rwo{gi}{half}")
                for hl2 in range(4):
                  for tbl in range(2):
                    fns = []
                    for slot in range(2):
                      def unit(P, slot=slot, hl2=hl2, tbl=tbl):
                        hl = hl2 * 2 + slot
                        h = gi * 8 + hl
                        pc = h * 12
                        cs = slice(h * 64, (h + 1) * 64)
                        tmp = tmps[slot]
                        obf = obfs[slot]
                        pA_, pB_ = (pABs[0], pABs[1]) if slot == 0 else (pKRs[0], pKRs[1])
                        if True:
                            tb = half * 2 + tbl
                            c0 = tb * 512
                            l0 = tbl * 512
                            gl_, bvl, yc, sq, rs_ = tmp[0:5]
                            t_a, t_b, t_c = sq, yc, sq
                            st = P.stage()
                        st['sp'].append(lambda e, cs=cs, c0=c0: e.dma_start(out=gl_[:], in_=gT[cs, c0:c0 + 512]))
                        st['sp'].append(lambda e, cs=cs, c0=c0: e.dma_start(out=bvl[:], in_=bvT[cs, c0:c0 + 512]))
                        st['pe'].append(lambda e, hl=hl, l0=l0: e.matmul(pA_[0:64, :], ones64[:], ybig[:, hl, l0:l0 + 512],
                                                                          start=True, stop=True))
                        st = P.stage()
                        st['dve'].append(lambda e, hl=hl, l0=l0: e.scalar_tensor_tensor(
                            out=yc[:], in0=pA_[0:64, :], scalar=-1.0 / 64, in1=ybig[:, hl, l0:l0 + 512],
                            op0=ALU.mult, op1=ALU.add))
                        st = P.stage()
                        st['act'].append(lambda e: e.activation(out=sq[:], in_=yc[:], func=AF.Square))
                        st = P.stage()
                        st['pe'].append(lambda e: e.matmul(pB_[0:64, :], ones64[:], sq[:], start=True, stop=True))
                        st = P.stage()
                        st['act'].append(lambda e: e.activation(out=rs_[:], in_=pB_[0:64, :], func=AF.Ln, bias=64.0 * GN_EPS))
                        st['act'].append(lambda e: e.activation(out=rs_[:], in_=rs_[:], func=AF.Exp, scale=-0.5))
                        st = P.stage()
                        st['dve'].append(lambda e: e.tensor_tensor(out=t_a[:], in0=yc[:], in1=rs_[:], op=ALU.mult))
                        st['dve'].append(lambda e, h=h, pc=pc: e.tensor_scalar(t_b[:], t_a[:], g8[:, h:h + 1],
                                                                                 ptab[:, pc + 10:pc + 11],
                                                                                 ALU.mult, ALU.add))
                        st['dve'].append(lambda e: e.tensor_tensor(out=t_c[:], in0=t_b[:], in1=bvl[:], op=ALU.add))
                        st['dve'].append(lambda e: e.tensor_tensor(out=obf[:], in0=t_c[:], in1=gl_[:], op=ALU.mult))
                        st = P.stage()
                        st['sp'].append(lambda e, cs=cs, c0=c0: e.dma_start(out=ybuf[cs, c0:c0 + 512], in_=obf[:]))
                      fns.append(unit)
                    merge_units(P, fns)
                P.emit()
                et2.close()
                et.close()
                if debug == 'post':
                    return nc

        if debug == 'rwkv':
            return nc
        with ExitStack() as es:
            ptab = sbt(es, "ptabL", [128, 72], F32)
            cch = sbt(es, "cch", [128, 8], F32)
            c2h = sbt(es, "c2h", [128, 8], F32)
            etmp = sbt(es, "etmp", [128, 8], F32)
            wrb = sbt(es, "wrb", [128, 8, 128], BF16)
            wib = sbt(es, "wib", [128, 8, 128], BF16)
            L_xpad = [sbt(es, f"xpad{i}", [128, 515], F32) for i in range(2)]
            L_gat = [sbt(es, f"gat{i}", [128, 512], F32) for i in range(2)]
            L_tl = [[sbt(es, f"tl{s_}_{i}", [128, 512], F32) for i in range(12)] for s_ in range(2)]
            L_xcb = [sbt(es, f"xcb{i}", [128, 512], BF16) for i in range(2)]
            L_ybf = [sbt(es, f"ybf{i}", [128, 512], BF16) for i in range(2)]
            L_hcar = [sbt(es, f"hcar{i}", [128, 1], F32) for i in range(2)]
            L_p1 = [pst(es, f"lp1_{i}", [128, 512], F32) for i in range(2)]
            L_p2 = [pst(es, f"lp2_{i}", [128, 512], F32) for i in range(2)]
            hcar = L_hcar[0]

            def merge_units_l(P, fns):
                subs = []
                for fn in fns:
                    Pu = Prog(nc, "sub")
                    fn(Pu)
                    subs.append(Pu.stages)
                n = len(subs[0])
                assert all(len(x) == n for x in subs)
                for k in range(n):
                    stg = P.stage()
                    for x in subs:
                        for e in ENGS:
                            stg[e].extend(x[k][e])
            P = Prog(nc, "lru")
            st = P.stage()
            st['sp'].append(lambda e: e.dma_start(out=ptab[:], in_=ptabu_d))
            st['poolq'].append(lambda e: e.dma_start(out=wrb[:], in_=wr_d))
            st['poolq'].append(lambda e: e.dma_start(out=wib[:], in_=wi_d))
            lamv = ptab[:, 0:72].rearrange("p (j k) -> p j k", k=9)[:, :, 7]
            st = P.stage()
            st['act'].append(lambda e: e.activation(out=etmp[:], in_=lamv, func=AF.Exp, scale=-1.0))
            st = P.stage()
            st['act'].append(lambda e: e.activation(out=etmp[:], in_=etmp[:], func=AF.Ln, bias=1.0))
            st = P.stage()
            st['dve'].append(lambda e: e.tensor_scalar(cch[:], etmp[:], -8.0, None, ALU.mult))
            st['dve'].append(lambda e: e.tensor_scalar(c2h[:], etmp[:], -16.0, None, ALU.mult))
            for jj in range(4):
              for tb in range(4):
                fns = []
                for slot in range(2):
                  def unit(P, slot=slot, jj=jj, tb=tb):
                    j = jj * 2 + slot
                    pc = j * 9
                    col = lambda k, pc=pc: ptab[:, pc + k:pc + k + 1]
                    xpad, gat, tl, xcb, ybf, hcar = L_xpad[slot], L_gat[slot], L_tl[slot], L_xcb[slot], L_ybf[slot], L_hcar[slot]
                    p1, p2 = L_p1[slot], L_p2[slot]
                    c0 = tb * 512
                    (xc0, xc1, x2, inner, inner2, sgm, gel, rg, ig, av, a2v, gx) = tl
                    rx = slice((28 + j) * 128, (29 + j) * 128)
                    rg_ = slice((36 + j) * 128, (37 + j) * 128)
                    st = P.stage()
                    if tb == 0:
                        st['sp'].append(lambda e, rx=rx: e.dma_start(out=xpad[:, 3:515], in_=pT[rx, 0:512]))
                    else:
                        st['sp'].append(lambda e, rx=rx, c0=c0: e.dma_start(out=xpad[:], in_=pT[rx, c0 - 3:c0 + 512]))
                    st['sp'].append(lambda e, rg_=rg_, c0=c0: e.dma_start(out=gat[:], in_=pT[rg_, c0:c0 + 512]))
                    xc = xc1
                    st = P.stage()
                    if tb == 0:
                        st['dve'].append(lambda e: e.memset(xpad[:, 0:3], 0.0))
                    st['dve'].append(lambda e, col=col: e.tensor_scalar(xc0[:], xpad[:, 0:512], col(0), col(4),
                                                                         ALU.mult, ALU.add))
                    st['dve'].append(lambda e, col=col: e.scalar_tensor_tensor(out=xc1[:], in0=xpad[:, 1:513], scalar=col(1),
                                                                                in1=xc0[:], op0=ALU.mult, op1=ALU.add))
                    st['dve'].append(lambda e, col=col: e.scalar_tensor_tensor(out=xc0[:], in0=xpad[:, 2:514], scalar=col(2),
                                                                                in1=xc1[:], op0=ALU.mult, op1=ALU.add))
                    st['dve'].append(lambda e, col=col: e.scalar_tensor_tensor(out=xc1[:], in0=xpad[:, 3:515], scalar=col(3),
                                                                                in1=xc0[:], op0=ALU.mult, op1=ALU.add))
                    st['act'].append(lambda e: e.activation(out=x2[:], in_=gat[:], func=AF.Square))
                    st = P.stage()
                    st['act'].append(lambda e: e.copy(xcb[:], xc[:]))
                    st['dve'].append(lambda e: e.tensor_scalar(inner[:], x2[:], 0.044715, 1.0, ALU.mult, ALU.add))
                    st['dve'].append(lambda e: e.tensor_tensor(out=inner2[:], in0=inner[:], in1=gat[:], op=ALU.mult))
                    st = P.stage()
                    st['pe'].append(lambda e, j=j: e.matmul(p1[:], wrb[:, j, :], xcb[:], start=True, stop=True))
                    st['pe'].append(lambda e, j=j: e.matmul(p2[:], wib[:, j, :], xcb[:], start=True, stop=True))
                    st['act'].append(lambda e: e.activation(out=sgm[:], in_=inner2[:], func=AF.Sigmoid,
                                                             scale=1.5957691216057308))
                    st = P.stage()
                    st['dve'].append(lambda e: e.tensor_copy(rg[:], p1[:]))
                    st['dve'].append(lambda e: e.tensor_copy(ig[:], p2[:]))
                    st['dve'].append(lambda e: e.tensor_tensor(out=gel[:], in0=gat[:], in1=sgm[:], op=ALU.mult))
                    st = P.stage()
                    st['act'].append(lambda e, col=col: e.activation(out=rg[:], in_=rg[:], func=AF.Sigmoid, bias=col(5)))
                    st['act'].append(lambda e, col=col: e.activation(out=ig[:], in_=ig[:], func=AF.Sigmoid, bias=col(6)))
                    st['act'].append(lambda e, j=j: e.activation(out=av[:], in_=rg[:], func=AF.Exp, scale=cch[:, j:j + 1]))
                    st['act'].append(lambda e, j=j: e.activation(out=a2v[:], in_=rg[:], func=AF.Exp, scale=c2h[:, j:j + 1]))
                    st = P.stage()
                    st['dve'].append(lambda e: e.tensor_tensor(out=gx[:], in0=ig[:], in1=xc[:], op=ALU.mult))
                    st['dve'].append(lambda e: e.tensor_scalar(x2[:], a2v[:], 1.0, -1.0, ALU.min, ALU.mult))
                    st = P.stage()
                    if tb == 0:
                        st['act'].append(lambda e: e.activation(out=inner[:, 1:512], in_=x2[:, 1:512], func=AF.Sqrt, bias=1.0))
                        st['dve'].append(lambda e: e.memset(inner[:, 0:1], 1.0))
                    else:
                        st['act'].append(lambda e: e.activation(out=inner[:], in_=x2[:], func=AF.Sqrt, bias=1.0))
                    st = P.stage()
                    st['dve'].append(lambda e: e.tensor_tensor(out=inner2[:], in0=inner[:], in1=gx[:], op=ALU.mult))
                    if tb == 0:
                        st['dve'].append(lambda e: e.tensor_tensor_scan(out=sgm[:], data0=av[:], data1=inner2[:],
                                                                         initial=0.0, op0=ALU.mult, op1=ALU.add))
                    else:
                        st['dve'].append(lambda e: e.tensor_tensor_scan(out=sgm[:], data0=av[:], data1=inner2[:],
                                                                         initial=hcar[:, 0:1], op0=ALU.mult, op1=ALU.add))
                    st['dve'].append(lambda e: e.tensor_tensor(out=ybf[:], in0=sgm[:], in1=gel[:], op=ALU.mult))
                    st = P.stage()
                    st['act'].append(lambda e: e.copy(hcar[:], sgm[:, 511:512]))
                    rows = slice(RW + j * 128, RW + (j + 1) * 128)
                    st['sp'].append(lambda e, rows=rows, c0=c0: e.dma_start(out=ybuf[rows, c0:c0 + 512], in_=ybf[:]))
                  fns.append(unit)
                merge_units_l(P, fns)
            st = P.stage()
            if early:
                st['pool'].append(lambda e: e.memset(hcar[:], 0.0))
            else:
              for k in range(4):
                if k > 0:
                    st = P.stage()
                st['pool'].append(lambda e, k=k: e.collective_compute(
                    "AllGather", ALU.bypass, replica_groups=[[2 * i, 2 * i + 1] for i in range(ncores // 2)],
                    ins=[ybuf[k * 512:(k + 1) * 512, :].opt()], outs=[gbufs[k].opt()]))
            if debug is not None and _os.environ.get('LRU_STOP'):
                P.stages = P.stages[:int(_os.environ['LRU_STOP'])]
            P.emit()

        if early or debug == 'gather':
            return nc
        with ExitStack() as es:
            yT = sbt(es, "yT", [128, KC, TO], BF16)
            sel = sbt(es, "sel", [128, 2], F32)
            ngt = sbt(es, "ngt", [128, 16], F32)
            ones32 = sbt(es, "ones32", [128, 128], F32)
            rbc = sbt(es, "rbc", [128, TO], F32)
            with ExitStack() as es1:
                gl = [sbt(es1, f"gl{i}", [128, 2, TO], BF16) for i in range(3)]
                tb_ = [sbt(es1, f"tbl{i}", [128, TO], BF16) for i in range(2)]
                sqf = [sbt(es1, f"sqf{i}", [128, TO], F32) for i in range(2)]
                pq = [pst(es1, f"f1p{i}", [128, 512], F32) for i in range(2)]
                P = Prog(nc, "blend")
                st = P.stage()
                st['sp'].append(lambda e: e.dma_start(out=sel[:], in_=sel_d))
                st['sp'].append(lambda e: e.dma_start(out=ngt[:], in_=ngtab_d))
                st['pool'].append(lambda e: e.memset(ones32[:], 1.0))
                def gsrc(cc):
                    r, q = cc // 16, cc % 16
                    return gbufs[q // 4][r * 512 + (q % 4) * 128:r * 512 + (q % 4 + 1) * 128, :]
                st['sp'].append(lambda e: e.dma_start(out=gl[0][:].rearrange("p a b -> p (a b)"), in_=gsrc(0)))
                for cc in range(KC + 2):
                    st = P.stage()
                    if cc + 1 < KC:
                        st['sp'].append(lambda e, cc=cc: e.dma_start(
                            out=gl[(cc + 1) % 3][:].rearrange("p a b -> p (a b)"),
                            in_=gsrc(cc + 1)))
                    if cc < KC:
                        st['dve'].append(lambda e, cc=cc: e.tensor_scalar(tb_[cc % 2][:], gl[cc % 3][:, 0, :],
                                                                           sel[:, 0:1], None, ALU.mult))
                    if 1 <= cc <= KC:
                        c1 = cc - 1
                        st['dve'].append(lambda e, c1=c1: e.scalar_tensor_tensor(
                            out=yT[:, c1, :], in0=gl[c1 % 3][:, 1, :], scalar=sel[:, 1:2], in1=tb_[c1 % 2][:],
                            op0=ALU.mult, op1=ALU.add))
                lch = list(range(8, 16)) + list(range(24, 32))
                for q in range(17):
                    st = P.stage()
                    if q < 16:
                        st['act'].append(lambda e, q=q: e.activation(out=sqf[q % 2][:], in_=yT[:, lch[q], :],
                                                                      func=AF.Square))
                    if q >= 1:
                        for hf in range(2):
                            st['pe'].append(lambda e, q=q, hf=hf: e.matmul(
                                pq[hf][:], ones32[:], sqf[(q - 1) % 2][:, hf * 512:(hf + 1) * 512],
                                start=(q == 1), stop=(q == 16)))
                st = P.stage()
                for hf in range(2):
                    st['act'].append(lambda e, hf=hf: e.activation(out=rbc[:, hf * 512:(hf + 1) * 512], in_=pq[hf][:],
                                                                    func=AF.Sqrt, scale=1.0 / 2048, bias=EPS))
                st = P.stage()
                st['dve'].append(lambda e: e.reciprocal(rbc[:], rbc[:]))
                st = P.stage()
                for q in range(16):
                    eng = 'dve'
                    st[eng].append(lambda e, q=q: e.scalar_tensor_tensor(
                        out=yT[:, lch[q], :], in0=yT[:, lch[q], :], scalar=ngt[:, q:q + 1], in1=rbc[:],
                        op0=ALU.mult, op1=ALU.mult))
                P.emit()
            with ExitStack() as es2:
                wo = [sbt(es2, f"wo{i}", [128, KC, 512], BF16) for i in range(2)]
                xot = [sbt(es2, f"xot{i}", [128, 2, 512], F32) for i in range(2)]
                hst = [sbt(es2, f"hst{i}", [128, 2, 512], F32) for i in range(2)]
                po = [pst(es2, f"po{i}", [128, 512], F32) for i in range(4)]
                xov = xo.rearrange("(a p) n -> p a n", p=128)
                h1v6 = h1buf.rearrange("(a p) n -> p a n", p=128)
                P = Prog(nc, "outproj")
                st = P.stage()
                st['poolq'].append(lambda e: e.dma_start(out=wo[0][:].rearrange("p a b -> p (a b)"), in_=wout[0]))
                NU = 32
                for u in range(NU + 2):
                    st = P.stage()
                    if u < NU:
                        db, tp = u // 4, u % 4
                        if db + 1 < 8:
                            st['poolq'].append(lambda e, db=db, tp=tp: e.dma_start(
                                out=wo[(db + 1) % 2][:, tp * 8:(tp + 1) * 8, :].rearrange("p a b -> p (a b)"),
                                in_=wout[db + 1][:, tp * 4096:(tp + 1) * 4096]))
                        for a in range(2):
                            tcx = tp * 2 + a
                            for cc in range(KC):
                                st['pe'].append(lambda e, u=u, db=db, tcx=tcx, cc=cc, a=a: e.matmul(
                                    po[2 * (u % 2) + a][:], yT[:, cc, tcx * 128:(tcx + 1) * 128], wo[db % 2][:, cc, :],
                                    start=(cc == 0), stop=(cc == KC - 1)))
                        st['sp'].append(lambda e, u=u, db=db, tp=tp: e.dma_start(
                            out=xot[u % 2][:], in_=xov[:, tp * 2:tp * 2 + 2, db * 512:(db + 1) * 512]))
                    if 1 <= u <= NU:
                        u1 = u - 1
                        for a in range(2):
                            st['dve'].append(lambda e, u1=u1, a=a: e.tensor_tensor(
                                out=hst[u1 % 2][:, a, :], in0=po[2 * (u1 % 2) + a][:], in1=xot[u1 % 2][:, a, :], op=ALU.add))
                    if u >= 2:
                        u2 = u - 2
                        db2, tp2 = u2 // 4, u2 % 4
                        st['sp'].append(lambda e, u2=u2, db2=db2, tp2=tp2: e.dma_start(
                            out=h1v6[:, tp2 * 2:tp2 * 2 + 2, db2 * 512:(db2 + 1) * 512], in_=hst[u2 % 2][:]))
                P.emit()
        if debug == 'p6':
            return nc

        with ExitStack() as es:
            u2T = sbt(es, "u2T", [128, KC, TO], BF16)
            with ExitStack() as es1:
                norm_transpose("n2", es1, h1buf, g2bc, TO, u2T, ident)
            with ExitStack() as es2:
                wgb = [sbt(es2, f"wgb{i}", [128, KC, 128], BF16) for i in range(2)]
                wub = [sbt(es2, f"wub{i}", [128, KC, 128], BF16) for i in range(2)]
                sgs = [sbt(es2, f"sgs{i}", [128, TO], F32) for i in range(2)]
                ups = [sbt(es2, f"ups{i}", [128, TO], F32) for i in range(2)]
                hs = [sbt(es2, f"hs{i}", [128, TO], BF16) for i in range(2)]
                pg = [[pst(es2, f"pg{s}_{q}", [128, 512], F32) for q in range(4)] for s in range(2)]
                P = Prog(nc, "gateup")
                st = P.stage()
                st['poolq'].append(lambda e: e.dma_start(out=wgb[0][:].rearrange("p a b -> p (a b)"), in_=wg[0]))
                st['poolq'].append(lambda e: e.dma_start(out=wub[0][:].rearrange("p a b -> p (a b)"), in_=wu[0]))
                for f in range(NF + 3):
                    st = P.stage()
                    if f + 1 < NF:
                        st['poolq'].append(lambda e, f=f: e.dma_start(
                            out=wgb[(f + 1) % 2][:].rearrange("p a b -> p (a b)"), in_=wg[f + 1]))
                        st['poolq'].append(lambda e, f=f: e.dma_start(
                            out=wub[(f + 1) % 2][:].rearrange("p a b -> p (a b)"), in_=wu[f + 1]))
                    if f < NF:
                        for kc in range(KC):
                            for q in range(4):
                                wsrc = wgb if q < 2 else wub
                                st['pe'].append(lambda e, f=f, kc=kc, q=q, wsrc=wsrc: e.matmul(
                                    pg[f % 2][q][:], wsrc[f % 2][:, kc, :],
                                    u2T[:, kc, (q % 2) * 512:(q % 2 + 1) * 512],
                                    start=(kc == 0), stop=(kc == KC - 1)))
                    if 1 <= f <= NF:
                        f1 = f - 1
                        for hf in range(2):
                            st['act'].append(lambda e, f1=f1, hf=hf: e.activation(
                                out=sgs[f1 % 2][:, hf * 512:(hf + 1) * 512], in_=pg[f1 % 2][hf][:], func=AF.Silu))
                            st['dve'].append(lambda e, f1=f1, hf=hf: e.tensor_copy(
                                ups[f1 % 2][:, hf * 512:(hf + 1) * 512], pg[f1 % 2][2 + hf][:]))
                    if 2 <= f <= NF + 1:
                        f2 = f - 2
                        st['pool'].append(lambda e, f2=f2: e.tensor_tensor(out=hs[f2 % 2][:], in0=sgs[f2 % 2][:],
                                                                            in1=ups[f2 % 2][:], op=ALU.mult))
                    if 3 <= f:
                        f3 = f - 3
                        st['sp'].append(lambda e, f3=f3: e.dma_start(out=hT[f3 * 128:(f3 + 1) * 128, :],
                                                                      in_=hs[f3 % 2][:]))
                P.emit()
        if debug == 'p8':
            return nc

        with ExitStack() as es:
            hTs = sbt(es, "hTs", [128, NF, 512], BF16)
            GS = [(0, 22), (22, 44), (44, 65), (65, 86)]
            wdb = [sbt(es, f"wdb{i}", [128, 22, 512], BF16) for i in range(2)]
            h1p = [sbt(es, f"h1p{i}", [128, 4, 512], F32) for i in range(2)]
            h2s = [sbt(es, f"h2s{i}", [128, 4, 512], F32) for i in range(2)]
            pd = [[pst(es, f"pd{s}_{q}", [128, 512], F32) for q in range(4)] for s in range(2)]
            hTv = hT.rearrange("(f p) t -> p f t", p=128)
            h1v = h1buf.rearrange("(a p) n -> p a n", p=128)
            h2v = h2buf.rearrange("(a p) n -> p a n", p=128)
            P = Prog(nc, "down")
            units = [(th, db, g) for th in range(2) for db in range(8) for g in range(4)]

            def wload(st, ui):
                th, db, g = units[ui]
                f0, f1 = GS[g]
                st['poolq'].append(lambda e, ui=ui, db=db, f0=f0, f1=f1: e.dma_start(
                    out=wdb[ui % 2][:, 0:f1 - f0, :], in_=wdn[db, :, f0:f1, :]))
            st = P.stage()
            wload(st, 0)
            for ui in range(len(units) + 2):
                st = P.stage()
                if ui < len(units):
                    th, db, g = units[ui]
                    f0, f1 = GS[g]
                    if db == 0 and g == 0:
                        pass
                    if ui + 1 < len(units):
                        wload(st, ui + 1)
                    if g == 0:
                        st['sp'].append(lambda e, th=th, db=db: e.dma_start(
                            out=h1p[db % 2][:], in_=h1v[:, th * 4:(th + 1) * 4, db * 512:(db + 1) * 512]))
                    for ff in range(f0, f1):
                        for tq in range(4):
                            st['pe'].append(lambda e, ui=ui, db=db, ff=ff, f0=f0, tq=tq: e.matmul(
                                pd[db % 2][tq][:], hTs[:, ff, tq * 128:(tq + 1) * 128], wdb[ui % 2][:, ff - f0, :],
                                start=(ff == 0), stop=(ff == NF - 1)))
                if ui >= 1 and ui - 1 < len(units) and units[ui - 1][2] == 3:
                    th1, db1, _ = units[ui - 1]
                    for tq in range(4):
                        st['dve'].append(lambda e, db1=db1, tq=tq: e.tensor_tensor(
                            out=h2s[db1 % 2][:, tq, :], in0=pd[db1 % 2][tq][:], in1=h1p[db1 % 2][:, tq, :], op=ALU.add))
                if ui >= 2 and ui - 2 < len(units) and units[ui - 2][2] == 3:
                    th2, db2, _ = units[ui - 2]
                    st['sp'].append(lambda e, th2=th2, db2=db2: e.dma_start(
                        out=h2v[:, th2 * 4:(th2 + 1) * 4, db2 * 512:(db2 + 1) * 512], in_=h2s[db2 % 2][:]))
                if ui + 1 < len(units) and units[ui + 1][1] == 0 and units[ui + 1][2] == 0 and ui + 1 > 0:
                    pass
            stages = P.stages
            def hload_stage(th):
                stl = {e: [] for e in ENGS}
                for k in range(0, NF, 11):
                    k1 = min(NF, k + 11)
                    stl['sp'].append(lambda e, k=k, k1=k1, th=th: e.dma_start(
                        out=hTs[:, k:k1, :], in_=hTv[:, k:k1, th * 512:(th + 1) * 512]))
                return stl
            new = [stages[0], hload_stage(0)]
            for si in range(1, len(stages)):
                ui = si - 1
                if ui == 32:
                    new.append(hload_stage(1))
                new.append(stages[si])
            P.stages = new
            P.emit()
        if debug == 'p9':
            return nc

        with ExitStack() as es:
            gbc = sbt(es, "g3", [128, D], F32)
            xt = [sbt(es, f"fxt{i}", [128, D], F32) for i in range(2)]
            ot = [sbt(es, f"fot{i}", [128, D], F32) for i in range(2)]
            junk = sbt(es, "fjunk", [128, D], BF16)
            ss = sbt(es, "fss", [128, 8], F32)
            rstd = sbt(es, "frstd", [128, 8], F32)
            P = Prog(nc, "fin")
            st = P.stage()
            st['sp'].append(lambda e: e.dma_start(out=gbc[:], in_=g3bc))
            st['sp'].append(lambda e: e.dma_start(out=xt[0][:], in_=h2buf[0:128, :]))
            st['dve'].append(lambda e: e.memset(ss[:], 0.0))
            NTL = TO // 128
            for i in range(NTL):
                b = i % 2
                st = P.stage()
                st['act'].append(lambda e, b=b, i=i: e.activation(out=junk[:], in_=xt[b][:], func=AF.Square,
                                                                   accum_out=ss[:, i:i + 1]))
                if i + 1 < NTL:
                    st['sp'].append(lambda e, i=i: e.dma_start(out=xt[(i + 1) % 2][:],
                                                               in_=h2buf[(i + 1) * 128:(i + 2) * 128, :]))
                st = P.stage()
                st['act'].append(lambda e, i=i: e.activation(out=rstd[:, i:i + 1], in_=ss[:, i:i + 1], func=AF.Sqrt,
                                                              scale=1.0 / D, bias=EPS))
                st = P.stage()
                st['dve'].append(lambda e, i=i: e.reciprocal(rstd[:, i:i + 1], rstd[:, i:i + 1]))
                st = P.stage()
                st['dve'].append(lambda e, b=b, i=i: e.scalar_tensor_tensor(out=ot[b][:], in0=xt[b][:],
                                                                             scalar=rstd[:, i:i + 1], in1=gbc[:],
                                                                             op0=ALU.mult, op1=ALU.mult))
                st = P.stage()
                st['sp'].append(lambda e, b=b, i=i: e.dma_start(out=out[i * 128:(i + 1) * 128, :], in_=ot[b][:]))
            P.emit()
    return nc


def _tile_cols(w, cols_list):
    nch = len(cols_list)
    outw = np.zeros((nch, 128, KC, 128), np.float32)
    for j, cols in enumerate(cols_list):
        blk = w[:, cols]
        outw[j, :, :, :blk.shape[1]] = blk.reshape(KC, 128, -1).transpose(1, 0, 2)
    return outw.reshape(nch, 128, KC * 128)


def _prep(inp):
    f = lambda k: np.asarray(inp[k], dtype=np.float32)
    x = f("x")
    w_in = f("w_in")[0]
    mu = f("mu_shift")[0]
    R = 2048
    o3 = 3 * R
    g1bc = np.ascontiguousarray(np.broadcast_to(f("norm_mix_g")[0][None, :], (128, D)))
    g2bc = np.ascontiguousarray(np.broadcast_to(f("norm_ffn_g")[0][None, :], (128, D)))
    g3bc = np.ascontiguousarray(np.broadcast_to(f("norm_final_g")[None, :], (128, D)))
    w_out = f("w_out")[0]
    perm = np.concatenate([np.arange(0, 1024), np.arange(2048, 3072), np.arange(1024, 2048), np.arange(3072, 4096)])
    wop = w_out[perm]
    wout_t = np.ascontiguousarray(wop.reshape(KC, 128, 8, 512).transpose(2, 1, 0, 3)).reshape(8, 128, KC * 512)
    wgate = f("ffn_w_gate")[0]
    wup = f("ffn_w_up")[0]
    wdown = f("ffn_w_down")[0]
    wg_t = np.ascontiguousarray(wgate.reshape(KC, 128, NF, 128).transpose(2, 1, 0, 3)).reshape(NF, 128, KC * 128)
    wu_t = np.ascontiguousarray(wup.reshape(KC, 128, NF, 128).transpose(2, 1, 0, 3)).reshape(NF, 128, KC * 128)
    wdn_t = np.ascontiguousarray(wdown.reshape(NF, 128, 8, 512).transpose(2, 1, 0, 3))
    ii = np.arange(64)
    su = (ii[:, None] < ii[None, :]).astype(np.float32)
    iu = (ii[:, None] <= ii[None, :]).astype(np.float32)
    sl = (ii[:, None] > ii[None, :]).astype(np.float32)
    masks = np.stack([np.tile(m, (1, 8)) for m in (su, iu, sl)], axis=1).astype(np.float32)
    masks2 = np.ascontiguousarray(np.tile(np.concatenate([su, iu], axis=1), (1, 8)).astype(np.float32))
    bones = np.kron(np.eye(2, dtype=np.float32), np.ones((64, 64), np.float32))
    cmask = np.ones((128, 512), np.float32)
    cmask[:, ::64] = 0.0
    ngtab = np.ascontiguousarray(f("lru_norm_g")[0].reshape(16, 128).T)
    per_half = {}
    for hh in range(2):
        rsl = np.arange(hh * 1024, (hh + 1) * 1024)
        cols_list = []
        for base in (0, R, 2 * R):
            for j in range(8):
                cols_list.append(base + rsl[j * 128:(j + 1) * 128])
        cols_list.append(np.arange(o3, o3 + 96))
        cols_list.append(np.arange(o3 + 96, o3 + 192))
        cols_list.append(np.arange(o3 + 192, o3 + 320))
        cols_list.append(np.arange(o3 + 320, o3 + 448))
        lb = o3 + 448
        for base in (lb, lb + R):
            for j in range(8):
                cols_list.append(base + rsl[j * 128:(j + 1) * 128])
        win_t = _tile_cols(w_in, cols_list)
        ptabh = np.zeros((64, 192), np.float32)
        ptabl = np.zeros((128, 4), np.float32)
        ptabu = np.zeros((128, 72), np.float32)
        pv = {k: f(k)[0] for k in ("rwkv_w0", "rwkv_a0", "rwkv_k_k", "rwkv_k_a", "rwkv_ln_g", "rwkv_ln_b",
                                   "conv_b", "lru_br", "lru_bi", "lru_lambda")}
        rk = f("rwkv_r_k")[0].reshape(-1)
        cw = f("conv_w")[0]
        for h in range(16):
            ch = rsl[h * 64:(h + 1) * 64]
            b = h * 12
            ptabh[:, b + 0] = mu[ch]
            ptabh[:, b + 1] = mu[R + ch]
            ptabh[:, b + 2] = mu[2 * R + ch]
            ptabh[:, b + 3] = pv["rwkv_w0"][ch]
            ptabh[:, b + 4] = pv["rwkv_a0"][ch]
            ptabh[:, b + 5] = pv["rwkv_k_k"][ch]
            ptabh[:, b + 6] = pv["rwkv_k_a"][ch]
            ptabh[:, b + 8] = rk[ch]
            ptabh[:, b + 9] = pv["rwkv_ln_g"][ch]
            ptabh[:, b + 10] = pv["rwkv_ln_b"][ch]
        for j in range(8):
            ch = rsl[j * 128:(j + 1) * 128]
            b = j * 9
            for k in range(4):
                ptabu[:, b + k] = cw[k, ch]
            ptabu[:, b + 4] = pv["conv_b"][ch]
            ptabu[:, b + 5] = pv["lru_br"][ch]
            ptabu[:, b + 6] = pv["lru_bi"][ch]
            ptabu[:, b + 7] = pv["lru_lambda"][ch]
        ptabl[:96, 0] = mu[o3:o3 + 96]
        ptabl[:96, 1] = mu[o3 + 96:o3 + 192]
        ptabl[:, 2] = mu[o3 + 192:o3 + 320]
        ptabl[:, 3] = mu[o3 + 320:o3 + 448]
        per_half[hh] = dict(
            win=win_t, ptabh=ptabh, ptabl=ptabl, ptabu=ptabu,
            w2=np.ascontiguousarray(f("rwkv_w2")[0][:, rsl]),
            a2=np.ascontiguousarray(f("rwkv_a2")[0][:, rsl]),
            g2w=np.ascontiguousarray(f("rwkv_g2")[0][:, rsl].reshape(2, 128, 1024).transpose(1, 0, 2)),
            wr=np.ascontiguousarray(f("lru_wr")[0][hh * 8:(hh + 1) * 8].transpose(1, 0, 2)),
            wi=np.ascontiguousarray(f("lru_wi")[0][hh * 8:(hh + 1) * 8].transpose(1, 0, 2)),
        )
    in_maps = []
    for c in range(8):
        b, hh = c // 2, c % 2
        sel = np.zeros((128, 2), np.float32)
        sel[:, hh] = 1.0
        m = dict(xb=np.ascontiguousarray(x[b]), xo=np.ascontiguousarray(x[b, hh * TO:(hh + 1) * TO]),
                 g1bc=g1bc, g2bc=g2bc, g3bc=g3bc, ngtab=ngtab, masks=masks, masks2=masks2, bones=bones, cmask=cmask,
                 wout=wout_t, wg=wg_t, wu=wu_t, wdn=wdn_t, sel=sel)
        m.update(per_half[hh])
        in_maps.append(m)
    return in_maps


def kernel(**inp):
    in_maps = _prep(inp)
    nc = build_nc()
    res = run_bass_kernel_spmd(nc, in_maps, core_ids=list(range(8)))
    outp = np.zeros((4, T, D), np.float32)
    for c in range(8):
        b, hh = c // 2, c % 2
        outp[b, hh * TO:(hh + 1) * TO] = res.results[c]["out"]
    return outp
```

```python
import os as _os
import numpy as np
from contextlib import ExitStack
import concourse.bass as bass
import concourse.mybir as mybir
from concourse.bass_utils import run_bass_kernel_spmd

F32, BF16 = mybir.dt.float32, mybir.dt.bfloat16
AF = mybir.ActivationFunctionType
ALU = mybir.AluOpType

T = 2048
TO = 1024
D = 4096
KC = 32
NCH = 44
DFF = 11008
NF = 86
RW = 1024
CH = 64
NCK = T // CH
NPT = 176
EPS = 1e-6
GN_EPS = 64e-5

ENGS = ['pe', 'dve', 'act', 'pool', 'sp', 'poolq', 'actq']
HOST = {'pe': 'tensor', 'dve': 'vector', 'act': 'scalar', 'pool': 'gpsimd', 'sp': 'sync',
        'poolq': 'gpsimd', 'actq': 'scalar'}


class Prog:
    cnt = 0

    def __init__(self, nc, name):
        self.nc = nc
        self.name = name
        self.stages = []

    def stage(self):
        st = {e: [] for e in ENGS}
        self.stages.append(st)
        return st

    def emit(self):
        nc = self.nc
        with ExitStack() as es:
            Prog.cnt += 1
            sems = {e: es.enter_context(nc.semaphore(f"{self.name}{Prog.cnt}_{e}")) for e in ENGS}
            inc = {e: (16 if e in ('sp', 'poolq', 'actq') else 1) for e in ENGS}
            cum = {e: [] for e in ENGS}
            run = {e: 0 for e in ENGS}
            for st in self.stages:
                for e in ENGS:
                    if st[e]:
                        run[e] += inc[e] * (len(st[e]) if inc[e] == 16 else 1)
                    cum[e].append(run[e])
            stages = self.stages
            with nc.Block() as blk0:
                def clr(eng):
                    for e in ENGS:
                        eng.sem_clear(sems[e])
                blk0.gpsimd(clr)
            blk = es.enter_context(nc.Block())

            def make(hosteng):
                mine = [e for e in ENGS if HOST[e] == hosteng]

                def f(eng):
                    waited = {e: 0 for e in ENGS}
                    for k, st in enumerate(stages):
                        if not any(st[e] for e in mine):
                            continue
                        if k > 0:
                            for x in ENGS:
                                need = cum[x][k - 1]
                                if need > waited[x]:
                                    eng.wait_ge(sems[x], need)
                                    waited[x] = need
                        for e in mine:
                            ops = st[e]
                            for i, op in enumerate(ops):
                                ins = op(eng)
                                if inc[e] == 16:
                                    ins.then_inc(sems[e], 16)
                                elif i == len(ops) - 1:
                                    ins.then_inc(sems[e], 1)
                    for x in ENGS:
                        if run[x] > waited[x]:
                            eng.wait_ge(sems[x], run[x])
                return f
            blk.tensor(make('tensor'))
            blk.vector(make('vector'))
            blk.scalar(make('scalar'))
            blk.gpsimd(make('gpsimd'))
            blk.sync(make('sync'))


def build_nc(debug=None, ncores=8):
    early = debug in ('p1', 'p2', 'lora', 'prep', 'scan', 'post', 'rwkv', 'mix')
    nc = bass.Bass("TRN2", target_bir_lowering=False)

    MIXKEEP = ("xb", "g1bc", "win", "ptabh", "ptabl", "ptabu", "masks2", "w2", "a2", "g2w", "wr", "wi", "masks", "bones", "cmask")

    def din(name, shape, dt=F32):
        if early and name not in MIXKEEP:
            shape = [1, 1]
        return nc.dram_tensor(name, shape, dt, kind="ExternalInput").ap()

    xb = din("xb", [T, D])
    xo = din("xo", [TO, D])
    win = din("win", [NCH, 128, KC * 128])
    g1bc = din("g1bc", [128, D])
    g2bc = din("g2bc", [128, D])
    g3bc = din("g3bc", [128, D])
    ptabh_d = din("ptabh", [64, 192])
    ptabl_d = din("ptabl", [128, 4])
    ptabu_d = din("ptabu", [128, 72])
    ngtab_d = din("ngtab", [128, 16])
    w2_d = din("w2", [96, RW])
    a2_d = din("a2", [96, RW])
    g2w_d = din("g2w", [128, 2, RW])
    wr_d = din("wr", [128, 8, 128])
    wi_d = din("wi", [128, 8, 128])
    masks_d = din("masks", [64, 3, 512])
    masks2_d = din("masks2", [64, 1024])
    bones_d = din("bones", [128, 128])
    cmask_d = din("cmask", [128, 512])
    wout = din("wout", [8, 128, KC * 512])
    wg = din("wg", [NF, 128, KC * 128])
    wu = din("wu", [NF, 128, KC * 128])
    wdn = din("wdn", [8, 128, NF, 512])
    sel_d = din("sel", [128, 2])
    out = nc.dram_tensor("out", [TO, D], F32, kind="ExternalOutput").ap()

    def dscr(name, shape, dt):
        return nc.dram_tensor(name, shape, dt).ap()

    if not early:
        pT = dscr("pT", [NCH * 128, T], F32)
    gT = dscr("gT", [RW, T], F32)
    bvT = dscr("bvT", [RW, T], F32)
    if early:
        ybuf = nc.dram_tensor("ybuf", [2048, T], BF16, kind="ExternalOutput").ap()
        pT = nc.dram_tensor("pT", [NCH * 128, T], F32, kind="ExternalOutput").ap()
    else:
        ybuf = dscr("ybuf", [2048, T], BF16)
    gbufs = [dscr(f"gbuf{k}", [1024, T], BF16) for k in range(4)]
    h1buf = dscr("h1buf", [TO, D], F32)
    hT = dscr("hT", [DFF, TO], BF16)
    h2buf = dscr("h2buf", [TO, D], F32)

    uid = [0]

    def sbt(es, name, shape, dt):
        uid[0] += 1
        return es.enter_context(nc.sbuf_tensor(f"s{uid[0]}_{name}", shape, dt))

    def pst(es, name, shape, dt):
        uid[0] += 1
        return es.enter_context(nc.psum_tensor(f"p{uid[0]}_{name}", shape, dt))

    def norm_transpose(name, es, src, gbc_d, ntok, dstT, ident):
        ntile = ntok // 128
        gbc = sbt(es, name + "gbc", [128, D], F32)
        xt = [sbt(es, f"{name}xt{i}", [128, D], F32) for i in range(2)]
        xn = [sbt(es, f"{name}xn{i}", [128, D], BF16) for i in range(2)]
        junk = sbt(es, name + "junk", [128, D], BF16)
        ss = sbt(es, name + "ss", [128, 2], F32)
        rstd = sbt(es, name + "rstd", [128, 2], F32)
        ptr = [pst(es, f"{name}ptr{i}", [128, 1024], BF16) for i in range(4)]
        P = Prog(nc, name)
        st = P.stage()
        st['sp'].append(lambda e: e.dma_start(out=gbc[:], in_=gbc_d))
        st['sp'].append(lambda e: e.dma_start(out=xt[0][:], in_=src[0:128, :]))
        st['dve'].append(lambda e: e.memset(ss[:], 0.0))
        for i in range(ntile + 1):
            b = i % 2
            pb = (i - 1) % 2
            s1 = P.stage()
            if i < ntile:
                s1['act'].append(lambda e, b=b: e.activation(out=junk[:], in_=xt[b][:], func=AF.Square,
                                                               accum_out=ss[:, b:b + 1]))
            if i + 1 < ntile:
                s1['sp'].append(lambda e, i=i: e.dma_start(out=xt[(i + 1) % 2][:],
                                                           in_=src[(i + 1) * 128:(i + 2) * 128, :]))
            if i >= 1:
                for kc in range(KC):
                    s1['pe'].append(lambda e, kc=kc, pb=pb: e.transpose(
                        ptr[kc // 8][:, (kc % 8) * 128:(kc % 8 + 1) * 128],
                        xn[pb][:, kc * 128:(kc + 1) * 128], ident[:]))
            s2 = P.stage()
            if i < ntile:
                s2['act'].append(lambda e, b=b: e.activation(out=rstd[:, b:b + 1], in_=ss[:, b:b + 1], func=AF.Sqrt,
                                                              scale=1.0 / D, bias=EPS))
            if i >= 1:
                t0 = (i - 1) * 128
                for q in range(4):
                    eng = 'act' if q % 2 == 0 else 'dve'
                    o = dstT[:, q * 8:(q + 1) * 8, t0:t0 + 128]
                    src_ps = ptr[q][:].rearrange("p (a b) -> p a b", a=8)
                    if eng == 'act':
                        s2['act'].append(lambda e, o=o, s=src_ps: e.copy(o, s))
                    else:
                        s2['dve'].append(lambda e, o=o, s=src_ps: e.tensor_copy(o, s))
            if i < ntile:
                s3 = P.stage()
                s3['dve'].append(lambda e, b=b: e.reciprocal(rstd[:, b:b + 1], rstd[:, b:b + 1]))
                s3['pool'].append(lambda e, b=b: e.memset(ss[:, b:b + 1], 0.0))
                s4 = P.stage()
                s4['dve'].append(lambda e, b=b: e.scalar_tensor_tensor(out=xn[b][:], in0=xt[b][:],
                                                                        scalar=rstd[:, b:b + 1], in1=gbc[:],
                                                                        op0=ALU.mult, op1=ALU.mult))
        P.emit()

    with ExitStack() as top:
        ident = sbt(top, "ident", [128, 128], BF16)
        P = Prog(nc, "init")
        st = P.stage()
        st['pool'].append(lambda e: e.memset(ident[:], 0.0))
        st = P.stage()
        st['pool'].append(lambda e: e.affine_select(out=ident[:], in_=ident[:], pattern=[[-1, 128]],
                                                    compare_op=ALU.not_equal, fill=1.0, base=0,
                                                    channel_multiplier=1))
        P.emit()

        with ExitStack() as es:
            uT = sbt(es, "uT", [128, KC, T], BF16)
            with ExitStack() as es1:
                norm_transpose("n1", es1, xb, g1bc, T, uT, ident)
            if debug == 'p1':
                return nc
            with ExitStack() as es2:
                wb = [sbt(es2, f"wb{i}", [128, KC, 128], BF16) for i in range(2)]
                ost = [sbt(es2, f"ost{i}", [128, T], F32) for i in range(2)]
                pp = [[pst(es2, f"pp{s}_{q}", [128, 512], F32) for q in range(4)] for s in range(2)]
                P = Prog(nc, "inproj")
                st = P.stage()
                st['poolq'].append(lambda e: e.dma_start(out=wb[0][:].rearrange("p a b -> p (a b)"), in_=win[0]))
                for s in range(NCH + 2):
                    st = P.stage()
                    if s + 1 < NCH:
                        st['poolq'].append(lambda e, s=s: e.dma_start(
                            out=wb[(s + 1) % 2][:].rearrange("p a b -> p (a b)"), in_=win[s + 1]))
                    if s < NCH:
                        for kc in range(KC):
                            for q in range(4):
                                st['pe'].append(lambda e, s=s, kc=kc, q=q: e.matmul(
                                    pp[s % 2][q][:], wb[s % 2][:, kc, :], uT[:, kc, q * 512:(q + 1) * 512],
                                    start=(kc == 0), stop=(kc == KC - 1)))
                    if 1 <= s <= NCH:
                        j = s - 1
                        for q in range(4):
                            o = ost[j % 2][:, q * 512:(q + 1) * 512]
                            if q % 2 == 0:
                                st['act'].append(lambda e, o=o, j=j, q=q: e.copy(o, pp[j % 2][q][:]))
                            else:
                                st['dve'].append(lambda e, o=o, j=j, q=q: e.tensor_copy(o, pp[j % 2][q][:]))
                    if 2 <= s:
                        j = s - 2
                        st['sp'].append(lambda e, j=j: e.dma_start(out=pT[j * 128:(j + 1) * 128, :],
                                                                    in_=ost[j % 2][:]))
                P.emit()
            if debug == 'p2':
                return nc

        with ExitStack() as es:
            ptab = sbt(es, "ptabh", [64, 192], F32)
            ptl = sbt(es, "ptabl", [128, 4], F32)
            omk = sbt(es, "omk", [64, 16], F32)
            g8 = sbt(es, "g8", [64, 16], F32)
            lora = sbt(es, "lora", [128, 4, T], BF16)
            w2b = sbt(es, "w2b", [96, RW], BF16)
            a2b = sbt(es, "a2b", [96, RW], BF16)
            g2b = sbt(es, "g2b", [128, 2, RW], BF16)
            masks = sbt(es, "masks", [64, 3, 512], F32)
            masks2 = sbt(es, "masks2", [64, 1024], F32)
            ones64 = sbt(es, "ones64", [64, 64], F32)
            cmask = sbt(es, "cmask", [64, 512], F32)
            HT = 1024
            ARs = sbt(es, "ARs", [64, 8, 2 * HT], BF16)
            Bs = sbt(es, "Bs", [64, 8, HT], BF16)
            Ks = sbt(es, "Ks", [64, 8, HT], BF16)
            Vs = sbt(es, "Vs", [64, 8, HT], BF16)
            WC = sbt(es, "WC", [64, 8, 16], F32)
            obfs = [sbt(es, f"obf{i}", [64, 512], BF16) for i in range(2)]
            STs = [sbt(es, f"ST{g}", [64, 8, 64], F32) for g in range(2)]
            STbs = [sbt(es, f"STb{g}", [64, 8, 64], BF16) for g in range(2)]
            id64 = ident[0:64, 0:64]

            def merge_units(P, fns):
                subs = []
                for fn in fns:
                    Pu = Prog(nc, "sub")
                    fn(Pu)
                    subs.append(Pu.stages)
                n = len(subs[0])
                assert all(len(x) == n for x in subs)
                for k in range(n):
                    stg = P.stage()
                    for x in subs:
                        for e in ENGS:
                            stg[e].extend(x[k][e])

            def M3(i):
                return masks[:, i, :].rearrange("p (a b) -> p a b", a=8)

            def v8(t):
                return t[0:64, :].rearrange("p (a b) -> p a b", a=8)

            et = ExitStack()
            raw = sbt(et, "raw", [128, 4, 513], F32)
            dd = sbt(et, "dd", [128, 4, 512], F32)
            sh = sbt(et, "sh", [128, 4, 512], F32)
            P = Prog(nc, "rw0")
            st = P.stage()
            for (dst, srcd) in ((ptab, ptabh_d), (ptl, ptabl_d), (masks, masks_d), (masks2, masks2_d), (cmask, cmask_d[0:64, :])):
                st['sp'].append(lambda e, dst=dst, srcd=srcd: e.dma_start(out=dst[:], in_=srcd))
            for (dst, srcd) in ((w2b, w2_d), (a2b, a2_d), (g2b, g2w_d)):
                st['poolq'].append(lambda e, dst=dst, srcd=srcd: e.dma_start(out=dst[:], in_=srcd))
            st['pool'].append(lambda e: e.memset(ones64[:], 1.0))
            for g in range(2):
                st['dve'].append(lambda e, g=g: e.memset(STs[g][:], 0.0))
                st['pool'].append(lambda e, g=g: e.memset(STbs[g][:], 0.0))
            st = P.stage()
            kav = ptab[:, :].rearrange("p (h k) -> p h k", k=12)[:, :, 6]
            lgv = ptab[:, :].rearrange("p (h k) -> p h k", k=12)[:, :, 9]
            st['dve'].append(lambda e: e.tensor_scalar(omk[:], kav, -1.0, 1.0, ALU.mult, ALU.add))
            st['dve'].append(lambda e: e.tensor_scalar(g8[:], lgv, 8.0, None, ALU.mult))
            for tb in range(4):
                c0 = tb * 512
                st = P.stage()
                for q in range(4):
                    rows = slice((24 + q) * 128, (25 + q) * 128)
                    if tb == 0:
                        st['sp'].append(lambda e, q=q, rows=rows: e.dma_start(out=raw[:, q, 1:513],
                                                                               in_=pT[rows, 0:512]))
                    else:
                        st['sp'].append(lambda e, q=q, rows=rows, c0=c0: e.dma_start(
                            out=raw[:, q, 0:513], in_=pT[rows, c0 - 1:c0 + 512]))
                if tb == 0:
                    st['pool'].append(lambda e: e.memset(raw[:, :, 0:1], 0.0))
                st = P.stage()
                st['dve'].append(lambda e: e.tensor_tensor(out=dd[:], in0=raw[:, :, 0:512], in1=raw[:, :, 1:513],
                                                            op=ALU.subtract))
                st = P.stage()
                for q in range(4):
                    st['dve'].append(lambda e, q=q: e.scalar_tensor_tensor(
                        out=sh[:, q, :], in0=dd[:, q, :], scalar=ptl[:, q:q + 1], in1=raw[:, q, 1:513],
                        op0=ALU.mult, op1=ALU.add))
                st = P.stage()
                for q in range(4):
                    fn = [AF.Tanh, AF.Copy, AF.Sigmoid, AF.Sigmoid][q]
                    st['act'].append(lambda e, q=q, fn=fn, c0=c0: e.activation(out=lora[:, q, c0:c0 + 512],
                                                                               in_=sh[:, q, :], func=fn))
            P.emit()
            et.close()
            if debug == 'lora':
                return nc

            for gi in range(2):
              for half in range(2):
                ST, STb = STs[gi], STbs[gi]
                et = ExitStack()
                raws = [sbt(et, f"raw{i}", [64, 3, 513], F32) for i in range(2)]
                shs = [sbt(et, f"sh{i}", [64, 3, 512], F32) for i in range(2)]
                tmpbs = [sbt(et, f"tmpb{i}", [64, 14, 512], F32) for i in range(2)]
                pps = [[pst(et, f"pp{i}_{q}", [128, 512], F32) for q in range(4)] for i in range(2)]
                P = Prog(nc, f"rwp{gi}{half}")
                for hl2 in range(4):
                  for tbl in range(2):
                    fns = []
                    for slot in range(2):
                      def unit(P, slot=slot, hl2=hl2, tbl=tbl):
                        hl = hl2 * 2 + slot
                        h = gi * 8 + hl
                        pc = h * 12
                        col = lambda k, pc=pc: ptab[:, pc + k:pc + k + 1]
                        cs = slice(h * 64, (h + 1) * 64)
                        raw, sh, tmpb = raws[slot], shs[slot], tmpbs[slot]
                        pA, pB, pC, pD = pps[slot]
                        dd = tmpb[:, 11:14, :]
                        tmp = [tmpb[:, i, :] for i in range(14)]
                        if True:
                            tb = half * 2 + tbl
                            c0 = tb * 512
                            l0 = tbl * 512
                            (kk, kk2, sg, aicl, gst, rinv, logw, t1, cum, kkn, k2, Wt, Winv, lp) = tmp
                            st = P.stage()
                        for q in range(3):
                            rows = slice(q * 1024 + h * 64, q * 1024 + (h + 1) * 64)
                            if tb == 0:
                                st['sp'].append(lambda e, q=q, rows=rows: e.dma_start(out=raw[:, q, 1:513],
                                                                                       in_=pT[rows, 0:512]))
                            else:
                                st['sp'].append(lambda e, q=q, rows=rows, c0=c0: e.dma_start(
                                    out=raw[:, q, 0:513], in_=pT[rows, c0 - 1:c0 + 512]))
                        if tb == 0:
                            st['pool'].append(lambda e: e.memset(raw[:, 0:3, 0:1], 0.0))
                        st = P.stage()
                        st['dve'].append(lambda e: e.tensor_tensor(out=dd[:], in0=raw[:, :, 0:512],
                                                                    in1=raw[:, :, 1:513], op=ALU.subtract))
                        for q in range(3):
                            st['dve'].append(lambda e, q=q, col=col: e.scalar_tensor_tensor(
                                out=sh[:, q, :], in0=dd[:, q, :], scalar=col(q), in1=raw[:, q, 1:513],
                                op0=ALU.mult, op1=ALU.add))
                        rs, ks, vs = sh[:, 0, :], sh[:, 1, :], sh[:, 2, :]
                        A3 = lambda ap: ap.rearrange("p (c n) -> p c n", n=64)
                        ARv = ARs[:, hl, tbl * 1024:(tbl + 1) * 1024].rearrange("p (c two n) -> p c two n", two=2, n=64)
                        st = P.stage()
                        st['pe'].append(lambda e, cs=cs, c0=c0: e.matmul(pA[0:64, :], w2b[0:96, cs], lora[0:96, 0, c0:c0 + 512],
                                                                           start=True, stop=True))
                        st['pe'].append(lambda e, cs=cs, c0=c0: e.matmul(pB[0:64, :], a2b[0:96, cs], lora[0:96, 1, c0:c0 + 512],
                                                                           start=True, stop=True))
                        st['pe'].append(lambda e, cs=cs, c0=c0: e.matmul(pC[0:64, :], g2b[:, 0, cs], lora[:, 2, c0:c0 + 512],
                                                                           start=True, stop=False))
                        st['pe'].append(lambda e, cs=cs, c0=c0: e.matmul(pC[0:64, :], g2b[:, 1, cs], lora[:, 3, c0:c0 + 512],
                                                                           start=False, stop=True))
                        st['act'].append(lambda e, col=col, ks=ks: e.activation(out=kk2[:], in_=ks, func=AF.Square,
                                                                                 scale=col(5)))
                        st['act'].append(lambda e, col=col, ks=ks: e.activation(out=kk[:], in_=ks, func=AF.Copy,
                                                                                 scale=col(5)))
                        st = P.stage()
                        st['pe'].append(lambda e: e.matmul(pD[0:64, :], ones64[:], kk2[:], start=True, stop=True))
                        st['act'].append(lambda e, col=col: e.activation(out=sg[:], in_=pA[0:64, :], func=AF.Sigmoid,
                                                                          bias=col(3)))
                        st['act'].append(lambda e, col=col: e.activation(out=aicl[:], in_=pB[0:64, :], func=AF.Sigmoid,
                                                                          bias=col(4)))
                        st['dve'].append(lambda e: e.tensor_copy(gst[:], pC[0:64, :]))
                        st = P.stage()
                        st['act'].append(lambda e: e.activation(out=rinv[:], in_=pD[0:64, :], func=AF.Ln, bias=1e-24))
                        st['act'].append(lambda e: e.activation(out=rinv[:], in_=rinv[:], func=AF.Exp, scale=-0.5))
                        st['act'].append(lambda e: e.mul(logw[:], sg[:], -0.6065306597126334))
                        st['dve'].append(lambda e, col=col, h=h: e.tensor_scalar(t1[:], aicl[:], col(6), omk[:, h:h + 1],
                                                                                  ALU.mult, ALU.add))
                        st['dve'].append(lambda e, ks=ks: e.tensor_tensor(out=k2[:], in0=ks, in1=t1[:], op=ALU.mult))
                        st['dve'].append(lambda e, rs=rs: e.tensor_tensor(out=kk2[:], in0=rs, in1=k2[:], op=ALU.mult))
                        st['dve'].append(lambda e, col=col: e.tensor_scalar(t1[:], kk2[:], col(8), None, ALU.mult))
                        st['sp'].append(lambda e, cs=cs, c0=c0: e.dma_start(out=gT[cs, c0:c0 + 512], in_=gst[:]))
                        st = P.stage()
                        st['dve'].append(lambda e: e.tensor_tensor_scan(out=cum[:], data0=cmask[:], data1=logw[:],
                                                                         initial=0.0, op0=ALU.mult, op1=ALU.add))
                        st['dve'].append(lambda e: e.tensor_tensor(out=lp[:], in0=cum[:], in1=logw[:], op=ALU.subtract))
                        st['dve'].append(lambda e: e.tensor_tensor(out=kkn[:], in0=kk[:], in1=rinv[:], op=ALU.mult))
                        st['dve'].append(lambda e: e.tensor_tensor(out=kk[:], in0=kkn[:], in1=aicl[:], op=ALU.mult))
                        st['pe'].append(lambda e: e.matmul(pA[0:64, :], ones64[:], t1[:], start=True, stop=True))
                        st['act'].append(lambda e, hl=hl, l0=l0, vs=vs: e.copy(Vs[:, hl, l0:l0 + 512], vs))
                        st = P.stage()
                        st['act'].append(lambda e: e.activation(out=Wt[:], in_=cum[:], func=AF.Exp))
                        st['act'].append(lambda e: e.activation(out=Winv[:], in_=cum[:], func=AF.Exp, scale=-1.0))
                        st['act'].append(lambda e: e.activation(out=sg[:], in_=lp[:], func=AF.Exp))
                        st['dve'].append(lambda e, vs=vs: e.tensor_tensor(out=gst[:], in0=pA[0:64, :], in1=vs, op=ALU.mult))
                        st = P.stage()
                        st['dve'].append(lambda e, rs=rs, ARv=ARv: e.tensor_tensor(out=ARv[:, :, 1, :], in0=A3(rs), in1=A3(Wt),
                                                                                    op=ALU.mult))
                        st['dve'].append(lambda e, ARv=ARv: e.scalar_tensor_tensor(out=ARv[:, :, 0, :], in0=A3(kkn), scalar=-1.0,
                                                                                    in1=A3(sg), op0=ALU.mult, op1=ALU.mult))
                        st['dve'].append(lambda e, hl=hl, l0=l0: e.tensor_tensor(out=Ks[:, hl, l0:l0 + 512], in0=k2[:],
                                                                                  in1=Winv[:], op=ALU.mult))
                        st['dve'].append(lambda e, hl=hl, l0=l0: e.tensor_tensor(out=Bs[:, hl, l0:l0 + 512], in0=kk[:],
                                                                                  in1=Winv[:], op=ALU.mult))
                        st['dve'].append(lambda e, hl=hl, tbl=tbl: e.tensor_copy(
                            WC[:, hl, tbl * 8:(tbl + 1) * 8], A3(Wt)[:, :, 63]))
                        st['sp'].append(lambda e, cs=cs, c0=c0: e.dma_start(out=bvT[cs, c0:c0 + 512], in_=gst[:]))
                      fns.append(unit)
                    merge_units(P, fns)
                P.emit()
                et.close()
                if debug == 'prep':
                    return nc

                et = ExitStack()
                ybig = sbt(et, "ybig", [64, 8, HT], F32)
                ptr1 = pst(et, "ptr1", [128, 1024], BF16)
                ptr2 = pst(et, "ptr2", [128, 1024], BF16)
                pABs = [pst(et, f"pAB{i}", [128, 512], F32) for i in range(2)]
                pKRs = [pst(et, f"pKR{i}", [128, 512], F32) for i in range(2)]

                def hb2(ts, h):
                    return ts[h // 4][0:64, (h % 4) * 128:(h % 4 + 1) * 128]
                pT_ = pst(et, "pT_", [128, 512], F32)
                pF = pst(et, "pF", [128, 512], F32)
                et_s = ExitStack()
                tmpS = sbt(et_s, "tmpS", [64, 8, 64], F32)
                tokms = [sbt(et_s, f"tokm{i}", [64, 3, 8, 64], BF16) for i in range(2)]
                ABm = sbt(et_s, "ABm", [64, 8, 128], BF16)
                AKm = sbt(et_s, "AKm", [64, 8, 128], BF16)
                Pp = [sbt(et_s, f"Pp{i}", [64, 8, 64], BF16) for i in range(2)]
                ZQ = [sbt(et_s, f"ZQ{i}", [64, 8, 128], BF16) for i in range(2)]
                Zq = [sbt(et_s, f"Zq{i}", [64, 8, 64], F32) for i in range(2)]
                UTb = sbt(et_s, "UTb", [64, 8, 64], BF16)
                m2v = masks2[:, :].rearrange("p (a b) -> p a b", a=8)

                def v16(t):
                    return t[0:64, :].rearrange("p (a b) -> p a b", a=8)
                P = Prog(nc, f"rws{gi}{half}")
                for c in range(HT // CH):
                    tc = slice(c * 64, (c + 1) * 64)
                    ac = slice(c * 128, c * 128 + 64)
                    rc = slice(c * 128 + 64, c * 128 + 128)
                    arc = slice(c * 128, (c + 1) * 128)
                    tokm = tokms[c % 2]

                    def emit_T(stg, cn):
                        tcn = slice(cn * 64, (cn + 1) * 64)
                        for h in range(8):
                            hc = slice(h * 64, (h + 1) * 64)
                            stg['pe'].append(lambda e, h=h, hc=hc, tcn=tcn: e.transpose(ptr1[0:64, hc], Bs[:, h, tcn], id64))
                            stg['pe'].append(lambda e, h=h, hc=hc, tcn=tcn: e.transpose(
                                ptr1[0:64, 512 + h * 64:512 + (h + 1) * 64], Ks[:, h, tcn], id64))
                            stg['pe'].append(lambda e, h=h, hc=hc, tcn=tcn: e.transpose(ptr2[0:64, hc], Vs[:, h, tcn], id64))

                    def emit_Tevac(stg, cn):
                        tk = tokms[cn % 2]
                        stg['act'].append(lambda e, tk=tk: e.copy(tk[:, 0:2, :, :].rearrange("p a b c -> p (a b c)"), ptr1[0:64, :]))
                        stg['act'].append(lambda e, tk=tk: e.copy(tk[:, 2, :, :].rearrange("p b c -> p (b c)"), ptr2[0:64, 0:512]))
                    if c == 0:
                        st = P.stage()
                        emit_T(st, 0)
                        st = P.stage()
                        emit_Tevac(st, 0)
                    st = P.stage()
                    for h in range(8):
                        hc = slice(h * 64, (h + 1) * 64)
                        hc2 = slice(h * 128, (h + 1) * 128)
                        Bt, Kt = Bs[:, h, tc], Ks[:, h, tc]
                        At, ARc = ARs[:, h, ac], ARs[:, h, arc]
                        st['pe'].append(lambda e, h=h, Bt=Bt, ARc=ARc: e.matmul(hb2(pABs, h), Bt, ARc, start=True, stop=True))
                        st['pe'].append(lambda e, h=h, Kt=Kt, ARc=ARc: e.matmul(hb2(pKRs, h), Kt, ARc, start=True, stop=True))
                        st['pe'].append(lambda e, hc=hc, At=At, Bt=Bt: e.matmul(pT_[0:64, hc], At, Bt, start=True, stop=True))
                    st = P.stage()
                    for hb in range(2):
                        hs4 = slice(hb * 4, hb * 4 + 4)
                        st['dve'].append(lambda e, hs4=hs4, hb=hb: e.tensor_tensor(
                            out=AKm[:, hs4, :], in0=pKRs[hb][0:64, :].rearrange("p (a b) -> p a b", a=4), in1=m2v[:, hs4, :], op=ALU.mult))
                        st['dve'].append(lambda e, hs4=hs4, hb=hb: e.tensor_tensor(
                            out=ABm[:, hs4, :], in0=pABs[hb][0:64, :].rearrange("p (a b) -> p a b", a=4), in1=m2v[:, hs4, :], op=ALU.mult))
                    st['dve'].append(lambda e: e.tensor_tensor(out=ZQ[0][:, :, 64:128], in0=v8(pT_), in1=M3(2), op=ALU.mult))
                    st = P.stage()
                    for h in range(8):
                        hc = slice(h * 64, (h + 1) * 64)
                        At = ARs[:, h, ac]
                        VTh = tokm[:, 2, h, :]
                        st['pe'].append(lambda e, At=At, h=h, hc=hc: e.matmul(
                            pF[0:64, hc], At, STb[:, h, :], start=True, stop=False))
                        st['pe'].append(lambda e, h=h, VTh=VTh, hc=hc: e.matmul(
                            pF[0:64, hc], AKm[:, h, 0:64], VTh, start=False, stop=True))
                    st = P.stage()
                    st['act'].append(lambda e: e.copy(Zq[0][:], v8(pF)))
                    st['act'].append(lambda e: e.copy(ZQ[0][:, :, 0:64], v8(pF)))
                    for j in range(1, 7):
                        st = P.stage()
                        if j == 1 and c + 1 < HT // CH:
                            emit_T(st, c + 1)
                        zi, zo = Zq[(j - 1) % 2], Zq[j % 2]
                        zqi, zqo = ZQ[(j - 1) % 2], ZQ[j % 2]
                        for h in range(8):
                            hc = slice(h * 64, (h + 1) * 64)
                            hc2 = slice(h * 128, (h + 1) * 128)
                            Pj = ABm[:, h, 0:64] if j == 1 else Pp[j % 2][:, h, :]
                            if j < 6:
                                st['pe'].append(lambda e, Pj=Pj, zqi=zqi, h=h: e.matmul(
                                    hb2(pKRs, h), Pj, zqi[:, h, :], start=True, stop=True))
                                st['pe'].append(lambda e, hc=hc, Pj=Pj, zqi=zqi, h=h: e.matmul(
                                    pT_[0:64, hc], zqi[:, h, 64:128], Pj, start=True, stop=True))
                            else:
                                st['pe'].append(lambda e, Pj=Pj, zqi=zqi, h=h: e.matmul(
                                    hb2(pKRs, h)[:, 0:64], Pj, zqi[:, h, 0:64], start=True, stop=True))
                        st = P.stage()
                        for hb in range(2):
                            hs4 = slice(hb * 4, hb * 4 + 4)
                            pk = pKRs[hb][0:64, :].rearrange("p (a b) -> p a b", a=4)
                            zp = pk[:, :, 0:64]
                            if j < 6:
                                st['dve'].append(lambda e, zi=zi, zo=zo, zp=zp, hs4=hs4: e.tensor_tensor(
                                    out=zo[:, hs4, :], in0=zp, in1=zi[:, hs4, :], op=ALU.add))
                                st['dve'].append(lambda e, zi=zi, zqo=zqo, zp=zp, hs4=hs4: e.tensor_tensor(
                                    out=zqo[:, hs4, 0:64], in0=zp, in1=zi[:, hs4, :], op=ALU.add))
                                st['dve'].append(lambda e, zqo=zqo, pk=pk, hs4=hs4: e.tensor_copy(zqo[:, hs4, 64:128], pk[:, :, 64:128]))
                            else:
                                st['dve'].append(lambda e, zi=zi, zp=zp, hs4=hs4: e.tensor_tensor(
                                    out=UTb[:, hs4, :], in0=zp, in1=zi[:, hs4, :], op=ALU.add))
                        if j < 6:
                            st['act'].append(lambda e, j=j: e.copy(Pp[(j + 1) % 2][:], v8(pT_)))
                        if j == 2 and c + 1 < HT // CH:
                            emit_Tevac(st, c + 1)
                    st = P.stage()
                    for h in range(8):
                        hc = slice(h * 64, (h + 1) * 64)
                        Rt = ARs[:, h, rc]
                        VTh = tokm[:, 2, h, :]
                        BTh = tokm[:, 0, h, :]
                        KTh = tokm[:, 1, h, :]
                        oy = pABs[0][0:64, hc]
                        os_ = pABs[1][0:64, hc]
                        st['pe'].append(lambda e, oy=oy, h=h, Rt=Rt: e.matmul(oy, STb[:, h, :], Rt,
                                                                               start=True, stop=False))
                        st['pe'].append(lambda e, oy=oy, h=h: e.matmul(oy, UTb[:, h, :], ABm[:, h, 64:128],
                                                                        start=False, stop=False))
                        st['pe'].append(lambda e, oy=oy, h=h, VTh=VTh: e.matmul(oy, VTh, AKm[:, h, 64:128],
                                                                                 start=False, stop=True))
                        st['pe'].append(lambda e, os_=os_, h=h, BTh=BTh: e.matmul(os_, BTh, UTb[:, h, :],
                                                                                   start=True, stop=False))
                        st['pe'].append(lambda e, os_=os_, KTh=KTh, VTh=VTh: e.matmul(os_, KTh, VTh,
                                                                                       start=False, stop=True))
                    st = P.stage()
                    st['act'].append(lambda e, tc=tc: e.copy(ybig[:, :, tc], v8(pABs[0])))
                    st['dve'].append(lambda e: e.tensor_tensor(out=tmpS[:], in0=v8(pABs[1]), in1=ST[:], op=ALU.add))
                    wcb = WC[:, :, c].unsqueeze(2).to_broadcast([64, 8, 64])
                    st['dve'].append(lambda e, wcb=wcb: e.tensor_tensor(out=ST[:], in0=tmpS[:], in1=wcb, op=ALU.mult))
                    st['dve'].append(lambda e, wcb=wcb: e.tensor_tensor(out=STb[:], in0=tmpS[:], in1=wcb, op=ALU.mult))
                if debug == 'scan' and _os.environ.get('SCAN_STOP'):
                    P.stages = P.stages[:int(_os.environ['SCAN_STOP'])]
                P.emit()
                et_s.close()
                if debug == 'scan':
                    et.close()
                    return nc

                et2 = ExitStack()
                tmps = [[sbt(et2, f"tmp{s}_{i}", [64, 512], F32) for i in range(5)] for s in range(4)]
                obfs4 = obfs + [sbt(et2, f"obfx{i}", [64, 512], BF16) for i in range(2)]
                pbank = [pABs[0], pABs[1], pKRs[0], pKRs[1]]
                P = Prog(nc, f"rwo{gi}{half}")
                for hl2 in range(2):
                  for tbl in range(2):
                    fns = []
                    for slot in range(4):
                      def unit(P, slot=slot, hl2=hl2, tbl=tbl):
                        hl = hl2 * 4 + slot
                        h = gi * 8 + hl
                        pc = h * 12
                        cs = slice(h * 64, (h + 1) * 64)
                        tmp = tmps[slot]
                        obf = obfs4[slot]
                        pA_ = pB_ = pbank[slot]
                        if True:
                            tb = half * 2 + tbl
                            c0 = tb * 512
                            l0 = tbl * 512
                            gl_, bvl, yc, sq, rs_ = tmp[0:5]
                            t_a, t_b, t_c = sq, yc, sq
                            st = P.stage()
                        st['sp'].append(lambda e, cs=cs, c0=c0: e.dma_start(out=gl_[:], in_=gT[cs, c0:c0 + 512]))
                        st['sp'].append(lambda e, cs=cs, c0=c0: e.dma_start(out=bvl[:], in_=bvT[cs, c0:c0 + 512]))
                        st['pe'].append(lambda e, hl=hl, l0=l0: e.matmul(pA_[0:64, :], ones64[:], ybig[:, hl, l0:l0 + 512],
                                                                          start=True, stop=True))
                        st = P.stage()
                        st['dve'].append(lambda e, hl=hl, l0=l0: e.scalar_tensor_tensor(
                            out=yc[:], in0=pA_[0:64, :], scalar=-1.0 / 64, in1=ybig[:, hl, l0:l0 + 512],
                            op0=ALU.mult, op1=ALU.add))
                        st = P.stage()
                        st['act'].append(lambda e: e.activation(out=sq[:], in_=yc[:], func=AF.Square))
                        st = P.stage()
                        st['pe'].append(lambda e: e.matmul(pB_[0:64, :], ones64[:], sq[:], start=True, stop=True))
                        st = P.stage()
                        st['act'].append(lambda e: e.activation(out=rs_[:], in_=pB_[0:64, :], func=AF.Ln, bias=64.0 * GN_EPS))
                        st['act'].append(lambda e: e.activation(out=rs_[:], in_=rs_[:], func=AF.Exp, scale=-0.5))
                        st = P.stage()
                        st['dve'].append(lambda e: e.tensor_tensor(out=t_a[:], in0=yc[:], in1=rs_[:], op=ALU.mult))
                        st['dve'].append(lambda e, h=h, pc=pc: e.tensor_scalar(t_b[:], t_a[:], g8[:, h:h + 1],
                                                                                 ptab[:, pc + 10:pc + 11],
                                                                                 ALU.mult, ALU.add))
                        st['dve'].append(lambda e: e.tensor_tensor(out=t_c[:], in0=t_b[:], in1=bvl[:], op=ALU.add))
                        st['dve'].append(lambda e: e.tensor_tensor(out=obf[:], in0=t_c[:], in1=gl_[:], op=ALU.mult))
                        st = P.stage()
                        st['sp'].append(lambda e, cs=cs, c0=c0: e.dma_start(out=ybuf[cs, c0:c0 + 512], in_=obf[:]))
                      fns.append(unit)
                    merge_units(P, fns)
                P.emit()
                et2.close()
                et.close()
                if debug == 'post':
                    return nc

        if debug == 'rwkv':
            return nc
        with ExitStack() as es:
            ptab = sbt(es, "ptabL", [128, 72], F32)
            cch = sbt(es, "cch", [128, 8], F32)
            c2h = sbt(es, "c2h", [128, 8], F32)
            etmp = sbt(es, "etmp", [128, 8], F32)
            wrb = sbt(es, "wrb", [128, 8, 128], BF16)
            wib = sbt(es, "wib", [128, 8, 128], BF16)
            L_xpad = [sbt(es, f"xpad{i}", [128, 515], F32) for i in range(2)]
            L_gat = [sbt(es, f"gat{i}", [128, 512], F32) for i in range(2)]
            L_tl = [[sbt(es, f"tl{s_}_{i}", [128, 512], F32) for i in range(12)] for s_ in range(2)]
            L_xcb = [sbt(es, f"xcb{i}", [128, 512], BF16) for i in range(2)]
            L_ybf = [sbt(es, f"ybf{i}", [128, 512], BF16) for i in range(2)]
            L_hcar = [sbt(es, f"hcar{i}", [128, 1], F32) for i in range(2)]
            L_p1 = [pst(es, f"lp1_{i}", [128, 512], F32) for i in range(2)]
            L_p2 = [pst(es, f"lp2_{i}", [128, 512], F32) for i in range(2)]
            hcar = L_hcar[0]

            def merge_units_l(P, fns):
                subs = []
                for fn in fns:
                    Pu = Prog(nc, "sub")
                    fn(Pu)
                    subs.append(Pu.stages)
                n = len(subs[0])
                assert all(len(x) == n for x in subs)
                for k in range(n):
                    stg = P.stage()
                    for x in subs:
                        for e in ENGS:
                            stg[e].extend(x[k][e])
            P = Prog(nc, "lru")
            st = P.stage()
            st['sp'].append(lambda e: e.dma_start(out=ptab[:], in_=ptabu_d))
            st['poolq'].append(lambda e: e.dma_start(out=wrb[:], in_=wr_d))
            st['poolq'].append(lambda e: e.dma_start(out=wib[:], in_=wi_d))
            lamv = ptab[:, 0:72].rearrange("p (j k) -> p j k", k=9)[:, :, 7]
            st = P.stage()
            st['act'].append(lambda e: e.activation(out=etmp[:], in_=lamv, func=AF.Exp, scale=-1.0))
            st = P.stage()
            st['act'].append(lambda e: e.activation(out=etmp[:], in_=etmp[:], func=AF.Ln, bias=1.0))
            st = P.stage()
            st['dve'].append(lambda e: e.tensor_scalar(cch[:], etmp[:], -8.0, None, ALU.mult))
            st['dve'].append(lambda e: e.tensor_scalar(c2h[:], etmp[:], -16.0, None, ALU.mult))
            for jj in range(4):
              for tb in range(4):
                fns = []
                for slot in range(2):
                  def unit(P, slot=slot, jj=jj, tb=tb):
                    j = jj * 2 + slot
                    pc = j * 9
                    col = lambda k, pc=pc: ptab[:, pc + k:pc + k + 1]
                    xpad, gat, tl, xcb, ybf, hcar = L_xpad[slot], L_gat[slot], L_tl[slot], L_xcb[slot], L_ybf[slot], L_hcar[slot]
                    p1, p2 = L_p1[slot], L_p2[slot]
                    c0 = tb * 512
                    (xc0, xc1, x2, inner, inner2, sgm, gel, rg, ig, av, a2v, gx) = tl
                    rx = slice((28 + j) * 128, (29 + j) * 128)
                    rg_ = slice((36 + j) * 128, (37 + j) * 128)
                    st = P.stage()
                    if tb == 0:
                        st['sp'].append(lambda e, rx=rx: e.dma_start(out=xpad[:, 3:515], in_=pT[rx, 0:512]))
                    else:
                        st['sp'].append(lambda e, rx=rx, c0=c0: e.dma_start(out=xpad[:], in_=pT[rx, c0 - 3:c0 + 512]))
                    st['sp'].append(lambda e, rg_=rg_, c0=c0: e.dma_start(out=gat[:], in_=pT[rg_, c0:c0 + 512]))
                    xc = xc1
                    st = P.stage()
                    if tb == 0:
                        st['dve'].append(lambda e: e.memset(xpad[:, 0:3], 0.0))
                    st['dve'].append(lambda e, col=col: e.tensor_scalar(xc0[:], xpad[:, 0:512], col(0), col(4),
                                                                         ALU.mult, ALU.add))
                    st['dve'].append(lambda e, col=col: e.scalar_tensor_tensor(out=xc1[:], in0=xpad[:, 1:513], scalar=col(1),
                                                                                in1=xc0[:], op0=ALU.mult, op1=ALU.add))
                    st['dve'].append(lambda e, col=col: e.scalar_tensor_tensor(out=xc0[:], in0=xpad[:, 2:514], scalar=col(2),
                                                                                in1=xc1[:], op0=ALU.mult, op1=ALU.add))
                    st['dve'].append(lambda e, col=col: e.scalar_tensor_tensor(out=xc1[:], in0=xpad[:, 3:515], scalar=col(3),
                                                                                in1=xc0[:], op0=ALU.mult, op1=ALU.add))
                    st['act'].append(lambda e: e.activation(out=x2[:], in_=gat[:], func=AF.Square))
                    st = P.stage()
                    st['act'].append(lambda e: e.copy(xcb[:], xc[:]))
                    st['dve'].append(lambda e: e.tensor_scalar(inner[:], x2[:], 0.044715, 1.0, ALU.mult, ALU.add))
                    st['dve'].append(lambda e: e.tensor_tensor(out=inner2[:], in0=inner[:], in1=gat[:], op=ALU.mult))
                    st = P.stage()
                    st['pe'].append(lambda e, j=j: e.matmul(p1[:], wrb[:, j, :], xcb[:], start=True, stop=True))
                    st['pe'].append(lambda e, j=j: e.matmul(p2[:], wib[:, j, :], xcb[:], start=True, stop=True))
                    st['act'].append(lambda e: e.activation(out=sgm[:], in_=inner2[:], func=AF.Sigmoid,
                                                             scale=1.5957691216057308))
                    st = P.stage()
                    st['dve'].append(lambda e: e.tensor_copy(rg[:], p1[:]))
                    st['dve'].append(lambda e: e.tensor_copy(ig[:], p2[:]))
                    st['dve'].append(lambda e: e.tensor_tensor(out=gel[:], in0=gat[:], in1=sgm[:], op=ALU.mult))
                    st = P.stage()
                    st['act'].append(lambda e, col=col: e.activation(out=rg[:], in_=rg[:], func=AF.Sigmoid, bias=col(5)))
                    st['act'].append(lambda e, col=col: e.activation(out=ig[:], in_=ig[:], func=AF.Sigmoid, bias=col(6)))
                    st['act'].append(lambda e, j=j: e.activation(out=av[:], in_=rg[:], func=AF.Exp, scale=cch[:, j:j + 1]))
                    st['act'].append(lambda e, j=j: e.activation(out=a2v[:], in_=rg[:], func=AF.Exp, scale=c2h[:, j:j + 1]))
                    st = P.stage()
                    st['dve'].append(lambda e: e.tensor_tensor(out=gx[:], in0=ig[:], in1=xc[:], op=ALU.mult))
                    st['dve'].append(lambda e: e.tensor_scalar(x2[:], a2v[:], 1.0, -1.0, ALU.min, ALU.mult))
                    st = P.stage()
                    if tb == 0:
                        st['act'].append(lambda e: e.activation(out=inner[:, 1:512], in_=x2[:, 1:512], func=AF.Sqrt, bias=1.0))
                        st['dve'].append(lambda e: e.memset(inner[:, 0:1], 1.0))
                    else:
                        st['act'].append(lambda e: e.activation(out=inner[:], in_=x2[:], func=AF.Sqrt, bias=1.0))
                    st = P.stage()
                    st['dve'].append(lambda e: e.tensor_tensor(out=inner2[:], in0=inner[:], in1=gx[:], op=ALU.mult))
                    if tb == 0:
                        st['dve'].append(lambda e: e.tensor_tensor_scan(out=sgm[:], data0=av[:], data1=inner2[:],
                                                                         initial=0.0, op0=ALU.mult, op1=ALU.add))
                    else:
                        st['dve'].append(lambda e: e.tensor_tensor_scan(out=sgm[:], data0=av[:], data1=inner2[:],
                                                                         initial=hcar[:, 0:1], op0=ALU.mult, op1=ALU.add))
                    st['dve'].append(lambda e: e.tensor_tensor(out=ybf[:], in0=sgm[:], in1=gel[:], op=ALU.mult))
                    st = P.stage()
                    st['act'].append(lambda e: e.copy(hcar[:], sgm[:, 511:512]))
                    rows = slice(RW + j * 128, RW + (j + 1) * 128)
                    st['sp'].append(lambda e, rows=rows, c0=c0: e.dma_start(out=ybuf[rows, c0:c0 + 512], in_=ybf[:]))
                  fns.append(unit)
                merge_units_l(P, fns)
            st = P.stage()
            if early:
                st['pool'].append(lambda e: e.memset(hcar[:], 0.0))
            else:
              for k in range(4):
                if k > 0:
                    st = P.stage()
                st['pool'].append(lambda e, k=k: e.collective_compute(
                    "AllGather", ALU.bypass, replica_groups=[[2 * i, 2 * i + 1] for i in range(ncores // 2)],
                    ins=[ybuf[k * 512:(k + 1) * 512, :].opt()], outs=[gbufs[k].opt()]))
            if debug is not None and _os.environ.get('LRU_STOP'):
                P.stages = P.stages[:int(_os.environ['LRU_STOP'])]
            P.emit()

        if early or debug == 'gather':
            return nc
        with ExitStack() as es:
            yT = sbt(es, "yT", [128, KC, TO], BF16)
            sel = sbt(es, "sel", [128, 2], F32)
            ngt = sbt(es, "ngt", [128, 16], F32)
            ones32 = sbt(es, "ones32", [128, 128], F32)
            rbc = sbt(es, "rbc", [128, TO], F32)
            with ExitStack() as es1:
                gl = [sbt(es1, f"gl{i}", [128, 2, TO], BF16) for i in range(3)]
                tb_ = [sbt(es1, f"tbl{i}", [128, TO], BF16) for i in range(2)]
                sqf = [sbt(es1, f"sqf{i}", [128, TO], F32) for i in range(2)]
                pq = [pst(es1, f"f1p{i}", [128, 512], F32) for i in range(2)]
                P = Prog(nc, "blend")
                st = P.stage()
                st['sp'].append(lambda e: e.dma_start(out=sel[:], in_=sel_d))
                st['sp'].append(lambda e: e.dma_start(out=ngt[:], in_=ngtab_d))
                st['pool'].append(lambda e: e.memset(ones32[:], 1.0))
                def gsrc(cc):
                    r, q = cc // 16, cc % 16
                    return gbufs[q // 4][r * 512 + (q % 4) * 128:r * 512 + (q % 4 + 1) * 128, :]
                st['sp'].append(lambda e: e.dma_start(out=gl[0][:].rearrange("p a b -> p (a b)"), in_=gsrc(0)))
                for cc in range(KC + 2):
                    st = P.stage()
                    if cc + 1 < KC:
                        st['sp'].append(lambda e, cc=cc: e.dma_start(
                            out=gl[(cc + 1) % 3][:].rearrange("p a b -> p (a b)"),
                            in_=gsrc(cc + 1)))
                    if cc < KC:
                        st['dve'].append(lambda e, cc=cc: e.tensor_scalar(tb_[cc % 2][:], gl[cc % 3][:, 0, :],
                                                                           sel[:, 0:1], None, ALU.mult))
                    if 1 <= cc <= KC:
                        c1 = cc - 1
                        st['dve'].append(lambda e, c1=c1: e.scalar_tensor_tensor(
                            out=yT[:, c1, :], in0=gl[c1 % 3][:, 1, :], scalar=sel[:, 1:2], in1=tb_[c1 % 2][:],
                            op0=ALU.mult, op1=ALU.add))
                lch = list(range(8, 16)) + list(range(24, 32))
                for q in range(17):
                    st = P.stage()
                    if q < 16:
                        st['act'].append(lambda e, q=q: e.activation(out=sqf[q % 2][:], in_=yT[:, lch[q], :],
                                                                      func=AF.Square))
                    if q >= 1:
                        for hf in range(2):
                            st['pe'].append(lambda e, q=q, hf=hf: e.matmul(
                                pq[hf][:], ones32[:], sqf[(q - 1) % 2][:, hf * 512:(hf + 1) * 512],
                                start=(q == 1), stop=(q == 16)))
                st = P.stage()
                for hf in range(2):
                    st['act'].append(lambda e, hf=hf: e.activation(out=rbc[:, hf * 512:(hf + 1) * 512], in_=pq[hf][:],
                                                                    func=AF.Sqrt, scale=1.0 / 2048, bias=EPS))
                st = P.stage()
                st['dve'].append(lambda e: e.reciprocal(rbc[:], rbc[:]))
                st = P.stage()
                for q in range(16):
                    eng = 'dve'
                    st[eng].append(lambda e, q=q: e.scalar_tensor_tensor(
                        out=yT[:, lch[q], :], in0=yT[:, lch[q], :], scalar=ngt[:, q:q + 1], in1=rbc[:],
                        op0=ALU.mult, op1=ALU.mult))
                P.emit()
            with ExitStack() as es2:
                wo = [sbt(es2, f"wo{i}", [128, KC, 512], BF16) for i in range(2)]
                xot = [sbt(es2, f"xot{i}", [128, 2, 512], F32) for i in range(2)]
                hst = [sbt(es2, f"hst{i}", [128, 2, 512], F32) for i in range(2)]
                po = [pst(es2, f"po{i}", [128, 512], F32) for i in range(4)]
                xov = xo.rearrange("(a p) n -> p a n", p=128)
                h1v6 = h1buf.rearrange("(a p) n -> p a n", p=128)
                P = Prog(nc, "outproj")
                st = P.stage()
                st['poolq'].append(lambda e: e.dma_start(out=wo[0][:].rearrange("p a b -> p (a b)"), in_=wout[0]))
                NU = 32
                for u in range(NU + 2):
                    st = P.stage()
                    if u < NU:
                        db, tp = u // 4, u % 4
                        if db + 1 < 8:
                            st['poolq'].append(lambda e, db=db, tp=tp: e.dma_start(
                                out=wo[(db + 1) % 2][:, tp * 8:(tp + 1) * 8, :].rearrange("p a b -> p (a b)"),
                                in_=wout[db + 1][:, tp * 4096:(tp + 1) * 4096]))
                        for a in range(2):
                            tcx = tp * 2 + a
                            for cc in range(KC):
                                st['pe'].append(lambda e, u=u, db=db, tcx=tcx, cc=cc, a=a: e.matmul(
                                    po[2 * (u % 2) + a][:], yT[:, cc, tcx * 128:(tcx + 1) * 128], wo[db % 2][:, cc, :],
                                    start=(cc == 0), stop=(cc == KC - 1)))
                        st['sp'].append(lambda e, u=u, db=db, tp=tp: e.dma_start(
                            out=xot[u % 2][:], in_=xov[:, tp * 2:tp * 2 + 2, db * 512:(db + 1) * 512]))
                    if 1 <= u <= NU:
                        u1 = u - 1
                        for a in range(2):
                            st['dve'].append(lambda e, u1=u1, a=a: e.tensor_tensor(
                                out=hst[u1 % 2][:, a, :], in0=po[2 * (u1 % 2) + a][:], in1=xot[u1 % 2][:, a, :], op=ALU.add))
                    if u >= 2:
                        u2 = u - 2
                        db2, tp2 = u2 // 4, u2 % 4
                        st['sp'].append(lambda e, u2=u2, db2=db2, tp2=tp2: e.dma_start(
                            out=h1v6[:, tp2 * 2:tp2 * 2 + 2, db2 * 512:(db2 + 1) * 512], in_=hst[u2 % 2][:]))
                P.emit()
        if debug == 'p6':
            return nc

        with ExitStack() as es:
            u2T = sbt(es, "u2T", [128, KC, TO], BF16)
            with ExitStack() as es1:
                norm_transpose("n2", es1, h1buf, g2bc, TO, u2T, ident)
            with ExitStack() as es2:
                wgb = [sbt(es2, f"wgb{i}", [128, KC, 128], BF16) for i in range(2)]
                wub = [sbt(es2, f"wub{i}", [128, KC, 128], BF16) for i in range(2)]
                sgs = [sbt(es2, f"sgs{i}", [128, TO], F32) for i in range(2)]
                ups = [sbt(es2, f"ups{i}", [128, TO], F32) for i in range(2)]
                hs = [sbt(es2, f"hs{i}", [128, TO], BF16) for i in range(2)]
                pg = [[pst(es2, f"pg{s}_{q}", [128, 512], F32) for q in range(4)] for s in range(2)]
                P = Prog(nc, "gateup")
                st = P.stage()
                st['poolq'].append(lambda e: e.dma_start(out=wgb[0][:].rearrange("p a b -> p (a b)"), in_=wg[0]))
                st['poolq'].append(lambda e: e.dma_start(out=wub[0][:].rearrange("p a b -> p (a b)"), in_=wu[0]))
                for f in range(NF + 3):
                    st = P.stage()
                    if f + 1 < NF:
                        st['poolq'].append(lambda e, f=f: e.dma_start(
                            out=wgb[(f + 1) % 2][:].rearrange("p a b -> p (a b)"), in_=wg[f + 1]))
                        st['poolq'].append(lambda e, f=f: e.dma_start(
                            out=wub[(f + 1) % 2][:].rearrange("p a b -> p (a b)"), in_=wu[f + 1]))
                    if f < NF:
                        for kc in range(KC):
                            for q in range(4):
                                wsrc = wgb if q < 2 else wub
                                st['pe'].append(lambda e, f=f, kc=kc, q=q, wsrc=wsrc: e.matmul(
                                    pg[f % 2][q][:], wsrc[f % 2][:, kc, :],
                                    u2T[:, kc, (q % 2) * 512:(q % 2 + 1) * 512],
                                    start=(kc == 0), stop=(kc == KC - 1)))
                    if 1 <= f <= NF:
                        f1 = f - 1
                        for hf in range(2):
                            st['act'].append(lambda e, f1=f1, hf=hf: e.activation(
                                out=sgs[f1 % 2][:, hf * 512:(hf + 1) * 512], in_=pg[f1 % 2][hf][:], func=AF.Silu))
                            st['dve'].append(lambda e, f1=f1, hf=hf: e.tensor_copy(
                                ups[f1 % 2][:, hf * 512:(hf + 1) * 512], pg[f1 % 2][2 + hf][:]))
                    if 2 <= f <= NF + 1:
                        f2 = f - 2
                        st['pool'].append(lambda e, f2=f2: e.tensor_tensor(out=hs[f2 % 2][:], in0=sgs[f2 % 2][:],
                                                                            in1=ups[f2 % 2][:], op=ALU.mult))
                    if 3 <= f:
                        f3 = f - 3
                        st['sp'].append(lambda e, f3=f3: e.dma_start(out=hT[f3 * 128:(f3 + 1) * 128, :],
                                                                      in_=hs[f3 % 2][:]))
                P.emit()
        if debug == 'p8':
            return nc

        with ExitStack() as es:
            hTs = sbt(es, "hTs", [128, NF, 512], BF16)
            GS = [(0, 22), (22, 44), (44, 65), (65, 86)]
            wdb = [sbt(es, f"wdb{i}", [128, 22, 512], BF16) for i in range(2)]
            h1p = [sbt(es, f"h1p{i}", [128, 4, 512], F32) for i in range(2)]
            h2s = [sbt(es, f"h2s{i}", [128, 4, 512], F32) for i in range(2)]
            pd = [[pst(es, f"pd{s}_{q}", [128, 512], F32) for q in range(4)] for s in range(2)]
            hTv = hT.rearrange("(f p) t -> p f t", p=128)
            h1v = h1buf.rearrange("(a p) n -> p a n", p=128)
            h2v = h2buf.rearrange("(a p) n -> p a n", p=128)
            P = Prog(nc, "down")
            units = [(th, db, g) for th in range(2) for db in range(8) for g in range(4)]

            def wload(st, ui):
                th, db, g = units[ui]
                f0, f1 = GS[g]
                st['poolq'].append(lambda e, ui=ui, db=db, f0=f0, f1=f1: e.dma_start(
                    out=wdb[ui % 2][:, 0:f1 - f0, :], in_=wdn[db, :, f0:f1, :]))
            st = P.stage()
            wload(st, 0)
            for ui in range(len(units) + 2):
                st = P.stage()
                if ui < len(units):
                    th, db, g = units[ui]
                    f0, f1 = GS[g]
                    if db == 0 and g == 0:
                        pass
                    if ui + 1 < len(units):
                        wload(st, ui + 1)
                    if g == 0:
                        st['sp'].append(lambda e, th=th, db=db: e.dma_start(
                            out=h1p[db % 2][:], in_=h1v[:, th * 4:(th + 1) * 4, db * 512:(db + 1) * 512]))
                    for ff in range(f0, f1):
                        for tq in range(4):
                            st['pe'].append(lambda e, ui=ui, db=db, ff=ff, f0=f0, tq=tq: e.matmul(
                                pd[db % 2][tq][:], hTs[:, ff, tq * 128:(tq + 1) * 128], wdb[ui % 2][:, ff - f0, :],
                                start=(ff == 0), stop=(ff == NF - 1)))
                if ui >= 1 and ui - 1 < len(units) and units[ui - 1][2] == 3:
                    th1, db1, _ = units[ui - 1]
                    for tq in range(4):
                        st['dve'].append(lambda e, db1=db1, tq=tq: e.tensor_tensor(
                            out=h2s[db1 % 2][:, tq, :], in0=pd[db1 % 2][tq][:], in1=h1p[db1 % 2][:, tq, :], op=ALU.add))
                if ui >= 2 and ui - 2 < len(units) and units[ui - 2][2] == 3:
                    th2, db2, _ = units[ui - 2]
                    st['sp'].append(lambda e, th2=th2, db2=db2: e.dma_start(
                        out=h2v[:, th2 * 4:(th2 + 1) * 4, db2 * 512:(db2 + 1) * 512], in_=h2s[db2 % 2][:]))
                if ui + 1 < len(units) and units[ui + 1][1] == 0 and units[ui + 1][2] == 0 and ui + 1 > 0:
                    pass
            stages = P.stages
            def hload_stage(th):
                stl = {e: [] for e in ENGS}
                for k in range(0, NF, 11):
                    k1 = min(NF, k + 11)
                    stl['sp'].append(lambda e, k=k, k1=k1, th=th: e.dma_start(
                        out=hTs[:, k:k1, :], in_=hTv[:, k:k1, th * 512:(th + 1) * 512]))
                return stl
            new = [stages[0], hload_stage(0)]
            for si in range(1, len(stages)):
                ui = si - 1
                if ui == 32:
                    new.append(hload_stage(1))
                new.append(stages[si])
            P.stages = new
            P.emit()
        if debug == 'p9':
            return nc

        with ExitStack() as es:
            gbc = sbt(es, "g3", [128, D], F32)
            xt = [sbt(es, f"fxt{i}", [128, D], F32) for i in range(2)]
            ot = [sbt(es, f"fot{i}", [128, D], F32) for i in range(2)]
            junk = sbt(es, "fjunk", [128, D], BF16)
            ss = sbt(es, "fss", [128, 8], F32)
            rstd = sbt(es, "frstd", [128, 8], F32)
            P = Prog(nc, "fin")
            st = P.stage()
            st['sp'].append(lambda e: e.dma_start(out=gbc[:], in_=g3bc))
            st['sp'].append(lambda e: e.dma_start(out=xt[0][:], in_=h2buf[0:128, :]))
            st['dve'].append(lambda e: e.memset(ss[:], 0.0))
            NTL = TO // 128
            for i in range(NTL):
                b = i % 2
                st = P.stage()
                st['act'].append(lambda e, b=b, i=i: e.activation(out=junk[:], in_=xt[b][:], func=AF.Square,
                                                                   accum_out=ss[:, i:i + 1]))
                if i + 1 < NTL:
                    st['sp'].append(lambda e, i=i: e.dma_start(out=xt[(i + 1) % 2][:],
                                                               in_=h2buf[(i + 1) * 128:(i + 2) * 128, :]))
                st = P.stage()
                st['act'].append(lambda e, i=i: e.activation(out=rstd[:, i:i + 1], in_=ss[:, i:i + 1], func=AF.Sqrt,
                                                              scale=1.0 / D, bias=EPS))
                st = P.stage()
                st['dve'].append(lambda e, i=i: e.reciprocal(rstd[:, i:i + 1], rstd[:, i:i + 1]))
                st = P.stage()
                st['dve'].append(lambda e, b=b, i=i: e.scalar_tensor_tensor(out=ot[b][:], in0=xt[b][:],
                                                                             scalar=rstd[:, i:i + 1], in1=gbc[:],
                                                                             op0=ALU.mult, op1=ALU.mult))
                st = P.stage()
                st['sp'].append(lambda e, b=b, i=i: e.dma_start(out=out[i * 128:(i + 1) * 128, :], in_=ot[b][:]))
            P.emit()
    return nc


def _tile_cols(w, cols_list):
    nch = len(cols_list)
    outw = np.zeros((nch, 128, KC, 128), np.float32)
    for j, cols in enumerate(cols_list):
        blk = w[:, cols]
        outw[j, :, :, :blk.shape[1]] = blk.reshape(KC, 128, -1).transpose(1, 0, 2)
    return outw.reshape(nch, 128, KC * 128)


def _prep(inp):
    f = lambda k: np.asarray(inp[k], dtype=np.float32)
    x = f("x")
    w_in = f("w_in")[0]
    mu = f("mu_shift")[0]
    R = 2048
    o3 = 3 * R
    g1bc = np.ascontiguousarray(np.broadcast_to(f("norm_mix_g")[0][None, :], (128, D)))
    g2bc = np.ascontiguousarray(np.broadcast_to(f("norm_ffn_g")[0][None, :], (128, D)))
    g3bc = np.ascontiguousarray(np.broadcast_to(f("norm_final_g")[None, :], (128, D)))
    w_out = f("w_out")[0]
    perm = np.concatenate([np.arange(0, 1024), np.arange(2048, 3072), np.arange(1024, 2048), np.arange(3072, 4096)])
    wop = w_out[perm]
    wout_t = np.ascontiguousarray(wop.reshape(KC, 128, 8, 512).transpose(2, 1, 0, 3)).reshape(8, 128, KC * 512)
    wgate = f("ffn_w_gate")[0]
    wup = f("ffn_w_up")[0]
    wdown = f("ffn_w_down")[0]
    wg_t = np.ascontiguousarray(wgate.reshape(KC, 128, NF, 128).transpose(2, 1, 0, 3)).reshape(NF, 128, KC * 128)
    wu_t = np.ascontiguousarray(wup.reshape(KC, 128, NF, 128).transpose(2, 1, 0, 3)).reshape(NF, 128, KC * 128)
    wdn_t = np.ascontiguousarray(wdown.reshape(NF, 128, 8, 512).transpose(2, 1, 0, 3))
    ii = np.arange(64)
    su = (ii[:, None] < ii[None, :]).astype(np.float32)
    iu = (ii[:, None] <= ii[None, :]).astype(np.float32)
    sl = (ii[:, None] > ii[None, :]).astype(np.float32)
    masks = np.stack([np.tile(m, (1, 8)) for m in (su, iu, sl)], axis=1).astype(np.float32)
    masks2 = np.ascontiguousarray(np.tile(np.concatenate([su, iu], axis=1), (1, 8)).astype(np.float32))
    bones = np.kron(np.eye(2, dtype=np.float32), np.ones((64, 64), np.float32))
    cmask = np.ones((128, 512), np.float32)
    cmask[:, ::64] = 0.0
    ngtab = np.ascontiguousarray(f("lru_norm_g")[0].reshape(16, 128).T)
    per_half = {}
    for hh in range(2):
        rsl = np.arange(hh * 1024, (hh + 1) * 1024)
        cols_list = []
        for base in (0, R, 2 * R):
            for j in range(8):
                cols_list.append(base + rsl[j * 128:(j + 1) * 128])
        cols_list.append(np.arange(o3, o3 + 96))
        cols_list.append(np.arange(o3 + 96, o3 + 192))
        cols_list.append(np.arange(o3 + 192, o3 + 320))
        cols_list.append(np.arange(o3 + 320, o3 + 448))
        lb = o3 + 448
        for base in (lb, lb + R):
            for j in range(8):
                cols_list.append(base + rsl[j * 128:(j + 1) * 128])
        win_t = _tile_cols(w_in, cols_list)
        ptabh = np.zeros((64, 192), np.float32)
        ptabl = np.zeros((128, 4), np.float32)
        ptabu = np.zeros((128, 72), np.float32)
        pv = {k: f(k)[0] for k in ("rwkv_w0", "rwkv_a0", "rwkv_k_k", "rwkv_k_a", "rwkv_ln_g", "rwkv_ln_b",
                                   "conv_b", "lru_br", "lru_bi", "lru_lambda")}
        rk = f("rwkv_r_k")[0].reshape(-1)
        cw = f("conv_w")[0]
        for h in range(16):
            ch = rsl[h * 64:(h + 1) * 64]
            b = h * 12
            ptabh[:, b + 0] = mu[ch]
            ptabh[:, b + 1] = mu[R + ch]
            ptabh[:, b + 2] = mu[2 * R + ch]
            ptabh[:, b + 3] = pv["rwkv_w0"][ch]
            ptabh[:, b + 4] = pv["rwkv_a0"][ch]
            ptabh[:, b + 5] = pv["rwkv_k_k"][ch]
            ptabh[:, b + 6] = pv["rwkv_k_a"][ch]
            ptabh[:, b + 8] = rk[ch]
            ptabh[:, b + 9] = pv["rwkv_ln_g"][ch]
            ptabh[:, b + 10] = pv["rwkv_ln_b"][ch]
        for j in range(8):
            ch = rsl[j * 128:(j + 1) * 128]
            b = j * 9
            for k in range(4):
                ptabu[:, b + k] = cw[k, ch]
            ptabu[:, b + 4] = pv["conv_b"][ch]
            ptabu[:, b + 5] = pv["lru_br"][ch]
            ptabu[:, b + 6] = pv["lru_bi"][ch]
            ptabu[:, b + 7] = pv["lru_lambda"][ch]
        ptabl[:96, 0] = mu[o3:o3 + 96]
        ptabl[:96, 1] = mu[o3 + 96:o3 + 192]
        ptabl[:, 2] = mu[o3 + 192:o3 + 320]
        ptabl[:, 3] = mu[o3 + 320:o3 + 448]
        per_half[hh] = dict(
            win=win_t, ptabh=ptabh, ptabl=ptabl, ptabu=ptabu,
            w2=np.ascontiguousarray(f("rwkv_w2")[0][:, rsl]),
            a2=np.ascontiguousarray(f("rwkv_a2")[0][:, rsl]),
            g2w=np.ascontiguousarray(f("rwkv_g2")[0][:, rsl].reshape(2, 128, 1024).transpose(1, 0, 2)),
            wr=np.ascontiguousarray(f("lru_wr")[0][hh * 8:(hh + 1) * 8].transpose(1, 0, 2)),
            wi=np.ascontiguousarray(f("lru_wi")[0][hh * 8:(hh + 1) * 8].transpose(1, 0, 2)),
        )
    in_maps = []
    for c in range(8):
        b, hh = c // 2, c % 2
        sel = np.zeros((128, 2), np.float32)
        sel[:, hh] = 1.0
        m = dict(xb=np.ascontiguousarray(x[b]), xo=np.ascontiguousarray(x[b, hh * TO:(hh + 1) * TO]),
                 g1bc=g1bc, g2bc=g2bc, g3bc=g3bc, ngtab=ngtab, masks=masks, masks2=masks2, bones=bones, cmask=cmask,
                 wout=wout_t, wg=wg_t, wu=wu_t, wdn=wdn_t, sel=sel)
        m.update(per_half[hh])
        in_maps.append(m)
    return in_maps


def kernel(**inp):
    in_maps = _prep(inp)
    nc = build_nc()
    res = run_bass_kernel_spmd(nc, in_maps, core_ids=list(range(8)))
    outp = np.zeros((4, T, D), np.float32)
    for c in range(8):
        b, hh = c // 2, c % 2
        outp[b, hh * TO:(hh + 1) * TO] = res.results[c]["out"]
    return outp
```

```python
import os as _os
import numpy as np
from contextlib import ExitStack
import concourse.bass as bass
import concourse.mybir as mybir
from concourse.bass_utils import run_bass_kernel_spmd

F32, BF16 = mybir.dt.float32, mybir.dt.bfloat16
AF = mybir.ActivationFunctionType
ALU = mybir.AluOpType

T = 2048
TO = 1024
D = 4096
KC = 32
NCH = 44
DFF = 11008
NF = 86
RW = 1024
CH = 64
NCK = T // CH
NPT = 176
EPS = 1e-6
GN_EPS = 64e-5

ENGS = ['pe', 'dve', 'act', 'pool', 'sp', 'poolq', 'actq']
HOST = {'pe': 'tensor', 'dve': 'vector', 'act': 'scalar', 'pool': 'gpsimd', 'sp': 'sync',
        'poolq': 'gpsimd', 'actq': 'scalar'}


class Prog:
    cnt = 0

    def __init__(self, nc, name):
        self.nc = nc
        self.name = name
        self.stages = []

    def stage(self):
        st = {e: [] for e in ENGS}
        self.stages.append(st)
        return st

    def emit(self):
        nc = self.nc
        with ExitStack() as es:
            Prog.cnt += 1
            sems = {e: es.enter_context(nc.semaphore(f"{self.name}{Prog.cnt}_{e}")) for e in ENGS}
            inc = {e: (16 if e in ('sp', 'poolq', 'actq') else 1) for e in ENGS}
            cum = {e: [] for e in ENGS}
            run = {e: 0 for e in ENGS}
            for st in self.stages:
                for e in ENGS:
                    if st[e]:
                        run[e] += inc[e] * (len(st[e]) if inc[e] == 16 else 1)
                    cum[e].append(run[e])
            stages = self.stages
            with nc.Block() as blk0:
                def clr(eng):
                    for e in ENGS:
                        eng.sem_clear(sems[e])
                blk0.gpsimd(clr)
            blk = es.enter_context(nc.Block())

            def make(hosteng):
                mine = [e for e in ENGS if HOST[e] == hosteng]

                def f(eng):
                    waited = {e: 0 for e in ENGS}
                    for k, st in enumerate(stages):
                        if not any(st[e] for e in mine):
                            continue
                        if k > 0:
                            for x in ENGS:
                                need = cum[x][k - 1]
                                if need > waited[x]:
                                    eng.wait_ge(sems[x], need)
                                    waited[x] = need
                        for e in mine:
                            ops = st[e]
                            for i, op in enumerate(ops):
                                ins = op(eng)
                                if inc[e] == 16:
                                    ins.then_inc(sems[e], 16)
                                elif i == len(ops) - 1:
                                    ins.then_inc(sems[e], 1)
                    for x in ENGS:
                        if run[x] > waited[x]:
                            eng.wait_ge(sems[x], run[x])
                return f
            blk.tensor(make('tensor'))
            blk.vector(make('vector'))
            blk.scalar(make('scalar'))
            blk.gpsimd(make('gpsimd'))
            blk.sync(make('sync'))


def build_nc(debug=None, ncores=8):
    early = debug in ('p1', 'p2', 'lora', 'prep', 'scan', 'post', 'rwkv', 'mix')
    nc = bass.Bass("TRN2", target_bir_lowering=False)

    MIXKEEP = ("xb", "g1bc", "win", "ptabh", "ptabl", "ptabu", "masks2", "w2", "a2", "g2w", "wr", "wi", "masks", "bones", "cmask")

    def din(name, shape, dt=F32):
        if early and name not in MIXKEEP:
            shape = [1, 1]
        return nc.dram_tensor(name, shape, dt, kind="ExternalInput").ap()

    xb = din("xb", [T, D])
    xo = din("xo", [TO, D])
    win = din("win", [NCH, 128, KC * 128])
    g1bc = din("g1bc", [128, D])
    g2bc = din("g2bc", [128, D])
    g3bc = din("g3bc", [128, D])
    ptabh_d = din("ptabh", [64, 192])
    ptabl_d = din("ptabl", [128, 4])
    ptabu_d = din("ptabu", [128, 72])
    ngtab_d = din("ngtab", [128, 16])
    w2_d = din("w2", [96, RW])
    a2_d = din("a2", [96, RW])
    g2w_d = din("g2w", [128, 2, RW])
    wr_d = din("wr", [128, 8, 128])
    wi_d = din("wi", [128, 8, 128])
    masks_d = din("masks", [64, 3, 512])
    masks2_d = din("masks2", [64, 1024])
    bones_d = din("bones", [128, 128])
    cmask_d = din("cmask", [128, 512])
    wout = din("wout", [8, 128, KC * 512])
    wg = din("wg", [NF, 128, KC * 128])
    wu = din("wu", [NF, 128, KC * 128])
    wdn = din("wdn", [8, 128, NF, 512])
    sel_d = din("sel", [128, 2])
    out = nc.dram_tensor("out", [TO, D], F32, kind="ExternalOutput").ap()

    def dscr(name, shape, dt):
        return nc.dram_tensor(name, shape, dt).ap()

    if not early:
        pT = dscr("pT", [NCH * 128, T], F32)
    gT = dscr("gT", [RW, T], F32)
    bvT = dscr("bvT", [RW, T], F32)
    if early:
        ybuf = nc.dram_tensor("ybuf", [2048, T], BF16, kind="ExternalOutput").ap()
        pT = nc.dram_tensor("pT", [NCH * 128, T], F32, kind="ExternalOutput").ap()
    else:
        ybuf = dscr("ybuf", [2048, T], BF16)
    gbufs = [dscr(f"gbuf{k}", [1024, T], BF16) for k in range(4)]
    h1buf = dscr("h1buf", [TO, D], F32)
    hT = dscr("hT", [DFF, TO], BF16)
    h2buf = dscr("h2buf", [TO, D], F32)

    uid = [0]

    def sbt(es, name, shape, dt):
        uid[0] += 1
        return es.enter_context(nc.sbuf_tensor(f"s{uid[0]}_{name}", shape, dt))

    def pst(es, name, shape, dt):
        uid[0] += 1
        return es.enter_context(nc.psum_tensor(f"p{uid[0]}_{name}", shape, dt))

    def norm_transpose(name, es, src, gbc_d, ntok, dstT, ident):
        ntile = ntok // 128
        gbc = sbt(es, name + "gbc", [128, D], F32)
        xt = [sbt(es, f"{name}xt{i}", [128, D], F32) for i in range(2)]
        xn = [sbt(es, f"{name}xn{i}", [128, D], BF16) for i in range(2)]
        junk = sbt(es, name + "junk", [128, D], BF16)
        ss = sbt(es, name + "ss", [128, 2], F32)
        rstd = sbt(es, name + "rstd", [128, 2], F32)
        ptr = [pst(es, f"{name}ptr{i}", [128, 1024], BF16) for i in range(4)]
        P = Prog(nc, name)
        st = P.stage()
        st['sp'].append(lambda e: e.dma_start(out=gbc[:], in_=gbc_d))
        st['sp'].append(lambda e: e.dma_start(out=xt[0][:], in_=src[0:128, :]))
        st['dve'].append(lambda e: e.memset(ss[:], 0.0))
        for i in range(ntile + 1):
            b = i % 2
            pb = (i - 1) % 2
            s1 = P.stage()
            if i < ntile:
                s1['act'].append(lambda e, b=b: e.activation(out=junk[:], in_=xt[b][:], func=AF.Square,
                                                               accum_out=ss[:, b:b + 1]))
            if i + 1 < ntile:
                s1['sp'].append(lambda e, i=i: e.dma_start(out=xt[(i + 1) % 2][:],
                                                           in_=src[(i + 1) * 128:(i + 2) * 128, :]))
            if i >= 1:
                for kc in range(KC):
                    s1['pe'].append(lambda e, kc=kc, pb=pb: e.transpose(
                        ptr[kc // 8][:, (kc % 8) * 128:(kc % 8 + 1) * 128],
                        xn[pb][:, kc * 128:(kc + 1) * 128], ident[:]))
            s2 = P.stage()
            if i < ntile:
                s2['act'].append(lambda e, b=b: e.activation(out=rstd[:, b:b + 1], in_=ss[:, b:b + 1], func=AF.Sqrt,
                                                              scale=1.0 / D, bias=EPS))
            if i >= 1:
                t0 = (i - 1) * 128
                for q in range(4):
                    eng = 'act' if q % 2 == 0 else 'dve'
                    o = dstT[:, q * 8:(q + 1) * 8, t0:t0 + 128]
                    src_ps = ptr[q][:].rearrange("p (a b) -> p a b", a=8)
                    if eng == 'act':
                        s2['act'].append(lambda e, o=o, s=src_ps: e.copy(o, s))
                    else:
                        s2['dve'].append(lambda e, o=o, s=src_ps: e.tensor_copy(o, s))
            if i < ntile:
                s3 = P.stage()
                s3['dve'].append(lambda e, b=b: e.reciprocal(rstd[:, b:b + 1], rstd[:, b:b + 1]))
                s3['pool'].append(lambda e, b=b: e.memset(ss[:, b:b + 1], 0.0))
                s4 = P.stage()
                s4['dve'].append(lambda e, b=b: e.scalar_tensor_tensor(out=xn[b][:], in0=xt[b][:],
                                                                        scalar=rstd[:, b:b + 1], in1=gbc[:],
                                                                        op0=ALU.mult, op1=ALU.mult))
        P.emit()

    with ExitStack() as top:
        ident = sbt(top, "ident", [128, 128], BF16)
        P = Prog(nc, "init")
        st = P.stage()
        st['pool'].append(lambda e: e.memset(ident[:], 0.0))
        st = P.stage()
        st['pool'].append(lambda e: e.affine_select(out=ident[:], in_=ident[:], pattern=[[-1, 128]],
                                                    compare_op=ALU.not_equal, fill=1.0, base=0,
                                                    channel_multiplier=1))
        P.emit()

        with ExitStack() as es:
            uT = sbt(es, "uT", [128, KC, T], BF16)
            with ExitStack() as es1:
                norm_transpose("n1", es1, xb, g1bc, T, uT, ident)
            if debug == 'p1':
                return nc
            with ExitStack() as es2:
                wb = [sbt(es2, f"wb{i}", [128, KC, 128], BF16) for i in range(2)]
                ost = [sbt(es2, f"ost{i}", [128, T], F32) for i in range(2)]
                pp = [[pst(es2, f"pp{s}_{q}", [128, 512], F32) for q in range(4)] for s in range(2)]
                P = Prog(nc, "inproj")
                st = P.stage()
                st['poolq'].append(lambda e: e.dma_start(out=wb[0][:].rearrange("p a b -> p (a b)"), in_=win[0]))
                for s in range(NCH + 2):
                    st = P.stage()
                    if s + 1 < NCH:
                        st['poolq'].append(lambda e, s=s: e.dma_start(
                            out=wb[(s + 1) % 2][:].rearrange("p a b -> p (a b)"), in_=win[s + 1]))
                    if s < NCH:
                        for kc in range(KC):
                            for q in range(4):
                                st['pe'].append(lambda e, s=s, kc=kc, q=q: e.matmul(
                                    pp[s % 2][q][:], wb[s % 2][:, kc, :], uT[:, kc, q * 512:(q + 1) * 512],
                                    start=(kc == 0), stop=(kc == KC - 1)))
                    if 1 <= s <= NCH:
                        j = s - 1
                        for q in range(4):
                            o = ost[j % 2][:, q * 512:(q + 1) * 512]
                            if q % 2 == 0:
                                st['act'].append(lambda e, o=o, j=j, q=q: e.copy(o, pp[j % 2][q][:]))
                            else:
                                st['dve'].append(lambda e, o=o, j=j, q=q: e.tensor_copy(o, pp[j % 2][q][:]))
                    if 2 <= s:
                        j = s - 2
                        st['sp'].append(lambda e, j=j: e.dma_start(out=pT[j * 128:(j + 1) * 128, :],
                                                                    in_=ost[j % 2][:]))
                P.emit()
            if debug == 'p2':
                return nc

        with ExitStack() as es:
            ptab = sbt(es, "ptabh", [64, 192], F32)
            ptl = sbt(es, "ptabl", [128, 4], F32)
            omk = sbt(es, "omk", [64, 16], F32)
            g8 = sbt(es, "g8", [64, 16], F32)
            lora = sbt(es, "lora", [128, 4, T], BF16)
            w2b = sbt(es, "w2b", [96, RW], BF16)
            a2b = sbt(es, "a2b", [96, RW], BF16)
            g2b = sbt(es, "g2b", [128, 2, RW], BF16)
            masks = sbt(es, "masks", [64, 3, 512], F32)
            masks2 = sbt(es, "masks2", [64, 1024], F32)
            ones64 = sbt(es, "ones64", [64, 64], F32)
            cmask = sbt(es, "cmask", [64, 512], F32)
            HT = 1024
            ARs = sbt(es, "ARs", [64, 8, 2 * HT], BF16)
            Bs = sbt(es, "Bs", [64, 8, HT], BF16)
            Ks = sbt(es, "Ks", [64, 8, HT], BF16)
            Vs = sbt(es, "Vs", [64, 8, HT], BF16)
            WC = sbt(es, "WC", [64, 8, 16], F32)
            obfs = [sbt(es, f"obf{i}", [64, 512], BF16) for i in range(2)]
            STs = [sbt(es, f"ST{g}", [64, 8, 64], F32) for g in range(2)]
            STbs = [sbt(es, f"STb{g}", [64, 8, 64], BF16) for g in range(2)]
            id64 = ident[0:64, 0:64]

            def merge_units(P, fns):
                subs = []
                for fn in fns:
                    Pu = Prog(nc, "sub")
                    fn(Pu)
                    subs.append(Pu.stages)
                n = len(subs[0])
                assert all(len(x) == n for x in subs)
                for k in range(n):
                    stg = P.stage()
                    for x in subs:
                        for e in ENGS:
                            stg[e].extend(x[k][e])

            def M3(i):
                return masks[:, i, :].rearrange("p (a b) -> p a b", a=8)

            def v8(t):
                return t[0:64, :].rearrange("p (a b) -> p a b", a=8)

            et = ExitStack()
            raw = sbt(et, "raw", [128, 4, 513], F32)
            dd = sbt(et, "dd", [128, 4, 512], F32)
            sh = sbt(et, "sh", [128, 4, 512], F32)
            P = Prog(nc, "rw0")
            st = P.stage()
            for (dst, srcd) in ((ptab, ptabh_d), (ptl, ptabl_d), (masks, masks_d), (masks2, masks2_d), (cmask, cmask_d[0:64, :])):
                st['sp'].append(lambda e, dst=dst, srcd=srcd: e.dma_start(out=dst[:], in_=srcd))
            for (dst, srcd) in ((w2b, w2_d), (a2b, a2_d), (g2b, g2w_d)):
                st['poolq'].append(lambda e, dst=dst, srcd=srcd: e.dma_start(out=dst[:], in_=srcd))
            st['pool'].append(lambda e: e.memset(ones64[:], 1.0))
            for g in range(2):
                st['dve'].append(lambda e, g=g: e.memset(STs[g][:], 0.0))
                st['pool'].append(lambda e, g=g: e.memset(STbs[g][:], 0.0))
            st = P.stage()
            kav = ptab[:, :].rearrange("p (h k) -> p h k", k=12)[:, :, 6]
            lgv = ptab[:, :].rearrange("p (h k) -> p h k", k=12)[:, :, 9]
            st['dve'].append(lambda e: e.tensor_scalar(omk[:], kav, -1.0, 1.0, ALU.mult, ALU.add))
            st['dve'].append(lambda e: e.tensor_scalar(g8[:], lgv, 8.0, None, ALU.mult))
            for tb in range(4):
                c0 = tb * 512
                st = P.stage()
                for q in range(4):
                    rows = slice((24 + q) * 128, (25 + q) * 128)
                    if tb == 0:
                        st['sp'].append(lambda e, q=q, rows=rows: e.dma_start(out=raw[:, q, 1:513],
                                                                               in_=pT[rows, 0:512]))
                    else:
                        st['sp'].append(lambda e, q=q, rows=rows, c0=c0: e.dma_start(
                            out=raw[:, q, 0:513], in_=pT[rows, c0 - 1:c0 + 512]))
                if tb == 0:
                    st['pool'].append(lambda e: e.memset(raw[:, :, 0:1], 0.0))
                st = P.stage()
                st['dve'].append(lambda e: e.tensor_tensor(out=dd[:], in0=raw[:, :, 0:512], in1=raw[:, :, 1:513],
                                                            op=ALU.subtract))
                st = P.stage()
                for q in range(4):
                    st['dve'].append(lambda e, q=q: e.scalar_tensor_tensor(
                        out=sh[:, q, :], in0=dd[:, q, :], scalar=ptl[:, q:q + 1], in1=raw[:, q, 1:513],
                        op0=ALU.mult, op1=ALU.add))
                st = P.stage()
                for q in range(4):
                    fn = [AF.Tanh, AF.Copy, AF.Sigmoid, AF.Sigmoid][q]
                    st['act'].append(lambda e, q=q, fn=fn, c0=c0: e.activation(out=lora[:, q, c0:c0 + 512],
                                                                               in_=sh[:, q, :], func=fn))
            P.emit()
            et.close()
            if debug == 'lora':
                return nc

            for gi in range(2):
              for half in range(2):
                ST, STb = STs[gi], STbs[gi]
                et = ExitStack()
                raws = [sbt(et, f"raw{i}", [64, 3, 513], F32) for i in range(2)]
                shs = [sbt(et, f"sh{i}", [64, 3, 512], F32) for i in range(2)]
                tmpbs = [sbt(et, f"tmpb{i}", [64, 14, 512], F32) for i in range(2)]
                pps = [[pst(et, f"pp{i}_{q}", [128, 512], F32) for q in range(4)] for i in range(2)]
                P = Prog(nc, f"rwp{gi}{half}")
                for hl2 in range(4):
                  for tbl in range(2):
                    fns = []
                    for slot in range(2):
                      def unit(P, slot=slot, hl2=hl2, tbl=tbl):
                        hl = hl2 * 2 + slot
                        h = gi * 8 + hl
                        pc = h * 12
                        col = lambda k, pc=pc: ptab[:, pc + k:pc + k + 1]
                        cs = slice(h * 64, (h + 1) * 64)
                        raw, sh, tmpb = raws[slot], shs[slot], tmpbs[slot]
                        pA, pB, pC, pD = pps[slot]
                        dd = tmpb[:, 11:14, :]
                        tmp = [tmpb[:, i, :] for i in range(14)]
                        if True:
                            tb = half * 2 + tbl
                            c0 = tb * 512
                            l0 = tbl * 512
                            (kk, kk2, sg, aicl, gst, rinv, logw, t1, cum, kkn, k2, Wt, Winv, lp) = tmp
                            st = P.stage()
                        for q in range(3):
                            rows = slice(q * 1024 + h * 64, q * 1024 + (h + 1) * 64)
                            if tb == 0:
                                st['sp'].append(lambda e, q=q, rows=rows: e.dma_start(out=raw[:, q, 1:513],
                                                                                       in_=pT[rows, 0:512]))
                            else:
                                st['sp'].append(lambda e, q=q, rows=rows, c0=c0: e.dma_start(
                                    out=raw[:, q, 0:513], in_=pT[rows, c0 - 1:c0 + 512]))
                        if tb == 0:
                            st['pool'].append(lambda e: e.memset(raw[:, 0:3, 0:1], 0.0))
                        st = P.stage()
                        st['dve'].append(lambda e: e.tensor_tensor(out=dd[:], in0=raw[:, :, 0:512],
                                                                    in1=raw[:, :, 1:513], op=ALU.subtract))
                        for q in range(3):
                            st['dve'].append(lambda e, q=q, col=col: e.scalar_tensor_tensor(
                                out=sh[:, q, :], in0=dd[:, q, :], scalar=col(q), in1=raw[:, q, 1:513],
                                op0=ALU.mult, op1=ALU.add))
                        rs, ks, vs = sh[:, 0, :], sh[:, 1, :], sh[:, 2, :]
                        A3 = lambda ap: ap.rearrange("p (c n) -> p c n", n=64)
                        ARv = ARs[:, hl, tbl * 1024:(tbl + 1) * 1024].rearrange("p (c two n) -> p c two n", two=2, n=64)
                        st = P.stage()
                        st['pe'].append(lambda e, cs=cs, c0=c0: e.matmul(pA[0:64, :], w2b[0:96, cs], lora[0:96, 0, c0:c0 + 512],
                                                                           start=True, stop=True))
                        st['pe'].append(lambda e, cs=cs, c0=c0: e.matmul(pB[0:64, :], a2b[0:96, cs], lora[0:96, 1, c0:c0 + 512],
                                                                           start=True, stop=True))
                        st['pe'].append(lambda e, cs=cs, c0=c0: e.matmul(pC[0:64, :], g2b[:, 0, cs], lora[:, 2, c0:c0 + 512],
                                                                           start=True, stop=False))
                        st['pe'].append(lambda e, cs=cs, c0=c0: e.matmul(pC[0:64, :], g2b[:, 1, cs], lora[:, 3, c0:c0 + 512],
                                                                           start=False, stop=True))
                        st['act'].append(lambda e, col=col, ks=ks: e.activation(out=kk2[:], in_=ks, func=AF.Square,
                                                                                 scale=col(5)))
                        st['act'].append(lambda e, col=col, ks=ks: e.activation(out=kk[:], in_=ks, func=AF.Copy,
                                                                                 scale=col(5)))
                        st = P.stage()
                        st['pe'].append(lambda e: e.matmul(pD[0:64, :], ones64[:], kk2[:], start=True, stop=True))
                        st['act'].append(lambda e, col=col: e.activation(out=sg[:], in_=pA[0:64, :], func=AF.Sigmoid,
                                                                          bias=col(3)))
                        st['act'].append(lambda e, col=col: e.activation(out=aicl[:], in_=pB[0:64, :], func=AF.Sigmoid,
                                                                          bias=col(4)))
                        st['dve'].append(lambda e: e.tensor_copy(gst[:], pC[0:64, :]))
                        st = P.stage()
                        st['act'].append(lambda e: e.activation(out=rinv[:], in_=pD[0:64, :], func=AF.Ln, bias=1e-24))
                        st['act'].append(lambda e: e.activation(out=rinv[:], in_=rinv[:], func=AF.Exp, scale=-0.5))
                        st['act'].append(lambda e: e.mul(logw[:], sg[:], -0.6065306597126334))
                        st['dve'].append(lambda e, col=col, h=h: e.tensor_scalar(t1[:], aicl[:], col(6), omk[:, h:h + 1],
                                                                                  ALU.mult, ALU.add))
                        st['dve'].append(lambda e, ks=ks: e.tensor_tensor(out=k2[:], in0=ks, in1=t1[:], op=ALU.mult))
                        st['dve'].append(lambda e, rs=rs: e.tensor_tensor(out=kk2[:], in0=rs, in1=k2[:], op=ALU.mult))
                        st['dve'].append(lambda e, col=col: e.tensor_scalar(t1[:], kk2[:], col(8), None, ALU.mult))
                        st['sp'].append(lambda e, cs=cs, c0=c0: e.dma_start(out=gT[cs, c0:c0 + 512], in_=gst[:]))
                        st = P.stage()
                        st['dve'].append(lambda e: e.tensor_tensor_scan(out=cum[:], data0=cmask[:], data1=logw[:],
                                                                         initial=0.0, op0=ALU.mult, op1=ALU.add))
                        st['dve'].append(lambda e: e.tensor_tensor(out=lp[:], in0=cum[:], in1=logw[:], op=ALU.subtract))
                        st['dve'].append(lambda e: e.tensor_tensor(out=kkn[:], in0=kk[:], in1=rinv[:], op=ALU.mult))
                        st['dve'].append(lambda e: e.tensor_tensor(out=kk[:], in0=kkn[:], in1=aicl[:], op=ALU.mult))
                        st['pe'].append(lambda e: e.matmul(pA[0:64, :], ones64[:], t1[:], start=True, stop=True))
                        st['act'].append(lambda e, hl=hl, l0=l0, vs=vs: e.copy(Vs[:, hl, l0:l0 + 512], vs))
                        st = P.stage()
                        st['act'].append(lambda e: e.activation(out=Wt[:], in_=cum[:], func=AF.Exp))
                        st['act'].append(lambda e: e.activation(out=Winv[:], in_=cum[:], func=AF.Exp, scale=-1.0))
                        st['act'].append(lambda e: e.activation(out=sg[:], in_=lp[:], func=AF.Exp))
                        st['dve'].append(lambda e, vs=vs: e.tensor_tensor(out=gst[:], in0=pA[0:64, :], in1=vs, op=ALU.mult))
                        st = P.stage()
                        st['dve'].append(lambda e, rs=rs, ARv=ARv: e.tensor_tensor(out=ARv[:, :, 1, :], in0=A3(rs), in1=A3(Wt),
                                                                                    op=ALU.mult))
                        st['dve'].append(lambda e, ARv=ARv: e.scalar_tensor_tensor(out=ARv[:, :, 0, :], in0=A3(kkn), scalar=-1.0,
                                                                                    in1=A3(sg), op0=ALU.mult, op1=ALU.mult))
                        st['dve'].append(lambda e, hl=hl, l0=l0: e.tensor_tensor(out=Ks[:, hl, l0:l0 + 512], in0=k2[:],
                                                                                  in1=Winv[:], op=ALU.mult))
                        st['dve'].append(lambda e, hl=hl, l0=l0: e.tensor_tensor(out=Bs[:, hl, l0:l0 + 512], in0=kk[:],
                                                                                  in1=Winv[:], op=ALU.mult))
                        st['dve'].append(lambda e, hl=hl, tbl=tbl: e.tensor_copy(
                            WC[:, hl, tbl * 8:(tbl + 1) * 8], A3(Wt)[:, :, 63]))
                        st['sp'].append(lambda e, cs=cs, c0=c0: e.dma_start(out=bvT[cs, c0:c0 + 512], in_=gst[:]))
                      fns.append(unit)
                    merge_units(P, fns)
                P.emit()
                et.close()
                if debug == 'prep':
                    return nc

                et = ExitStack()
                ybig = sbt(et, "ybig", [64, 8, HT], F32)
                ptr1 = pst(et, "ptr1", [128, 1024], BF16)
                ptr2 = pst(et, "ptr2", [128, 1024], BF16)
                pABs = [pst(et, f"pAB{i}", [128, 512], F32) for i in range(2)]
                pKRs = [pst(et, f"pKR{i}", [128, 512], F32) for i in range(2)]

                def hb2(ts, h):
                    return ts[h // 4][0:64, (h % 4) * 128:(h % 4 + 1) * 128]
                pT_ = pst(et, "pT_", [128, 512], F32)
                pF = pst(et, "pF", [128, 512], F32)
                et_s = ExitStack()
                tmpS = sbt(et_s, "tmpS", [64, 8, 64], F32)
                tokms = [sbt(et_s, f"tokm{i}", [64, 3, 8, 64], BF16) for i in range(2)]
                ABm = sbt(et_s, "ABm", [64, 8, 128], BF16)
                AKm = sbt(et_s, "AKm", [64, 8, 128], BF16)
                Pp = [sbt(et_s, f"Pp{i}", [64, 8, 64], BF16) for i in range(2)]
                ZQ = [sbt(et_s, f"ZQ{i}", [64, 8, 128], BF16) for i in range(2)]
                Zq = [sbt(et_s, f"Zq{i}", [64, 8, 64], F32) for i in range(2)]
                UTb = sbt(et_s, "UTb", [64, 8, 64], BF16)
                m2v = masks2[:, :].rearrange("p (a b) -> p a b", a=8)

                def v16(t):
                    return t[0:64, :].rearrange("p (a b) -> p a b", a=8)
                P = Prog(nc, f"rws{gi}{half}")
                for c in range(HT // CH):
                    tc = slice(c * 64, (c + 1) * 64)
                    ac = slice(c * 128, c * 128 + 64)
                    rc = slice(c * 128 + 64, c * 128 + 128)
                    arc = slice(c * 128, (c + 1) * 128)
                    tokm = tokms[c % 2]

                    def emit_T(stg, cn):
                        tcn = slice(cn * 64, (cn + 1) * 64)
                        for h in range(8):
                            hc = slice(h * 64, (h + 1) * 64)
                            stg['pe'].append(lambda e, h=h, hc=hc, tcn=tcn: e.transpose(ptr1[0:64, hc], Bs[:, h, tcn], id64))
                            stg['pe'].append(lambda e, h=h, hc=hc, tcn=tcn: e.transpose(
                                ptr1[0:64, 512 + h * 64:512 + (h + 1) * 64], Ks[:, h, tcn], id64))
                            stg['pe'].append(lambda e, h=h, hc=hc, tcn=tcn: e.transpose(ptr2[0:64, hc], Vs[:, h, tcn], id64))

                    def emit_Tevac(stg, cn):
                        tk = tokms[cn % 2]
                        stg['act'].append(lambda e, tk=tk: e.copy(tk[:, 0:2, :, :].rearrange("p a b c -> p (a b c)"), ptr1[0:64, :]))
                        stg['act'].append(lambda e, tk=tk: e.copy(tk[:, 2, :, :].rearrange("p b c -> p (b c)"), ptr2[0:64, 0:512]))
                    if c == 0:
                        st = P.stage()
                        emit_T(st, 0)
                        st = P.stage()
                        emit_Tevac(st, 0)
                    st = P.stage()
                    for h in range(8):
                        hc = slice(h * 64, (h + 1) * 64)
                        hc2 = slice(h * 128, (h + 1) * 128)
                        Bt, Kt = Bs[:, h, tc], Ks[:, h, tc]
                        At, ARc = ARs[:, h, ac], ARs[:, h, arc]
                        st['pe'].append(lambda e, h=h, Bt=Bt, ARc=ARc: e.matmul(hb2(pABs, h), Bt, ARc, start=True, stop=True))
                        st['pe'].append(lambda e, h=h, Kt=Kt, ARc=ARc: e.matmul(hb2(pKRs, h), Kt, ARc, start=True, stop=True))
                        st['pe'].append(lambda e, hc=hc, At=At, Bt=Bt: e.matmul(pT_[0:64, hc], At, Bt, start=True, stop=True))
                    st = P.stage()
                    for hb in range(2):
                        hs4 = slice(hb * 4, hb * 4 + 4)
                        st['dve'].append(lambda e, hs4=hs4, hb=hb: e.tensor_tensor(
                            out=AKm[:, hs4, :], in0=pKRs[hb][0:64, :].rearrange("p (a b) -> p a b", a=4), in1=m2v[:, hs4, :], op=ALU.mult))
                        st['dve'].append(lambda e, hs4=hs4, hb=hb: e.tensor_tensor(
                            out=ABm[:, hs4, :], in0=pABs[hb][0:64, :].rearrange("p (a b) -> p a b", a=4), in1=m2v[:, hs4, :], op=ALU.mult))
                    st['dve'].append(lambda e: e.tensor_tensor(out=ZQ[0][:, :, 64:128], in0=v8(pT_), in1=M3(2), op=ALU.mult))
                    st = P.stage()
                    for h in range(8):
                        hc = slice(h * 64, (h + 1) * 64)
                        At = ARs[:, h, ac]
                        VTh = tokm[:, 2, h, :]
                        st['pe'].append(lambda e, At=At, h=h, hc=hc: e.matmul(
                            pF[0:64, hc], At, STb[:, h, :], start=True, stop=False))
                        st['pe'].append(lambda e, h=h, VTh=VTh, hc=hc: e.matmul(
                            pF[0:64, hc], AKm[:, h, 0:64], VTh, start=False, stop=True))
                    st = P.stage()
                    st['act'].append(lambda e: e.copy(Zq[0][:], v8(pF)))
                    st['act'].append(lambda e: e.copy(ZQ[0][:, :, 0:64], v8(pF)))
                    for j in range(1, 7):
                        st = P.stage()
                        if j == 1 and c + 1 < HT // CH:
                            emit_T(st, c + 1)
                        zi, zo = Zq[(j - 1) % 2], Zq[j % 2]
                        zqi, zqo = ZQ[(j - 1) % 2], ZQ[j % 2]
                        for h in range(8):
                            hc = slice(h * 64, (h + 1) * 64)
                            hc2 = slice(h * 128, (h + 1) * 128)
                            Pj = ABm[:, h, 0:64] if j == 1 else Pp[j % 2][:, h, :]
                            if j < 6:
                                st['pe'].append(lambda e, Pj=Pj, zqi=zqi, h=h: e.matmul(
                                    hb2(pKRs, h), Pj, zqi[:, h, :], start=True, stop=True))
                                st['pe'].append(lambda e, hc=hc, Pj=Pj, zqi=zqi, h=h: e.matmul(
                                    pT_[0:64, hc], zqi[:, h, 64:128], Pj, start=True, stop=True))
                            else:
                                st['pe'].append(lambda e, Pj=Pj, zqi=zqi, h=h: e.matmul(
                                    hb2(pKRs, h)[:, 0:64], Pj, zqi[:, h, 0:64], start=True, stop=True))
                        st = P.stage()
                        for hb in range(2):
                            hs4 = slice(hb * 4, hb * 4 + 4)
                            pk = pKRs[hb][0:64, :].rearrange("p (a b) -> p a b", a=4)
                            zp = pk[:, :, 0:64]
                            if j < 6:
                                st['dve'].append(lambda e, zi=zi, zo=zo, zp=zp, hs4=hs4: e.tensor_tensor(
                                    out=zo[:, hs4, :], in0=zp, in1=zi[:, hs4, :], op=ALU.add))
                                st['dve'].append(lambda e, zi=zi, zqo=zqo, zp=zp, hs4=hs4: e.tensor_tensor(
                                    out=zqo[:, hs4, 0:64], in0=zp, in1=zi[:, hs4, :], op=ALU.add))
                                st['dve'].append(lambda e, zqo=zqo, pk=pk, hs4=hs4: e.tensor_copy(zqo[:, hs4, 64:128], pk[:, :, 64:128]))
                            else:
                                st['dve'].append(lambda e, zi=zi, zp=zp, hs4=hs4: e.tensor_tensor(
                                    out=UTb[:, hs4, :], in0=zp, in1=zi[:, hs4, :], op=ALU.add))
                        if j < 6:
                            st['act'].append(lambda e, j=j: e.copy(Pp[(j + 1) % 2][:], v8(pT_)))
                        if j == 2 and c + 1 < HT // CH:
                            emit_Tevac(st, c + 1)
                    st = P.stage()
                    for h in range(8):
                        hc = slice(h * 64, (h + 1) * 64)
                        Rt = ARs[:, h, rc]
                        VTh = tokm[:, 2, h, :]
                        BTh = tokm[:, 0, h, :]
                        KTh = tokm[:, 1, h, :]
                        oy = pABs[0][0:64, hc]
                        os_ = pABs[1][0:64, hc]
                        st['pe'].append(lambda e, oy=oy, h=h, Rt=Rt: e.matmul(oy, STb[:, h, :], Rt,
                                                                               start=True, stop=False))
                        st['pe'].append(lambda e, oy=oy, h=h: e.matmul(oy, UTb[:, h, :], ABm[:, h, 64:128],
                                                                        start=False, stop=False))
                        st['pe'].append(lambda e, oy=oy, h=h, VTh=VTh: e.matmul(oy, VTh, AKm[:, h, 64:128],
                                                                                 start=False, stop=True))
                        st['pe'].append(lambda e, os_=os_, h=h, BTh=BTh: e.matmul(os_, BTh, UTb[:, h, :],
                                                                                   start=True, stop=False))
                        st['pe'].append(lambda e, os_=os_, KTh=KTh, VTh=VTh: e.matmul(os_, KTh, VTh,
                                                                                       start=False, stop=True))
                    st = P.stage()
                    st['act'].append(lambda e, tc=tc: e.copy(ybig[:, :, tc], v8(pABs[0])))
                    st['dve'].append(lambda e: e.tensor_tensor(out=tmpS[:], in0=v8(pABs[1]), in1=ST[:], op=ALU.add))
                    wcb = WC[:, :, c].unsqueeze(2).to_broadcast([64, 8, 64])
                    st['dve'].append(lambda e, wcb=wcb: e.tensor_tensor(out=ST[:], in0=tmpS[:], in1=wcb, op=ALU.mult))
                    st['dve'].append(lambda e, wcb=wcb: e.tensor_tensor(out=STb[:], in0=tmpS[:], in1=wcb, op=ALU.mult))
                if debug == 'scan' and _os.environ.get('SCAN_STOP'):
                    P.stages = P.stages[:int(_os.environ['SCAN_STOP'])]
                P.emit()
                et_s.close()
                if debug == 'scan':
                    et.close()
                    return nc

                et2 = ExitStack()
                tmps = [[sbt(et2, f"tmp{s}_{i}", [64, 512], F32) for i in range(5)] for s in range(4)]
                obfs4 = obfs + [sbt(et2, f"obfx{i}", [64, 512], BF16) for i in range(2)]
                pbank = [pABs[0], pABs[1], pKRs[0], pKRs[1]]
                P = Prog(nc, f"rwo{gi}{half}")
                for hl2 in range(2):
                  for tbl in range(2):
                    fns = []
                    for slot in range(4):
                      def unit(P, slot=slot, hl2=hl2, tbl=tbl):
                        hl = hl2 * 4 + slot
                        h = gi * 8 + hl
                        pc = h * 12
                        cs = slice(h * 64, (h + 1) * 64)
                        tmp = tmps[slot]
                        obf = obfs4[slot]
                        pA_ = pB_ = pbank[slot]
                        if True:
                            tb = half * 2 + tbl
                            c0 = tb * 512
                            l0 = tbl * 512
                            gl_, bvl, yc, sq, rs_ = tmp[0:5]
                            t_a, t_b, t_c = sq, yc, sq
                            st = P.stage()
                        st['sp'].append(lambda e, cs=cs, c0=c0: e.dma_start(out=gl_[:], in_=gT[cs, c0:c0 + 512]))
                        st['sp'].append(lambda e, cs=cs, c0=c0: e.dma_start(out=bvl[:], in_=bvT[cs, c0:c0 + 512]))
                        st['pe'].append(lambda e, hl=hl, l0=l0: e.matmul(pA_[0:64, :], ones64[:], ybig[:, hl, l0:l0 + 512],
                                                                          start=True, stop=True))
                        st = P.stage()
                        st['dve'].append(lambda e, hl=hl, l0=l0: e.scalar_tensor_tensor(
                            out=yc[:], in0=pA_[0:64, :], scalar=-1.0 / 64, in1=ybig[:, hl, l0:l0 + 512],
                            op0=ALU.mult, op1=ALU.add))
                        st = P.stage()
                        st['act'].append(lambda e: e.activation(out=sq[:], in_=yc[:], func=AF.Square))
                        st = P.stage()
                        st['pe'].append(lambda e: e.matmul(pB_[0:64, :], ones64[:], sq[:], start=True, stop=True))
                        st = P.stage()
                        st['act'].append(lambda e: e.activation(out=rs_[:], in_=pB_[0:64, :], func=AF.Ln, bias=64.0 * GN_EPS))
                        st['act'].append(lambda e: e.activation(out=rs_[:], in_=rs_[:], func=AF.Exp, scale=-0.5))
                        st = P.stage()
                        st['dve'].append(lambda e: e.tensor_tensor(out=t_a[:], in0=yc[:], in1=rs_[:], op=ALU.mult))
                        st['dve'].append(lambda e, h=h, pc=pc: e.tensor_scalar(t_b[:], t_a[:], g8[:, h:h + 1],
                                                                                 ptab[:, pc + 10:pc + 11],
                                                                                 ALU.mult, ALU.add))
                        st['dve'].append(lambda e: e.tensor_tensor(out=t_c[:], in0=t_b[:], in1=bvl[:], op=ALU.add))
                        st['dve'].append(lambda e: e.tensor_tensor(out=obf[:], in0=t_c[:], in1=gl_[:], op=ALU.mult))
                        st = P.stage()
                        st['sp'].append(lambda e, cs=cs, c0=c0: e.dma_start(out=ybuf[cs, c0:c0 + 512], in_=obf[:]))
                      fns.append(unit)
                    merge_units(P, fns)
                P.emit()
                et2.close()
                et.close()
                if debug == 'post':
                    return nc

        if debug == 'rwkv':
            return nc
        with ExitStack() as es:
            ptab = sbt(es, "ptabL", [128, 72], F32)
            cch = sbt(es, "cch", [128, 8], F32)
            c2h = sbt(es, "c2h", [128, 8], F32)
            etmp = sbt(es, "etmp", [128, 8], F32)
            wrb = sbt(es, "wrb", [128, 8, 128], BF16)
            wib = sbt(es, "wib", [128, 8, 128], BF16)
            L_xpad = [sbt(es, f"xpad{i}", [128, 515], F32) for i in range(4)]
            L_gat = [sbt(es, f"gat{i}", [128, 512], F32) for i in range(4)]
            L_tl = [[sbt(es, f"tl{s_}_{i}", [128, 512], F32) for i in range(12)] for s_ in range(4)]
            L_xcb = [sbt(es, f"xcb{i}", [128, 512], BF16) for i in range(4)]
            L_ybf = [sbt(es, f"ybf{i}", [128, 512], BF16) for i in range(4)]
            L_hcar = [sbt(es, f"hcar{i}", [128, 1], F32) for i in range(4)]
            L_p1 = [pst(es, f"lp1_{i}", [128, 512], F32) for i in range(4)]
            L_p2 = [pst(es, f"lp2_{i}", [128, 512], F32) for i in range(4)]
            hcar = L_hcar[0]

            def merge_units_l(P, fns):
                subs = []
                for fn in fns:
                    Pu = Prog(nc, "sub")
                    fn(Pu)
                    subs.append(Pu.stages)
                n = len(subs[0])
                assert all(len(x) == n for x in subs)
                for k in range(n):
                    stg = P.stage()
                    for x in subs:
                        for e in ENGS:
                            stg[e].extend(x[k][e])
            P = Prog(nc, "lru")
            st = P.stage()
            st['sp'].append(lambda e: e.dma_start(out=ptab[:], in_=ptabu_d))
            st['poolq'].append(lambda e: e.dma_start(out=wrb[:], in_=wr_d))
            st['poolq'].append(lambda e: e.dma_start(out=wib[:], in_=wi_d))
            lamv = ptab[:, 0:72].rearrange("p (j k) -> p j k", k=9)[:, :, 7]
            st = P.stage()
            st['act'].append(lambda e: e.activation(out=etmp[:], in_=lamv, func=AF.Exp, scale=-1.0))
            st = P.stage()
            st['act'].append(lambda e: e.activation(out=etmp[:], in_=etmp[:], func=AF.Ln, bias=1.0))
            st = P.stage()
            st['dve'].append(lambda e: e.tensor_scalar(cch[:], etmp[:], -8.0, None, ALU.mult))
            st['dve'].append(lambda e: e.tensor_scalar(c2h[:], etmp[:], -16.0, None, ALU.mult))
            for jj in range(2):
              for tb in range(4):
                fns = []
                for slot in range(4):
                  def unit(P, slot=slot, jj=jj, tb=tb):
                    j = jj * 4 + slot
                    pc = j * 9
                    col = lambda k, pc=pc: ptab[:, pc + k:pc + k + 1]
                    xpad, gat, tl, xcb, ybf, hcar = L_xpad[slot], L_gat[slot], L_tl[slot], L_xcb[slot], L_ybf[slot], L_hcar[slot]
                    p1, p2 = L_p1[slot], L_p2[slot]
                    c0 = tb * 512
                    (xc0, xc1, x2, inner, inner2, sgm, gel, rg, ig, av, a2v, gx) = tl
                    rx = slice((28 + j) * 128, (29 + j) * 128)
                    rg_ = slice((36 + j) * 128, (37 + j) * 128)
                    st = P.stage()
                    if tb == 0:
                        st['sp'].append(lambda e, rx=rx: e.dma_start(out=xpad[:, 3:515], in_=pT[rx, 0:512]))
                    else:
                        st['sp'].append(lambda e, rx=rx, c0=c0: e.dma_start(out=xpad[:], in_=pT[rx, c0 - 3:c0 + 512]))
                    st['sp'].append(lambda e, rg_=rg_, c0=c0: e.dma_start(out=gat[:], in_=pT[rg_, c0:c0 + 512]))
                    xc = xc1
                    st = P.stage()
                    if tb == 0:
                        st['dve'].append(lambda e: e.memset(xpad[:, 0:3], 0.0))
                    st['dve'].append(lambda e, col=col: e.tensor_scalar(xc0[:], xpad[:, 0:512], col(0), col(4),
                                                                         ALU.mult, ALU.add))
                    st['dve'].append(lambda e, col=col: e.scalar_tensor_tensor(out=xc1[:], in0=xpad[:, 1:513], scalar=col(1),
                                                                                in1=xc0[:], op0=ALU.mult, op1=ALU.add))
                    st['dve'].append(lambda e, col=col: e.scalar_tensor_tensor(out=xc0[:], in0=xpad[:, 2:514], scalar=col(2),
                                                                                in1=xc1[:], op0=ALU.mult, op1=ALU.add))
                    st['dve'].append(lambda e, col=col: e.scalar_tensor_tensor(out=xc1[:], in0=xpad[:, 3:515], scalar=col(3),
                                                                                in1=xc0[:], op0=ALU.mult, op1=ALU.add))
                    st['act'].append(lambda e: e.activation(out=x2[:], in_=gat[:], func=AF.Square))
                    st = P.stage()
                    st['act'].append(lambda e: e.copy(xcb[:], xc[:]))
                    st['dve'].append(lambda e: e.tensor_scalar(inner[:], x2[:], 0.044715, 1.0, ALU.mult, ALU.add))
                    st['dve'].append(lambda e: e.tensor_tensor(out=inner2[:], in0=inner[:], in1=gat[:], op=ALU.mult))
                    st = P.stage()
                    st['pe'].append(lambda e, j=j: e.matmul(p1[:], wrb[:, j, :], xcb[:], start=True, stop=True))
                    st['pe'].append(lambda e, j=j: e.matmul(p2[:], wib[:, j, :], xcb[:], start=True, stop=True))
                    st['act'].append(lambda e: e.activation(out=sgm[:], in_=inner2[:], func=AF.Sigmoid,
                                                             scale=1.5957691216057308))
                    st = P.stage()
                    st['dve'].append(lambda e: e.tensor_copy(rg[:], p1[:]))
                    st['dve'].append(lambda e: e.tensor_copy(ig[:], p2[:]))
                    st['dve'].append(lambda e: e.tensor_tensor(out=gel[:], in0=gat[:], in1=sgm[:], op=ALU.mult))
                    st = P.stage()
                    st['act'].append(lambda e, col=col: e.activation(out=rg[:], in_=rg[:], func=AF.Sigmoid, bias=col(5)))
                    st['act'].append(lambda e, col=col: e.activation(out=ig[:], in_=ig[:], func=AF.Sigmoid, bias=col(6)))
                    st['act'].append(lambda e, j=j: e.activation(out=av[:], in_=rg[:], func=AF.Exp, scale=cch[:, j:j + 1]))
                    st['act'].append(lambda e, j=j: e.activation(out=a2v[:], in_=rg[:], func=AF.Exp, scale=c2h[:, j:j + 1]))
                    st = P.stage()
                    st['dve'].append(lambda e: e.tensor_tensor(out=gx[:], in0=ig[:], in1=xc[:], op=ALU.mult))
                    st['dve'].append(lambda e: e.tensor_scalar(x2[:], a2v[:], 1.0, -1.0, ALU.min, ALU.mult))
                    st = P.stage()
                    if tb == 0:
                        st['act'].append(lambda e: e.activation(out=inner[:, 1:512], in_=x2[:, 1:512], func=AF.Sqrt, bias=1.0))
                        st['dve'].append(lambda e: e.memset(inner[:, 0:1], 1.0))
                    else:
                        st['act'].append(lambda e: e.activation(out=inner[:], in_=x2[:], func=AF.Sqrt, bias=1.0))
                    st = P.stage()
                    st['dve'].append(lambda e: e.tensor_tensor(out=inner2[:], in0=inner[:], in1=gx[:], op=ALU.mult))
                    if tb == 0:
                        st['dve'].append(lambda e: e.tensor_tensor_scan(out=sgm[:], data0=av[:], data1=inner2[:],
                                                                         initial=0.0, op0=ALU.mult, op1=ALU.add))
                    else:
                        st['dve'].append(lambda e: e.tensor_tensor_scan(out=sgm[:], data0=av[:], data1=inner2[:],
                                                                         initial=hcar[:, 0:1], op0=ALU.mult, op1=ALU.add))
                    st['dve'].append(lambda e: e.tensor_tensor(out=ybf[:], in0=sgm[:], in1=gel[:], op=ALU.mult))
                    st = P.stage()
                    st['act'].append(lambda e: e.copy(hcar[:], sgm[:, 511:512]))
                    rows = slice(RW + j * 128, RW + (j + 1) * 128)
                    st['sp'].append(lambda e, rows=rows, c0=c0: e.dma_start(out=ybuf[rows, c0:c0 + 512], in_=ybf[:]))
                  fns.append(unit)
                merge_units_l(P, fns)
            st = P.stage()
            if early:
                st['pool'].append(lambda e: e.memset(hcar[:], 0.0))
            else:
              for k in range(4):
                if k > 0:
                    st = P.stage()
                st['pool'].append(lambda e, k=k: e.collective_compute(
                    "AllGather", ALU.bypass, replica_groups=[[2 * i, 2 * i + 1] for i in range(ncores // 2)],
                    ins=[ybuf[k * 512:(k + 1) * 512, :].opt()], outs=[gbufs[k].opt()]))
            if debug is not None and _os.environ.get('LRU_STOP'):
                P.stages = P.stages[:int(_os.environ['LRU_STOP'])]
            P.emit()

        if early or debug == 'gather':
            return nc
        with ExitStack() as es:
            yT = sbt(es, "yT", [128, KC, TO], BF16)
            sel = sbt(es, "sel", [128, 2], F32)
            ngt = sbt(es, "ngt", [128, 16], F32)
            ones32 = sbt(es, "ones32", [128, 128], F32)
            rbc = sbt(es, "rbc", [128, TO], F32)
            with ExitStack() as es1:
                gl = [sbt(es1, f"gl{i}", [128, 2, TO], BF16) for i in range(3)]
                tb_ = [sbt(es1, f"tbl{i}", [128, TO], BF16) for i in range(2)]
                sqf = [sbt(es1, f"sqf{i}", [128, TO], F32) for i in range(2)]
                pq = [pst(es1, f"f1p{i}", [128, 512], F32) for i in range(2)]
                P = Prog(nc, "blend")
                st = P.stage()
                st['sp'].append(lambda e: e.dma_start(out=sel[:], in_=sel_d))
                st['sp'].append(lambda e: e.dma_start(out=ngt[:], in_=ngtab_d))
                st['pool'].append(lambda e: e.memset(ones32[:], 1.0))
                def gsrc(cc):
                    r, q = cc // 16, cc % 16
                    return gbufs[q // 4][r * 512 + (q % 4) * 128:r * 512 + (q % 4 + 1) * 128, :]
                st['sp'].append(lambda e: e.dma_start(out=gl[0][:].rearrange("p a b -> p (a b)"), in_=gsrc(0)))
                for cc in range(KC + 2):
                    st = P.stage()
                    if cc + 1 < KC:
                        st['sp'].append(lambda e, cc=cc: e.dma_start(
                            out=gl[(cc + 1) % 3][:].rearrange("p a b -> p (a b)"),
                            in_=gsrc(cc + 1)))
                    if cc < KC:
                        st['dve'].append(lambda e, cc=cc: e.tensor_scalar(tb_[cc % 2][:], gl[cc % 3][:, 0, :],
                                                                           sel[:, 0:1], None, ALU.mult))
                    if 1 <= cc <= KC:
                        c1 = cc - 1
                        st['dve'].append(lambda e, c1=c1: e.scalar_tensor_tensor(
                            out=yT[:, c1, :], in0=gl[c1 % 3][:, 1, :], scalar=sel[:, 1:2], in1=tb_[c1 % 2][:],
                            op0=ALU.mult, op1=ALU.add))
                lch = list(range(8, 16)) + list(range(24, 32))
                for q in range(17):
                    st = P.stage()
                    if q < 16:
                        st['act'].append(lambda e, q=q: e.activation(out=sqf[q % 2][:], in_=yT[:, lch[q], :],
                                                                      func=AF.Square))
                    if q >= 1:
                        for hf in range(2):
                            st['pe'].append(lambda e, q=q, hf=hf: e.matmul(
                                pq[hf][:], ones32[:], sqf[(q - 1) % 2][:, hf * 512:(hf + 1) * 512],
                                start=(q == 1), stop=(q == 16)))
                st = P.stage()
                for hf in range(2):
                    st['act'].append(lambda e, hf=hf: e.activation(out=rbc[:, hf * 512:(hf + 1) * 512], in_=pq[hf][:],
                                                                    func=AF.Sqrt, scale=1.0 / 2048, bias=EPS))
                st = P.stage()
                st['dve'].append(lambda e: e.reciprocal(rbc[:], rbc[:]))
                st = P.stage()
                for q in range(16):
                    eng = 'dve'
                    st[eng].append(lambda e, q=q: e.scalar_tensor_tensor(
                        out=yT[:, lch[q], :], in0=yT[:, lch[q], :], scalar=ngt[:, q:q + 1], in1=rbc[:],
                        op0=ALU.mult, op1=ALU.mult))
                P.emit()
            with ExitStack() as es2:
                wo = [sbt(es2, f"wo{i}", [128, KC, 512], BF16) for i in range(2)]
                xot = [sbt(es2, f"xot{i}", [128, 2, 512], F32) for i in range(2)]
                hst = [sbt(es2, f"hst{i}", [128, 2, 512], F32) for i in range(2)]
                po = [pst(es2, f"po{i}", [128, 512], F32) for i in range(4)]
                xov = xo.rearrange("(a p) n -> p a n", p=128)
                h1v6 = h1buf.rearrange("(a p) n -> p a n", p=128)
                P = Prog(nc, "outproj")
                st = P.stage()
                st['poolq'].append(lambda e: e.dma_start(out=wo[0][:].rearrange("p a b -> p (a b)"), in_=wout[0]))
                NU = 32
                for u in range(NU + 2):
                    st = P.stage()
                    if u < NU:
                        db, tp = u // 4, u % 4
                        if db + 1 < 8:
                            st['poolq'].append(lambda e, db=db, tp=tp: e.dma_start(
                                out=wo[(db + 1) % 2][:, tp * 8:(tp + 1) * 8, :].rearrange("p a b -> p (a b)"),
                                in_=wout[db + 1][:, tp * 4096:(tp + 1) * 4096]))
                        for a in range(2):
                            tcx = tp * 2 + a
                            for cc in range(KC):
                                st['pe'].append(lambda e, u=u, db=db, tcx=tcx, cc=cc, a=a: e.matmul(
                                    po[2 * (u % 2) + a][:], yT[:, cc, tcx * 128:(tcx + 1) * 128], wo[db % 2][:, cc, :],
                                    start=(cc == 0), stop=(cc == KC - 1)))
                        st['sp'].append(lambda e, u=u, db=db, tp=tp: e.dma_start(
                            out=xot[u % 2][:], in_=xov[:, tp * 2:tp * 2 + 2, db * 512:(db + 1) * 512]))
                    if 1 <= u <= NU:
                        u1 = u - 1
                        for a in range(2):
                            st['dve'].append(lambda e, u1=u1, a=a: e.tensor_tensor(
                                out=hst[u1 % 2][:, a, :], in0=po[2 * (u1 % 2) + a][:], in1=xot[u1 % 2][:, a, :], op=ALU.add))
                    if u >= 2:
                        u2 = u - 2
                        db2, tp2 = u2 // 4, u2 % 4
                        st['sp'].append(lambda e, u2=u2, db2=db2, tp2=tp2: e.dma_start(
                            out=h1v6[:, tp2 * 2:tp2 * 2 + 2, db2 * 512:(db2 + 1) * 512], in_=hst[u2 % 2][:]))
                P.emit()
        if debug == 'p6':
            return nc

        with ExitStack() as es:
            u2T = sbt(es, "u2T", [128, KC, TO], BF16)
            with ExitStack() as es1:
                norm_transpose("n2", es1, h1buf, g2bc, TO, u2T, ident)
            with ExitStack() as es2:
                wgb = [sbt(es2, f"wgb{i}", [128, KC, 128], BF16) for i in range(2)]
                wub = [sbt(es2, f"wub{i}", [128, KC, 128], BF16) for i in range(2)]
                sgs = [sbt(es2, f"sgs{i}", [128, TO], F32) for i in range(2)]
                ups = [sbt(es2, f"ups{i}", [128, TO], F32) for i in range(2)]
                hs = [sbt(es2, f"hs{i}", [128, TO], BF16) for i in range(2)]
                pg = [[pst(es2, f"pg{s}_{q}", [128, 512], F32) for q in range(4)] for s in range(2)]
                P = Prog(nc, "gateup")
                st = P.stage()
                st['poolq'].append(lambda e: e.dma_start(out=wgb[0][:].rearrange("p a b -> p (a b)"), in_=wg[0]))
                st['poolq'].append(lambda e: e.dma_start(out=wub[0][:].rearrange("p a b -> p (a b)"), in_=wu[0]))
                for f in range(NF + 3):
                    st = P.stage()
                    if f + 1 < NF:
                        st['poolq'].append(lambda e, f=f: e.dma_start(
                            out=wgb[(f + 1) % 2][:].rearrange("p a b -> p (a b)"), in_=wg[f + 1]))
                        st['poolq'].append(lambda e, f=f: e.dma_start(
                            out=wub[(f + 1) % 2][:].rearrange("p a b -> p (a b)"), in_=wu[f + 1]))
                    if f < NF:
                        for kc in range(KC):
                            for q in range(4):
                                wsrc = wgb if q < 2 else wub
                                st['pe'].append(lambda e, f=f, kc=kc, q=q, wsrc=wsrc: e.matmul(
                                    pg[f % 2][q][:], wsrc[f % 2][:, kc, :],
                                    u2T[:, kc, (q % 2) * 512:(q % 2 + 1) * 512],
                                    start=(kc == 0), stop=(kc == KC - 1)))
                    if 1 <= f <= NF:
                        f1 = f - 1
                        for hf in range(2):
                            st['act'].append(lambda e, f1=f1, hf=hf: e.activation(
                                out=sgs[f1 % 2][:, hf * 512:(hf + 1) * 512], in_=pg[f1 % 2][hf][:], func=AF.Silu))
                            st['dve'].append(lambda e, f1=f1, hf=hf: e.tensor_copy(
                                ups[f1 % 2][:, hf * 512:(hf + 1) * 512], pg[f1 % 2][2 + hf][:]))
                    if 2 <= f <= NF + 1:
                        f2 = f - 2
                        st['pool'].append(lambda e, f2=f2: e.tensor_tensor(out=hs[f2 % 2][:], in0=sgs[f2 % 2][:],
                                                                            in1=ups[f2 % 2][:], op=ALU.mult))
                    if 3 <= f:
                        f3 = f - 3
                        st['sp'].append(lambda e, f3=f3: e.dma_start(out=hT[f3 * 128:(f3 + 1) * 128, :],
                                                                      in_=hs[f3 % 2][:]))
                P.emit()
        if debug == 'p8':
            return nc

        with ExitStack() as es:
            hTs = sbt(es, "hTs", [128, NF, 512], BF16)
            GS = [(0, 22), (22, 44), (44, 65), (65, 86)]
            wdb = [sbt(es, f"wdb{i}", [128, 22, 512], BF16) for i in range(2)]
            h1p = [sbt(es, f"h1p{i}", [128, 4, 512], F32) for i in range(2)]
            h2s = [sbt(es, f"h2s{i}", [128, 4, 512], F32) for i in range(2)]
            pd = [[pst(es, f"pd{s}_{q}", [128, 512], F32) for q in range(4)] for s in range(2)]
            hTv = hT.rearrange("(f p) t -> p f t", p=128)
            h1v = h1buf.rearrange("(a p) n -> p a n", p=128)
            h2v = h2buf.rearrange("(a p) n -> p a n", p=128)
            P = Prog(nc, "down")
            units = [(th, db, g) for th in range(2) for db in range(8) for g in range(4)]

            def wload(st, ui):
                th, db, g = units[ui]
                f0, f1 = GS[g]
                st['poolq'].append(lambda e, ui=ui, db=db, f0=f0, f1=f1: e.dma_start(
                    out=wdb[ui % 2][:, 0:f1 - f0, :], in_=wdn[db, :, f0:f1, :]))
            st = P.stage()
            wload(st, 0)
            for ui in range(len(units) + 2):
                st = P.stage()
                if ui < len(units):
                    th, db, g = units[ui]
                    f0, f1 = GS[g]
                    if db == 0 and g == 0:
                        pass
                    if ui + 1 < len(units):
                        wload(st, ui + 1)
                    if g == 0:
                        st['sp'].append(lambda e, th=th, db=db: e.dma_start(
                            out=h1p[db % 2][:], in_=h1v[:, th * 4:(th + 1) * 4, db * 512:(db + 1) * 512]))
                    for ff in range(f0, f1):
                        for tq in range(4):
                            st['pe'].append(lambda e, ui=ui, db=db, ff=ff, f0=f0, tq=tq: e.matmul(
                                pd[db % 2][tq][:], hTs[:, ff, tq * 128:(tq + 1) * 128], wdb[ui % 2][:, ff - f0, :],
                                start=(ff == 0), stop=(ff == NF - 1)))
                if ui >= 1 and ui - 1 < len(units) and units[ui - 1][2] == 3:
                    th1, db1, _ = units[ui - 1]
                    for tq in range(4):
                        st['dve'].append(lambda e, db1=db1, tq=tq: e.tensor_tensor(
                            out=h2s[db1 % 2][:, tq, :], in0=pd[db1 % 2][tq][:], in1=h1p[db1 % 2][:, tq, :], op=ALU.add))
                if ui >= 2 and ui - 2 < len(units) and units[ui - 2][2] == 3:
                    th2, db2, _ = units[ui - 2]
                    st['sp'].append(lambda e, th2=th2, db2=db2: e.dma_start(
                        out=h2v[:, th2 * 4:(th2 + 1) * 4, db2 * 512:(db2 + 1) * 512], in_=h2s[db2 % 2][:]))
                if ui + 1 < len(units) and units[ui + 1][1] == 0 and units[ui + 1][2] == 0 and ui + 1 > 0:
                    pass
            stages = P.stages
            def hload_stage(th):
                stl = {e: [] for e in ENGS}
                for k in range(0, NF, 11):
                    k1 = min(NF, k + 11)
                    stl['sp'].append(lambda e, k=k, k1=k1, th=th: e.dma_start(
                        out=hTs[:, k:k1, :], in_=hTv[:, k:k1, th * 512:(th + 1) * 512]))
                return stl
            new = [stages[0], hload_stage(0)]
            for si in range(1, len(stages)):
                ui = si - 1
                if ui == 32:
                    new.append(hload_stage(1))
                new.append(stages[si])
            P.stages = new
            P.emit()
        if debug == 'p9':
            return nc

        with ExitStack() as es:
            gbc = sbt(es, "g3", [128, D], F32)
            xt = [sbt(es, f"fxt{i}", [128, D], F32) for i in range(2)]
            ot = [sbt(es, f"fot{i}", [128, D], F32) for i in range(2)]
            junk = sbt(es, "fjunk", [128, D], BF16)
            ss = sbt(es, "fss", [128, 8], F32)
            rstd = sbt(es, "frstd", [128, 8], F32)
            P = Prog(nc, "fin")
            st = P.stage()
            st['sp'].append(lambda e: e.dma_start(out=gbc[:], in_=g3bc))
            st['sp'].append(lambda e: e.dma_start(out=xt[0][:], in_=h2buf[0:128, :]))
            st['dve'].append(lambda e: e.memset(ss[:], 0.0))
            NTL = TO // 128
            for i in range(NTL):
                b = i % 2
                st = P.stage()
                st['act'].append(lambda e, b=b, i=i: e.activation(out=junk[:], in_=xt[b][:], func=AF.Square,
                                                                   accum_out=ss[:, i:i + 1]))
                if i + 1 < NTL:
                    st['sp'].append(lambda e, i=i: e.dma_start(out=xt[(i + 1) % 2][:],
                                                               in_=h2buf[(i + 1) * 128:(i + 2) * 128, :]))
                st = P.stage()
                st['act'].append(lambda e, i=i: e.activation(out=rstd[:, i:i + 1], in_=ss[:, i:i + 1], func=AF.Sqrt,
                                                              scale=1.0 / D, bias=EPS))
                st = P.stage()
                st['dve'].append(lambda e, i=i: e.reciprocal(rstd[:, i:i + 1], rstd[:, i:i + 1]))
                st = P.stage()
                st['dve'].append(lambda e, b=b, i=i: e.scalar_tensor_tensor(out=ot[b][:], in0=xt[b][:],
                                                                             scalar=rstd[:, i:i + 1], in1=gbc[:],
                                                                             op0=ALU.mult, op1=ALU.mult))
                st = P.stage()
                st['sp'].append(lambda e, b=b, i=i: e.dma_start(out=out[i * 128:(i + 1) * 128, :], in_=ot[b][:]))
            P.emit()
    return nc


def _tile_cols(w, cols_list):
    nch = len(cols_list)
    outw = np.zeros((nch, 128, KC, 128), np.float32)
    for j, cols in enumerate(cols_list):
        blk = w[:, cols]
        outw[j, :, :, :blk.shape[1]] = blk.reshape(KC, 128, -1).transpose(1, 0, 2)
    return outw.reshape(nch, 128, KC * 128)


def _prep(inp):
    f = lambda k: np.asarray(inp[k], dtype=np.float32)
    x = f("x")
    w_in = f("w_in")[0]
    mu = f("mu_shift")[0]
    R = 2048
    o3 = 3 * R
    g1bc = np.ascontiguousarray(np.broadcast_to(f("norm_mix_g")[0][None, :], (128, D)))
    g2bc = np.ascontiguousarray(np.broadcast_to(f("norm_ffn_g")[0][None, :], (128, D)))
    g3bc = np.ascontiguousarray(np.broadcast_to(f("norm_final_g")[None, :], (128, D)))
    w_out = f("w_out")[0]
    perm = np.concatenate([np.arange(0, 1024), np.arange(2048, 3072), np.arange(1024, 2048), np.arange(3072, 4096)])
    wop = w_out[perm]
    wout_t = np.ascontiguousarray(wop.reshape(KC, 128, 8, 512).transpose(2, 1, 0, 3)).reshape(8, 128, KC * 512)
    wgate = f("ffn_w_gate")[0]
    wup = f("ffn_w_up")[0]
    wdown = f("ffn_w_down")[0]
    wg_t = np.ascontiguousarray(wgate.reshape(KC, 128, NF, 128).transpose(2, 1, 0, 3)).reshape(NF, 128, KC * 128)
    wu_t = np.ascontiguousarray(wup.reshape(KC, 128, NF, 128).transpose(2, 1, 0, 3)).reshape(NF, 128, KC * 128)
    wdn_t = np.ascontiguousarray(wdown.reshape(NF, 128, 8, 512).transpose(2, 1, 0, 3))
    ii = np.arange(64)
    su = (ii[:, None] < ii[None, :]).astype(np.float32)
    iu = (ii[:, None] <= ii[None, :]).astype(np.float32)
    sl = (ii[:, None] > ii[None, :]).astype(np.float32)
    masks = np.stack([np.tile(m, (1, 8)) for m in (su, iu, sl)], axis=1).astype(np.float32)
    masks2 = np.ascontiguousarray(np.tile(np.concatenate([su, iu], axis=1), (1, 8)).astype(np.float32))
    bones = np.kron(np.eye(2, dtype=np.float32), np.ones((64, 64), np.float32))
    cmask = np.ones((128, 512), np.float32)
    cmask[:, ::64] = 0.0
    ngtab = np.ascontiguousarray(f("lru_norm_g")[0].reshape(16, 128).T)
    per_half = {}
    for hh in range(2):
        rsl = np.arange(hh * 1024, (hh + 1) * 1024)
        cols_list = []
        for base in (0, R, 2 * R):
            for j in range(8):
                cols_list.append(base + rsl[j * 128:(j + 1) * 128])
        cols_list.append(np.arange(o3, o3 + 96))
        cols_list.append(np.arange(o3 + 96, o3 + 192))
        cols_list.append(np.arange(o3 + 192, o3 + 320))
        cols_list.append(np.arange(o3 + 320, o3 + 448))
        lb = o3 + 448
        for base in (lb, lb + R):
            for j in range(8):
                cols_list.append(base + rsl[j * 128:(j + 1) * 128])
        win_t = _tile_cols(w_in, cols_list)
        ptabh = np.zeros((64, 192), np.float32)
        ptabl = np.zeros((128, 4), np.float32)
        ptabu = np.zeros((128, 72), np.float32)
        pv = {k: f(k)[0] for k in ("rwkv_w0", "rwkv_a0", "rwkv_k_k", "rwkv_k_a", "rwkv_ln_g", "rwkv_ln_b",
                                   "conv_b", "lru_br", "lru_bi", "lru_lambda")}
        rk = f("rwkv_r_k")[0].reshape(-1)
        cw = f("conv_w")[0]
        for h in range(16):
            ch = rsl[h * 64:(h + 1) * 64]
            b = h * 12
            ptabh[:, b + 0] = mu[ch]
            ptabh[:, b + 1] = mu[R + ch]
            ptabh[:, b + 2] = mu[2 * R + ch]
            ptabh[:, b + 3] = pv["rwkv_w0"][ch]
            ptabh[:, b + 4] = pv["rwkv_a0"][ch]
            ptabh[:, b + 5] = pv["rwkv_k_k"][ch]
            ptabh[:, b + 6] = pv["rwkv_k_a"][ch]
            ptabh[:, b + 8] = rk[ch]
            ptabh[:, b + 9] = pv["rwkv_ln_g"][ch]
            ptabh[:, b + 10] = pv["rwkv_ln_b"][ch]
        for j in range(8):
            ch = rsl[j * 128:(j + 1) * 128]
            b = j * 9
            for k in range(4):
                ptabu[:, b + k] = cw[k, ch]
            ptabu[:, b + 4] = pv["conv_b"][ch]
            ptabu[:, b + 5] = pv["lru_br"][ch]
            ptabu[:, b + 6] = pv["lru_bi"][ch]
            ptabu[:, b + 7] = pv["lru_lambda"][ch]
        ptabl[:96, 0] = mu[o3:o3 + 96]
        ptabl[:96, 1] = mu[o3 + 96:o3 + 192]
        ptabl[:, 2] = mu[o3 + 192:o3 + 320]
        ptabl[:, 3] = mu[o3 + 320:o3 + 448]
        per_half[hh] = dict(
            win=win_t, ptabh=ptabh, ptabl=ptabl, ptabu=ptabu,
            w2=np.ascontiguousarray(f("rwkv_w2")[0][:, rsl]),
            a2=np.ascontiguousarray(f("rwkv_a2")[0][:, rsl]),
            g2w=np.ascontiguousarray(f("rwkv_g2")[0][:, rsl].reshape(2, 128, 1024).transpose(1, 0, 2)),
            wr=np.ascontiguousarray(f("lru_wr")[0][hh * 8:(hh + 1) * 8].transpose(1, 0, 2)),
            wi=np.ascontiguousarray(f("lru_wi")[0][hh * 8:(hh + 1) * 8].transpose(1, 0, 2)),
        )
    in_maps = []
    for c in range(8):
        b, hh = c // 2, c % 2
        sel = np.zeros((128, 2), np.float32)
        sel[:, hh] = 1.0
        m = dict(xb=np.ascontiguousarray(x[b]), xo=np.ascontiguousarray(x[b, hh * TO:(hh + 1) * TO]),
                 g1bc=g1bc, g2bc=g2bc, g3bc=g3bc, ngtab=ngtab, masks=masks, masks2=masks2, bones=bones, cmask=cmask,
                 wout=wout_t, wg=wg_t, wu=wu_t, wdn=wdn_t, sel=sel)
        m.update(per_half[hh])
        in_maps.append(m)
    return in_maps


def kernel(**inp):
    in_maps = _prep(inp)
    nc = build_nc()
    res = run_bass_kernel_spmd(nc, in_maps, core_ids=list(range(8)))
    outp = np.zeros((4, T, D), np.float32)
    for c in range(8):
        b, hh = c // 2, c % 2
        outp[b, hh * TO:(hh + 1) * TO] = res.results[c]["out"]
    return outp
```
